# Optimizing a Trainium2 kernel written in Bass

```python
import jax, jax.numpy as jnp
from jax import lax
import numpy as np

D_MODEL = 1024
BATCH = 16
SEQ = 2048
DEPTH = 1

CHUNK = 64
HEAD_DIM = 64
A_HEADS = 8
A_PREV_CHUNKS = 8
A_MAX_REL = 128
B_Q_HEADS = 8
B_KV_HEADS = 2
B_GROUP = B_Q_HEADS // B_KV_HEADS
B_WINDOW = 128
B_PREV_CHUNKS = (B_WINDOW - 1 + CHUNK - 1) // CHUNK
A_WIDTH = A_HEADS * HEAD_DIM
B_Q_WIDTH = B_Q_HEADS * HEAD_DIM
B_KV_WIDTH = B_KV_HEADS * HEAD_DIM
IN_COLS = 3 * A_WIDTH + B_Q_WIDTH + 2 * B_KV_WIDTH
D_FF = 2816
PLE_DIM = 256
EPS = 1e-6
NEG_INF = -1e30

kernel_name = "hybrid_chunked_relpos_swa_sink_macaron_ple"


def rms_norm(x, gain):
    xf = x.astype(jnp.float32)
    y = xf * lax.rsqrt(jnp.mean(xf * xf, axis=-1, keepdims=True) + EPS)
    return (y * gain.astype(jnp.float32)).astype(x.dtype)


def swiglu_ffn(x, w_gu, w_down):
    g, u = jnp.split(x @ w_gu, 2, axis=-1)
    return (jax.nn.silu(g) * u) @ w_down


def alibi_slopes(n_heads):
    return np.array([2.0 ** (-8.0 * (h + 1) / n_heads) for h in range(n_heads)], dtype=np.float32)


def band_distance(n_prev):
    i = np.arange(CHUNK)[:, None]
    j = np.arange((n_prev + 1) * CHUNK)[None, :]
    return i + n_prev * CHUNK - j


def chunk_band_attention(q, k, v, n_prev, bias, sink):
    b, hkv, g, s, dh = q.shape
    n_chunks = s // CHUNK
    band = (n_prev + 1) * CHUNK
    pad = n_prev * CHUNK
    kp = jnp.pad(k, ((0, 0), (0, 0), (pad, 0), (0, 0)))
    vp = jnp.pad(v, ((0, 0), (0, 0), (pad, 0), (0, 0)))
    scale = dh ** -0.5
    key_idx = jnp.arange(band)

    def one_chunk(c):
        start = c * CHUNK
        qc = lax.dynamic_slice_in_dim(q, start, CHUNK, axis=3)
        kc = lax.dynamic_slice_in_dim(kp, start, band, axis=2)
        vc = lax.dynamic_slice_in_dim(vp, start, band, axis=2)
        scores = jnp.einsum('bkgqd,bksd->bkgqs', qc.astype(jnp.float32),
                            kc.astype(jnp.float32)) * scale + bias
        valid = key_idx >= (n_prev - c) * CHUNK
        scores = jnp.where(valid, scores, NEG_INF)
        if sink is None:
            probs = jax.nn.softmax(scores, axis=-1)
        else:
            sink_col = jnp.broadcast_to(sink.astype(jnp.float32).reshape(1, hkv, g, 1, 1),
                                        (b, hkv, g, CHUNK, 1))
            probs = jax.nn.softmax(jnp.concatenate([scores, sink_col], axis=-1), axis=-1)[..., :band]
        out = jnp.einsum('bkgqs,bksd->bkgqd', probs, vc.astype(jnp.float32))
        return out.astype(v.dtype)

    outs = lax.map(one_chunk, jnp.arange(n_chunks))
    outs = jnp.transpose(outs, (1, 0, 4, 2, 3, 5))
    return outs.reshape(b, s, hkv * g * dh)


def setup_inputs(seed: int = 0) -> dict:
    key = jax.random.key(seed)
    ks = jax.random.split(key, 24)
    f32 = jnp.float32

    def w(k, shape, fan_in):
        return jax.random.normal(k, shape, f32) * (fan_in ** -0.5)

    def gain(k, n):
        return 1.0 + 0.05 * jax.random.normal(k, (DEPTH, n), f32)

    return {
        "x": jax.random.normal(ks[0], (BATCH, SEQ, D_MODEL), f32),
        "p": jax.random.normal(ks[1], (DEPTH, BATCH, SEQ, PLE_DIM), f32),
        "ffn1_norm": gain(ks[2], D_MODEL),
        "ffn1_w_gu": w(ks[3], (DEPTH, D_MODEL, 2 * D_FF), D_MODEL),
        "ffn1_w_down": w(ks[4], (DEPTH, D_FF, D_MODEL), D_FF),
        "mix_norm": gain(ks[5], D_MODEL),
        "w_in": w(ks[6], (DEPTH, D_MODEL, IN_COLS), D_MODEL),
        "a_q_norm": gain(ks[7], HEAD_DIM),
        "a_k_norm": gain(ks[8], HEAD_DIM),
        "a_rel_bias": 0.1 * jax.random.normal(ks[9], (DEPTH, A_HEADS, 2 * A_MAX_REL + 1), f32),
        "b_q_norm": gain(ks[10], HEAD_DIM),
        "b_k_norm": gain(ks[11], HEAD_DIM),
        "b_sinks": 0.5 * jax.random.normal(ks[12], (DEPTH, B_Q_HEADS), f32),
        "w_gate": w(ks[13], (DEPTH, D_MODEL, 2 * D_MODEL), D_MODEL),
        "w_proj_a": w(ks[14], (DEPTH, A_WIDTH, D_MODEL), A_WIDTH),
        "w_proj_b": w(ks[15], (DEPTH, B_Q_WIDTH, D_MODEL), B_Q_WIDTH),
        "w_out": w(ks[16], (DEPTH, D_MODEL, D_MODEL), D_MODEL),
        "ffn2_norm": gain(ks[17], D_MODEL),
        "ffn2_w_gu": w(ks[18], (DEPTH, D_MODEL, 2 * D_FF), D_MODEL),
        "ffn2_w_down": w(ks[19], (DEPTH, D_FF, D_MODEL), D_FF),
        "ple_norm": gain(ks[20], D_MODEL),
        "w_ple_gate": w(ks[21], (DEPTH, D_MODEL, D_MODEL), D_MODEL),
        "w_ple_proj": w(ks[22], (DEPTH, PLE_DIM, D_MODEL), PLE_DIM),
    }


def reference(x, p, ffn1_norm, ffn1_w_gu, ffn1_w_down, mix_norm, w_in, a_q_norm, a_k_norm,
              a_rel_bias, b_q_norm, b_k_norm, b_sinks, w_gate, w_proj_a, w_proj_b, w_out,
              ffn2_norm, ffn2_w_gu, ffn2_w_down, ple_norm, w_ple_gate, w_ple_proj):
    b, s, _ = x.shape
    split_points = list(np.cumsum([A_WIDTH, A_WIDTH, A_WIDTH, B_Q_WIDTH, B_KV_WIDTH]))
    a_rel_idx = np.clip(band_distance(A_PREV_CHUNKS), -A_MAX_REL, A_MAX_REL) + A_MAX_REL
    b_dist = np.abs(band_distance(B_PREV_CHUNKS)).astype(np.float32)
    b_alibi = jnp.asarray((-alibi_slopes(B_Q_HEADS)[:, None, None] * b_dist[None])
                          .reshape(B_KV_HEADS, B_GROUP, CHUNK, -1))

    h = x
    for i in range(DEPTH):
        h = h + 0.5 * swiglu_ffn(rms_norm(h, ffn1_norm[i]), ffn1_w_gu[i], ffn1_w_down[i])

        u = rms_norm(h, mix_norm[i])
        qa, ka, va, qb, kb, vb = jnp.split(u @ w_in[i], split_points, axis=-1)

        qa = rms_norm(qa.reshape(b, s, A_HEADS, HEAD_DIM), a_q_norm[i])
        ka = rms_norm(ka.reshape(b, s, A_HEADS, HEAD_DIM), a_k_norm[i])
        qa = jnp.transpose(qa, (0, 2, 1, 3))[:, :, None]
        ka = jnp.transpose(ka, (0, 2, 1, 3))
        va = jnp.transpose(va.reshape(b, s, A_HEADS, HEAD_DIM), (0, 2, 1, 3))
        a_bias = a_rel_bias[i].astype(jnp.float32)[:, a_rel_idx][:, None]
        ya = chunk_band_attention(qa, ka, va, A_PREV_CHUNKS, a_bias, None)

        qb = rms_norm(qb.reshape(b, s, B_KV_HEADS, B_GROUP, HEAD_DIM), b_q_norm[i])
        kb = rms_norm(kb.reshape(b, s, B_KV_HEADS, HEAD_DIM), b_k_norm[i])
        qb = jnp.transpose(qb, (0, 2, 3, 1, 4))
        kb = jnp.transpose(kb, (0, 2, 1, 3))
        vb = jnp.transpose(vb.reshape(b, s, B_KV_HEADS, HEAD_DIM), (0, 2, 1, 3))
        yb = chunk_band_attention(qb, kb, vb, B_PREV_CHUNKS, b_alibi,
                                  b_sinks[i].reshape(B_KV_HEADS, B_GROUP))

        ga, gb = jnp.split(jax.nn.sigmoid(u @ w_gate[i]), 2, axis=-1)
        merged = ga * (ya @ w_proj_a[i]) + gb * (yb @ w_proj_b[i])
        h = h + merged @ w_out[i]

        h = h + 0.5 * swiglu_ffn(rms_norm(h, ffn2_norm[i]), ffn2_w_gu[i], ffn2_w_down[i])

        ple_gate = jax.nn.sigmoid(rms_norm(h, ple_norm[i]) @ w_ple_gate[i])
        h = h + ple_gate * (p[i] @ w_ple_proj[i])
    return h
```

```python
import numpy as np
import concourse.bass as bass
import concourse.mybir as mybir
from concourse.bass_utils import run_bass_kernel_spmd

F32 = mybir.dt.float32
BF16 = mybir.dt.bfloat16
AF = mybir.ActivationFunctionType
ALU = mybir.AluOpType

N_CORES = 8
D = 1024
KC = 8
DFF = 2816
SEQ = 2048
PASS = 1024
SLAB = 512
TOK_CORE = 4096
EPS = 1e-6
NEG = -30000.0
STRICT = True


class Res:
    __slots__ = ("w", "rs")

    def __init__(self):
        self.w = None
        self.rs = {}


class Sched:
    def __init__(self, nc):
        self.nc = nc
        self.eng = {"pe": nc.tensor, "act": nc.scalar, "dve": nc.vector,
                    "pool": nc.gpsimd, "sp": nc.sync}
        self.sems = {}
        self.cnt = {}
        self.seen = {e: {} for e in self.eng}
        self.res = {}
        for e in ("pe", "act", "dve", "pool"):
            self.newsem(e)

    def newsem(self, key):
        if key not in self.sems:
            self.sems[key] = self.nc.alloc_semaphore("s_" + key)
            self.cnt[key] = 0

    def R(self, *key):
        r = self.res.get(key)
        if r is None:
            r = self.res[key] = Res()
        return r

    def _waits(self, eng, reads, writes):
        waits = {}

        def need(st):
            if st is None:
                return
            k, v = st
            if k == eng and (eng == "pe" or not STRICT):
                return
            if self.seen[eng].get(k, 0) >= v:
                return
            if waits.get(k, 0) < v:
                waits[k] = v

        for r in reads:
            need(r.w)
        for r in writes:
            need(r.w)
            for k, v in r.rs.items():
                need((k, v))
        for k, v in waits.items():
            self.eng[eng].wait_ge(self.sems[k], v)
            self.seen[eng][k] = v

    def _stamp(self, st, reads, writes):
        k, v = st
        for r in reads:
            if r.rs.get(k, 0) < v:
                r.rs[k] = v
        for r in writes:
            r.w = st
            r.rs = {}

    def op(self, eng, fn, reads=(), writes=(), signal=True):
        self._waits(eng, reads, writes)
        inst = fn()
        if eng == "pe" and not signal:
            st = ("pe", self.cnt["pe"] + 1)
        else:
            self.cnt[eng] += 1
            inst.then_inc(self.sems[eng], 1)
            st = (eng, self.cnt[eng])
        self._stamp(st, reads, writes)

    def dma(self, eng, fns, dsem, reads=(), writes=()):
        self.newsem(dsem)
        self._waits(eng, reads, writes)
        for fn in fns:
            inst = fn()
            self.cnt[dsem] += 16
            inst.then_inc(self.sems[dsem], 16)
        self._stamp((dsem, self.cnt[dsem]), reads, writes)

    def wait_sem(self, eng, key):
        if key in self.sems and self.cnt[key] > 0:
            self.eng[eng].wait_ge(self.sems[key], self.cnt[key])


def build_nc(stop=None, npass=4):
    nc = bass.Bass("TRN2", target_bir_lowering=False)
    S = Sched(nc)
    R = S.R

    def din(name, shape):
        return nc.dram_tensor(name, list(shape), F32, kind="ExternalInput").ap()

    x_d = din("x", [TOK_CORE, D])
    p_d = din("p", [TOK_CORE, 256])
    wgu_d = [din("ffn1_w_gu", [D, 2 * DFF]), din("ffn2_w_gu", [D, 2 * DFF])]
    wdn_d = [din("ffn1_w_down", [DFF, D]), din("ffn2_w_down", [DFF, D])]
    win_d = din("w_in", [D, 2304])
    wgate_d = din("w_gate", [D, 2 * D])
    wpa_d = din("w_proj_a", [512, D])
    wpb_d = din("w_proj_b", [512, D])
    wout_d = din("w_out", [D, D])
    wpg_d = din("w_ple_gate", [D, D])
    wpe_d = din("w_ple_proj", [256, D])
    gains_d = din("gains", [128, 32])
    gqk_d = din("gqk", [128, 4])
    sinks_d = din("sinks", [128, 8])
    ident_d = din("ident", [128, 128])
    biasA_d = din("biasA", [128, 8, 5, 128])
    maskA_d = din("maskA", [128, 8, 5, 128])
    biasB_d = din("biasB", [128, 8, 2, 128])
    out_d = nc.dram_tensor("out", [TOK_CORE, D], F32, kind="ExternalOutput").ap()
    dbg_d = None
    if stop is not None:
        dbg_d = nc.dram_tensor("dbg", [128, 8, PASS], F32, kind="ExternalOutput").ap()

    def sb(name, shape, dt):
        return nc.alloc_sbuf_tensor("sb_" + name, list(shape), dt)

    h = sb("h", [128, 8, PASS], F32)
    xn = sb("xn", [128, 8, PASS], BF16)
    QA = sb("QA", [128, 4, PASS], BF16)
    QB = sb("QB", [128, 4, PASS], BF16)
    KA = sb("KA", [128, 4, SEQ], BF16)
    KB = sb("KB", [128, 2, SEQ], BF16)
    V = sb("V", [128, 16, 10, 65], BF16)
    EA = sb("EA", [128, 8, 5, 128], BF16)
    EB = sb("EB", [128, 8, 2, 128], BF16)
    ring = [sb("ring%d" % i, [128, 6144], BF16) for i in range(2)]
    actb = [sb("act%d" % i, [128, 2, 512], BF16) for i in range(2)]
    Pbuf = [sb("P%d" % i, [128, 20 * 128], BF16) for i in range(2)]
    merged = sb("merged", [128, 8, PASS], BF16)
    stg = [merged[:, 2 * i:2 * i + 2, :].bitcast(F32) for i in range(4)]
    stg = [s.rearrange("p a b -> p (a b)") for s in stg]
    ostg = [Pbuf[i][:, 0:2048].bitcast(F32) for i in range(2)]
    for Q_ in (QA, QB):
        for i in range(2):
            ostg.append(Q_[:, 2 * i:2 * i + 2, :].bitcast(F32).rearrange("p a b -> p (a b)"))
    ptmp = [Pbuf[i][:, 0:2560].bitcast(F32) for i in range(2)]

    def stg_res(i):
        return [R("stg", i)] + [R("mg", 2 * i + a, s_) for a in range(2) for s_ in range(2)]

    def pbuf_res(i):
        return [R("p", i, c) for c in range(6)]

    def ostg_res(i):
        if i < 2:
            return pbuf_res(i)
        if i >= 6:
            nm = "t1" if i == 6 else "t2"
            return [R(nm, 0), R(nm, 1)]
        nm = "qa" if i < 4 else "qb"
        c0 = 2 * (i % 2)
        return [R(nm, c0 + a, qb_) for a in range(2) for qb_ in range(8)]
    sq = [sb("sq%d" % i, [128, 512], BF16) for i in range(4)]
    lnv = sb("lnv", [128, 512], F32)
    rstd = [sb("rstd%d" % i, [128, 512], F32) for i in range(2)]
    sg = [sb("sg%d" % i, [128, 512], BF16) for i in range(4)]
    t1all = sb("t1all", [128, 2, 512], F32)
    t2all = sb("t2all", [128, 2, 512], F32)
    t1 = [t1all[:, i, :] for i in range(2)]
    t2 = [t2all[:, i, :] for i in range(2)]
    ostg.append(t1all[:, :, :].rearrange("p a b -> p (a b)"))
    ostg.append(t2all[:, :, :].rearrange("p a b -> p (a b)"))
    NOST = len(ostg)
    pT = sb("pT", [128, 2, PASS], BF16)
    pstg = [sb("pstg%d" % i, [128, 256], F32) for i in range(2)]
    ytok = [sb("ytok%d" % i, [128, 256], BF16) for i in range(2)]
    ident = sb("ident", [128, 128], F32)
    ident_bf = sb("ident_bf", [128, 128], BF16)
    ones_bf = sb("ones_bf", [128, 128], BF16)
    bones = sb("bones", [128, 128], BF16)
    gains = sb("gains", [128, 32], F32)
    gqk = sb("gqk", [128, 4], F32)
    esink = sb("esink", [128, 8], F32)
    den = [sb("den%d" % i, [128, 4], F32) for i in range(2)]
    rcp = [sb("rcp%d" % i, [128, 4], F32) for i in range(2)]
    warm = sb("warm", [128, 2], F32)
    ps_all = nc.alloc_psum_tensor("ps_all", [128, 8, 512], F32)
    ps = [ps_all[:, i, :] for i in range(8)]

    pe, act, dve, pool, sp = nc.tensor, nc.scalar, nc.vector, nc.gpsimd, nc.sync

    S.dma("sp", [
        lambda: sp.dma_start(out=ident[:, :], in_=ident_d),
        lambda: sp.dma_start(out=gains[:, :], in_=gains_d),
        lambda: sp.dma_start(out=gqk[:, :], in_=gqk_d),
        lambda: sp.dma_start(out=esink[:, :], in_=sinks_d),
    ], "setup", writes=[R("ident"), R("gains"), R("gqk"), R("esink")])
    S.op("dve", lambda: dve.tensor_copy(out=ident_bf[:, :], in_=ident[:, :]),
         reads=[R("ident")], writes=[R("ident_bf")])
    S.op("dve", lambda: dve.memset(ones_bf[:, :], 1.0), writes=[R("ones")])
    S.op("dve", lambda: dve.memset(warm[:, :], 1.0), writes=[R("warm")])
    S.op("dve", lambda: dve.memset(bones[:, :], 0.0), writes=[R("bones")])
    S.op("dve", lambda: dve.memset(bones[0:64, 0:64], 1.0), writes=[R("bones")])
    S.op("dve", lambda: dve.memset(bones[64:128, 64:128], 1.0), writes=[R("bones")])
    S.op("dve", lambda: dve.memset(V[:, :, :, 64:65], 1.0), writes=[R("vones")])
    S.op("dve", lambda: dve.tensor_scalar(out=gqk[:, 0:1], in0=gqk[:, 0:1], scalar1=0.125,
                                          scalar2=None, op0=ALU.mult),
         reads=[R("gqk")], writes=[R("gqk")])
    S.op("dve", lambda: dve.tensor_scalar(out=gqk[:, 2:3], in0=gqk[:, 2:3], scalar1=0.125,
                                          scalar2=None, op0=ALU.mult),
         reads=[R("gqk")], writes=[R("gqk")])
    S.op("act", lambda: act.activation(out=esink[:, :], in_=esink[:, :], func=AF.Exp),
         reads=[R("esink")], writes=[R("esink")])
    def build_E_items():
        items = []
        for hd in range(8):
            def it(hd=hd):
                i = hd % 2
                a = ptmp[i][:, 0:640]
                b = ptmp[i][:, 640:1280]
                S.dma("sp", [
                    lambda: sp.dma_start(out=a, in_=biasA_d[:, hd].rearrange("p a b -> p (a b)")),
                    lambda: sp.dma_start(out=b, in_=maskA_d[:, hd].rearrange("p a b -> p (a b)")),
                ], "setupE%d" % i, writes=pbuf_res(i))
                S.op("dve", lambda: dve.tensor_tensor(out=a, in0=a, in1=b, op=ALU.add),
                     reads=pbuf_res(i), writes=pbuf_res(i))
                S.op("act", lambda: act.activation(
                    out=EA[:, hd].rearrange("p a b -> p (a b)"), in_=a, func=AF.Exp),
                    reads=pbuf_res(i), writes=[R("EA")])
            items.append(it)
        for hp in range(4):
            def it(hp=hp):
                i = hp % 2
                a = ptmp[i][:, 0:512]
                S.dma("sp", [
                    lambda: sp.dma_start(
                        out=a, in_=biasB_d[:, 2 * hp:2 * hp + 2].rearrange("p h a b -> p (h a b)")),
                ], "setupE%d" % i, writes=pbuf_res(i))
                S.op("act", lambda: act.activation(
                    out=EB[:, 2 * hp:2 * hp + 2].rearrange("p h a b -> p (h a b)"), in_=a, func=AF.Exp),
                    reads=pbuf_res(i), writes=[R("EB")])
            items.append(it)
        return items

    def wv(dram, p=128):
        return dram.rearrange("(kc p) n -> p kc n", p=p)

    def ffn_group(w, g):
        def pieces(slot):
            r = ring[slot]
            return [
                (r[:, 0:2048].rearrange("p (k n) -> p k n", k=8),
                 wv(wgu_d[w])[:, :, 256 * g:256 * g + 256], 0),
                (r[:, 2048:4096].rearrange("p (k n) -> p k n", k=8),
                 wv(wgu_d[w])[:, :, DFF + 256 * g:DFF + 256 * g + 256], 0),
                (r[:, 4096:6144].rearrange("p (k n) -> p k n", k=2),
                 wv(wdn_d[w])[:, 2 * g:2 * g + 2, :], 1),
            ]
        return pieces

    def cols_group(dram, c0, n, kc=8):
        def pieces(slot):
            r = ring[slot]
            return [(r[:, 0:kc * n].rearrange("p (k n) -> p k n", k=kc),
                     wv(dram)[:, :, c0:c0 + n], 0)]
        return pieces

    def kbvb_group():
        def pieces(slot):
            r = ring[slot]
            kd = r[:, 0:2048].rearrange("p (k n) -> p k n", k=8)
            out = []
            for kvh in range(2):
                for dup in range(2):
                    out.append((kd[:, :, kvh * 128 + dup * 64:kvh * 128 + dup * 64 + 64],
                                wv(win_d)[:, :, 2048 + kvh * 64:2048 + kvh * 64 + 64], 0))
            out.append((r[:, 2048:3072].rearrange("p (k n) -> p k n", k=8),
                        wv(win_d)[:, :, 2176:2304], 0))
            return out
        return pieces

    def m3_group(G):
        def pieces(slot):
            r = ring[slot]
            return [
                (r[:, 0:2048].rearrange("p (k n) -> p k n", k=8),
                 wv(wgate_d)[:, :, 256 * G:256 * G + 256], 0),
                (r[:, 2048:4096].rearrange("p (k n) -> p k n", k=8),
                 wv(wgate_d)[:, :, D + 256 * G:D + 256 * G + 256], 0),
                (r[:, 4096:5120].rearrange("p (k n) -> p k n", k=4),
                 wv(wpa_d)[:, :, 256 * G:256 * G + 256], 1),
                (r[:, 5120:6144].rearrange("p (k n) -> p k n", k=4),
                 wv(wpb_d)[:, :, 256 * G:256 * G + 256], 1),
            ]
        return pieces

    def ple_group(H):
        def pieces(slot):
            r = ring[slot]
            return [
                (r[:, 0:4096].rearrange("p (k n) -> p k n", k=8),
                 wv(wpg_d)[:, :, 512 * H:512 * H + 512], 0),
                (r[:, 4096:5120].rearrange("p (k n) -> p k n", k=2),
                 wv(wpe_d)[:, :, 512 * H:512 * H + 512], 1),
            ]
        return pieces

    pass_groups = ([ffn_group(0, g) for g in range(11)]
                   + [cols_group(win_d, 0, 512), cols_group(win_d, 512, 512),
                      cols_group(win_d, 1536, 512), kbvb_group(),
                      cols_group(win_d, 1024, 512)]
                   + [m3_group(G) for G in range(4)]
                   + [cols_group(wout_d, 0, 512), cols_group(wout_d, 512, 512)]
                   + [ffn_group(1, g) for g in range(11)]
                   + [ple_group(0), ple_group(1)])
    NG = len(pass_groups)
    all_groups = pass_groups * npass
    gstate = {"issued": [0, 0], "cur": -1}

    def issue_part(part):
        gi = gstate["issued"][part]
        if gi >= len(all_groups):
            return
        slot = gi % 2
        pcs = [(o, i) for (o, i, pt) in all_groups[gi](slot) if pt == part]
        if pcs:
            S.dma("pool", [(lambda o=o, i=i: pool.dma_start(out=o, in_=i)) for o, i in pcs],
                  "ring%d_%d" % (slot, part), writes=[R("ring", slot, part)])
        gstate["issued"][part] += 1

    def next_group(pf=True):
        gstate["cur"] += 1
        gi = gstate["cur"]
        for part in range(2):
            while gstate["issued"][part] <= gi:
                issue_part(part)
        if pf:
            prefetch(0)
            prefetch(1)
        return gi % 2

    def prefetch(part):
        if gstate["issued"][part] <= gstate["cur"] + 1:
            issue_part(part)

    def slab(s):
        return slice(s * SLAB, (s + 1) * SLAB)

    cp_rr = [0]

    def evac_copy(out, in_, reads, writes, force=None):
        if force is None:
            cp_rr[0] ^= 1
        if (force == "act") or (force is None and cp_rr[0]):
            S.op("act", lambda: act.activation(out=out, in_=in_, func=AF.Copy),
                 reads=reads, writes=writes)
        else:
            S.op("dve", lambda: dve.tensor_copy(out=out, in_=in_), reads=reads, writes=writes)

    xn_f32 = xn[:, :, :].bitcast(F32).rearrange("p a b -> p (a b)")

    def xstage(t):
        if t < 4:
            return stg[t], stg_res(t), "stg%d" % t
        j = t - 4
        return (xn_f32[:, j * 1024:(j + 1) * 1024],
                [R("xn", 2 * j + a, s_) for a in range(2) for s_ in range(2)], "stgB%d" % j)

    def issue_x(tok0, t):
        ap, res, sem = xstage(t)
        S.dma("sp", [lambda: sp.dma_start(
            out=ap, in_=x_d[tok0 + t * 128:tok0 + (t + 1) * 128, :])],
            sem, writes=res)

    def load_x(tok0):
        for t in range(8):
            st, res, _ = xstage(t)
            for hf in range(2):
                b = (2 * t + hf) % 8
                for j in range(4):
                    k = hf * 4 + j
                    S.op("pe", lambda b=b, j=j, k=k, st=st: pe.transpose(
                        out=ps[b][:, j * 128:(j + 1) * 128], in_=st[:, k * 128:(k + 1) * 128],
                        identity=ident[:, :]),
                        reads=res + [R("ident")], writes=[R("ps", b)], signal=(j == 3))
                evac_copy(h[:, hf * 4:hf * 4 + 4, t * 128:(t + 1) * 128],
                          ps[b][:, :].rearrange("p (a b) -> p a b", a=4),
                          [R("ps", b)], [R("h", hf * 4 + j, t // 4) for j in range(4)])

    def norm(gidx):
        S.op("act", lambda: act.activation(out=warm[:, 1:2], in_=warm[:, 0:1], func=AF.Ln),
             reads=[R("warm")], writes=[R("warm_o")])
        for s in range(2):
            nb = 6 + s
            for k in range(8):
                q = sq[k % 4]
                S.op("act", lambda k=k, q=q: act.activation(out=q[:, :], in_=h[:, k, slab(s)],
                                                            func=AF.Square),
                     reads=[R("h", k, s)], writes=[R("sq", k % 4)])
                S.op("pe", lambda k=k, q=q: pe.matmul(ps[nb][:, :], lhsT=ones_bf[:, :], rhs=q[:, :],
                                                      start=(k == 0), stop=(k == 7)),
                     reads=[R("sq", k % 4), R("ones")], writes=[R("ps", nb)], signal=True)
            S.op("act", lambda: act.activation(out=lnv[:, :], in_=ps[nb][:, :], func=AF.Ln,
                                               bias=EPS, scale=1.0 / D),
                 reads=[R("ps", nb)], writes=[R("lnv")])
            S.op("act", lambda: act.activation(out=ps[nb][:, :], in_=lnv[:, :], func=AF.Exp,
                                               scale=-0.5),
                 reads=[R("lnv")], writes=[R("ps", nb)])
            for k in range(8):
                S.op("dve", lambda k=k: dve.scalar_tensor_tensor(
                    out=xn[:, k, slab(s)], in0=h[:, k, slab(s)],
                    scalar=gains[:, gidx * 8 + k:gidx * 8 + k + 1], in1=ps[nb][:, :],
                    op0=ALU.mult, op1=ALU.mult),
                    reads=[R("h", k, s), R("ps", nb), R("gains")], writes=[R("xn", k, s)])

    step_ctr = [0]

    def ffn(w, extras=()):
        extras = list(extras)
        prev = None
        for g in range(11):
            slot = next_group(pf=False)
            prefetch(0)
            r = ring[slot]
            Wg = r[:, 0:2048].rearrange("p (k n) -> p k n", k=8)
            Wu = r[:, 2048:4096].rearrange("p (k n) -> p k n", k=8)
            Wd = r[:, 4096:6144].rearrange("p (k n) -> p k n", k=2)
            for s in range(2):
                ab = step_ctr[0] % 2
                step_ctr[0] += 1
                mmlist = []
                for jj in range(2):
                    for k in range(8):
                        mmlist.append((Wg, jj, jj, k))
                        mmlist.append((Wu, jj, 2 + jj, k))
                for qi in range(4):
                    for (W, jj, b, k) in mmlist[8 * qi:8 * qi + 8]:
                        S.op("pe", lambda W=W, b=b, k=k, jj=jj: pe.matmul(
                            ps[b][:, :], lhsT=W[:, k, jj * 128:(jj + 1) * 128],
                            rhs=xn[:, k, slab(s)], start=(k == 0), stop=(k == 7)),
                            reads=[R("ring", slot, 0), R("xn", k, s)], writes=[R("ps", b)],
                            signal=(k == 7))
                    jj = qi // 2
                    if qi in (1, 3):
                        sgi = 2 * ab + jj
                        S.op("act", lambda jj=jj, sgi=sgi: act.activation(
                            out=sg[sgi][:, :], in_=ps[jj][:, :], func=AF.Silu),
                            reads=[R("ps", jj)], writes=[R("sg", sgi)])
                        S.op("dve", lambda jj=jj, sgi=sgi, ab=ab: dve.tensor_tensor(
                            out=actb[ab][:, jj, :], in0=ps[2 + jj][:, :], in1=sg[sgi][:, :],
                            op=ALU.mult),
                            reads=[R("ps", 2 + jj), R("sg", sgi)], writes=[R("act", ab, jj)])
                    if prev is not None:
                        prev[2 * qi]()
                        prev[2 * qi + 1]()
                if s == 0:
                    prefetch(1)

                def mk_pair(m, s=s, ab=ab, Wd=Wd, slot=slot):
                    def pair():
                        b = 4 + m % 4
                        for jj in range(2):
                            S.op("pe", lambda jj=jj: pe.matmul(
                                ps[b][:, :], lhsT=Wd[:, jj, m * 128:(m + 1) * 128],
                                rhs=actb[ab][:, jj, :], start=(jj == 0), stop=(jj == 1)),
                                reads=[R("ring", slot, 1), R("act", ab, jj)], writes=[R("ps", b)],
                                signal=(jj == 1))
                        S.op("dve", lambda: dve.scalar_tensor_tensor(
                            out=h[:, m, slab(s)], in0=ps[b][:, :], scalar=0.5,
                            in1=h[:, m, slab(s)], op0=ALU.mult, op1=ALU.add),
                            reads=[R("ps", b), R("h", m, s)], writes=[R("h", m, s)])
                    return pair
                prev = [mk_pair(m) for m in range(8)]
                if s == 1 and extras:
                    extras.pop(0)()
        for pr in prev:
            pr()
        for ex in extras:
            ex()

    qk_ctr = [0]

    def qk_chunks(items):
        work = [(it, s) for it in items for s in range(2)]
        pend = None
        for (it, s) in work:
            lhsT_fn, gcol, dest_fn, dres_fn, slot = it
            i = qk_ctr[0]
            qk_ctr[0] += 1
            b = i % 4
            for k in range(8):
                S.op("pe", lambda k=k, b=b, lhsT_fn=lhsT_fn, s=s: pe.matmul(
                    ps[b][:, :], lhsT=lhsT_fn(k), rhs=xn[:, k, slab(s)],
                    start=(k == 0), stop=(k == 7)),
                    reads=[R("ring", slot, 0), R("xn", k, s)], writes=[R("ps", b)], signal=(k == 7))
            S.op("act", lambda b=b, i=i: act.activation(out=sq[i % 4][:, :], in_=ps[b][:, :],
                                                        func=AF.Square),
                 reads=[R("ps", b)], writes=[R("sq", i % 4)])
            if pend is not None:
                pend()

            def rest(i=i, b=b, gcol=gcol, dest_fn=dest_fn, dres_fn=dres_fn, s=s):
                q = sq[i % 4]
                sb_ = 4 + i % 2
                rs = rstd[i % 2]
                S.op("pe", lambda: pe.matmul(ps[sb_][:, :], lhsT=bones[:, :], rhs=q[:, :],
                                             start=True, stop=True),
                     reads=[R("sq", i % 4), R("bones")], writes=[R("ps", sb_)], signal=True)
                S.op("act", lambda: act.activation(out=lnv[:, :], in_=ps[sb_][:, :], func=AF.Ln,
                                                   bias=EPS, scale=1.0 / 64),
                     reads=[R("ps", sb_)], writes=[R("lnv")])
                S.op("act", lambda: act.activation(out=rs[:, :], in_=lnv[:, :], func=AF.Exp,
                                                   scale=-0.5),
                     reads=[R("lnv")], writes=[R("rstd", i % 2)])
                S.op("dve", lambda: dve.scalar_tensor_tensor(
                    out=dest_fn(s), in0=ps[b][:, :], scalar=gqk[:, gcol:gcol + 1], in1=rs[:, :],
                    op0=ALU.mult, op1=ALU.mult),
                    reads=[R("ps", b), R("rstd", i % 2), R("gqk")], writes=dres_fn(s))
            pend = rest
        pend()

    def m1(half):
        kb0 = half * 8
        t0 = half * PASS
        slot = next_group()
        W = ring[slot][:, 0:4096].rearrange("p (k n) -> p k n", k=8)
        qk_chunks([((lambda k, c=c, W=W: W[:, k, c * 128:(c + 1) * 128]), 0,
                    (lambda s, c=c: QA[:, c, slab(s)]),
                    (lambda s, c=c: [R("qa", c, 4 * s + j) for j in range(4)]), slot)
                   for c in range(4)])
        slot = next_group()
        W = ring[slot][:, 0:4096].rearrange("p (k n) -> p k n", k=8)
        qk_chunks([((lambda k, c=c, W=W: W[:, k, c * 128:(c + 1) * 128]), 1,
                    (lambda s, c=c: KA[:, c, t0 + s * SLAB:t0 + (s + 1) * SLAB]),
                    (lambda s, c=c: [R("ka", c, kb0 + 4 * s + j) for j in range(4)]), slot)
                   for c in range(4)])
        slot = next_group()
        W = ring[slot][:, 0:4096].rearrange("p (k n) -> p k n", k=8)
        qk_chunks([((lambda k, c=c, W=W: W[:, k, c * 128:(c + 1) * 128]), 2,
                    (lambda s, c=c: QB[:, c, slab(s)]),
                    (lambda s, c=c: [R("qb", c, 4 * s + j) for j in range(4)]), slot)
                   for c in range(4)])
        slot = next_group()
        W = ring[slot][:, 0:2048].rearrange("p (k n) -> p k n", k=8)
        Wvb = ring[slot][:, 2048:3072].rearrange("p (k n) -> p k n", k=8)
        qk_chunks([((lambda k, c=c, W=W: W[:, k, c * 128:(c + 1) * 128]), 3,
                    (lambda s, c=c: KB[:, c, t0 + s * SLAB:t0 + (s + 1) * SLAB]),
                    (lambda s, c=c: [R("kb", c, kb0 + 4 * s + j) for j in range(4)]), slot)
                   for c in range(2)])
        for t in range(8):
            b = 6 + t % 2
            for k in range(8):
                S.op("pe", lambda k=k, t=t: pe.matmul(
                    ps[b][:, 0:128], lhsT=xn[:, k, t * 128:(t + 1) * 128], rhs=Wvb[:, k, :],
                    start=(k == 0), stop=(k == 7)),
                    reads=[R("ring", slot, 0), R("xn", k, t // 4)], writes=[R("ps", b)],
                    signal=(k == 7))
            evac_copy(V[:, kb0 + t, 8:10, 0:64],
                      ps[b][:, 0:128].rearrange("p (a b) -> p a b", a=2),
                      [R("ps", b)], [R("vb", kb0 + t)])
        slot = next_group()
        Wva = ring[slot][:, 0:4096].rearrange("p (k n) -> p k n", k=8)
        for t in range(8):
            b = 6 + t % 2
            for k in range(8):
                S.op("pe", lambda k=k, t=t, b=b: pe.matmul(
                    ps[b][:, :], lhsT=xn[:, k, t * 128:(t + 1) * 128], rhs=Wva[:, k, :],
                    start=(k == 0), stop=(k == 7)),
                    reads=[R("ring", slot, 0), R("xn", k, t // 4)], writes=[R("ps", b)],
                    signal=(k == 7))
            evac_copy(V[:, kb0 + t, 0:8, 0:64],
                      ps[b][:, :].rearrange("p (a b) -> p a b", a=8),
                      [R("ps", b)], [R("va", kb0 + t)])

    sbank_ctr = [0]
    unit_ctr = [0]

    def m2(half):
        units = []
        for qb in range(8):
            m16 = half * 8 + qb
            for mixer in ("A", "B"):
                for g in range(2):
                    units.append((qb, m16, mixer, g))

        def stage1(u):
            qb, m16, mixer, g = u["spec"]
            nkb_full = 5 if mixer == "A" else 2
            kbs = [kb for kb in range(nkb_full) if m16 - (nkb_full - 1) + kb >= 0]
            nkb = len(kbs)
            u["kbs"] = kbs
            ui = unit_ctr[0] % 2
            unit_ctr[0] += 1
            u["ui"] = ui
            P_ = Pbuf[ui]
            order = [0, 2, 1, 3]
            u["order"] = order
            slots = [(hh, kb) for hh in order for kb in kbs]
            chunks = []
            for gi in range(2):
                base = gi * 2 * nkb
                for off in range(0, 2 * nkb, 4):
                    chunks.append(list(range(base + off, min(base + off + 4, base + 2 * nkb))))
            chunk_of = {}
            for ci, ch in enumerate(chunks):
                for sl in ch:
                    chunk_of[sl] = ci
            u["chunk_of"] = chunk_of
            SB = [0, 1, 2, 3]
            nch = len(chunks) // 2
            pair_emitters = []
            for cp in range(nch):
                def emit_pair(cp=cp):
                    pair = [(cp, chunks[cp]), (nch + cp, chunks[nch + cp])]
                    banks = []
                    for _ in pair:
                        banks.append(SB[sbank_ctr[0] % 4])
                        sbank_ctr[0] += 1
                    n = len(pair[0][1])
                    for j in range(n):
                        for pi, (ci, ch) in enumerate(pair):
                            b = banks[pi]
                            sl = ch[j]
                            hh, kb = slots[sl]
                            hd = 4 * g + hh
                            kblk = m16 - (nkb_full - 1) + kb
                            r0 = (hd % 2) * 64
                            c = hd // 2
                            if mixer == "A":
                                lhsT = KA[r0:r0 + 64, c, kblk * 128:(kblk + 1) * 128]
                                rhs = QA[r0:r0 + 64, c, qb * 128:(qb + 1) * 128]
                                rd = [R("ka", c, kblk), R("qa", c, qb)]
                            else:
                                kvh = hd // 4
                                lhsT = KB[r0:r0 + 64, kvh, kblk * 128:(kblk + 1) * 128]
                                rhs = QB[r0:r0 + 64, c, qb * 128:(qb + 1) * 128]
                                rd = [R("kb", kvh, kblk), R("qb", c, qb)]
                            S.op("pe", lambda b=b, j=j, lhsT=lhsT, rhs=rhs: pe.matmul(
                                ps[b][:, j * 128:(j + 1) * 128], lhsT=lhsT, rhs=rhs,
                                start=True, stop=True),
                                reads=rd, writes=[R("ps", b)], signal=(j == n - 1))
                    b0 = banks[0]
                    assert banks[1] == b0 + 1
                    s0 = pair[0][1][0]
                    gsz = 2 * nkb * 128
                    outv = P_[:, 0:2 * gsz].rearrange("p (g x) -> p g x", g=2)[:, :, s0 * 128:(s0 + n) * 128]
                    S.op("act", lambda: act.activation(
                        out=outv, in_=ps_all[:, b0:b0 + 2, 0:n * 128], func=AF.Exp),
                        reads=[R("ps", b0), R("ps", b0 + 1)],
                        writes=[R("p", ui, pair[0][0]), R("p", ui, pair[1][0])])
                pair_emitters.append(emit_pair)
            u["pairs"] = pair_emitters

        def stage1b(u):
            qb, m16, mixer, g = u["spec"]
            kbs, ui, nkb = u["kbs"], u["ui"], len(u["kbs"])
            order, chunk_of = u["order"], u["chunk_of"]
            P_ = Pbuf[ui]
            E = EA if mixer == "A" else EB
            for pos, hh in enumerate(order):
                hd = 4 * g + hh
                lo = pos * nkb
                segs = sorted(set(chunk_of[sl] for sl in range(lo, lo + nkb)))
                rr = [R("p", ui, sgm) for sgm in segs]
                en_ = "pool" if pos == 3 else "dve"
                eo_ = pool if pos == 3 else dve
                S.op(en_, lambda lo=lo, hd=hd, P_=P_, E=E, kbs=kbs, nkb=nkb, eo_=eo_: eo_.tensor_tensor(
                    out=P_[:, lo * 128:(lo + nkb) * 128], in0=P_[:, lo * 128:(lo + nkb) * 128],
                    in1=E[:, hd, kbs[0]:kbs[0] + nkb, :].rearrange("p a b -> p (a b)"), op=ALU.mult),
                    reads=rr + [R("EA" if mixer == "A" else "EB")], writes=rr)

        def stage2_head(u, hh):
            qb, m16, mixer, g = u["spec"]
            nkb_full = 5 if mixer == "A" else 2
            kbs, ui, nkb = u["kbs"], u["ui"], len(u["kbs"])
            P_ = Pbuf[ui]
            ob = 4 + ui
            u["ob"] = ob
            hd = 4 * g + hh
            vh = hd if mixer == "A" else 8 + hd // 4
            for i, kb in enumerate(kbs):
                ti = u["order"].index(hh) * nkb + i
                kblk = m16 - (nkb_full - 1) + kb
                S.op("pe", lambda ti=ti, kblk=kblk, i=i: pe.matmul(
                    ps[ob][:, hh * 65:(hh + 1) * 65], lhsT=P_[:, ti * 128:(ti + 1) * 128],
                    rhs=V[:, kblk, vh, :], start=(i == 0), stop=(i == nkb - 1)),
                    reads=[R("p", ui, u["chunk_of"][ti]), R("va" if mixer == "A" else "vb", kblk),
                           R("vones")],
                    writes=[R("ps", ob)], signal=(i == nkb - 1))

        def stage2_norm(u):
            qb, m16, mixer, g = u["spec"]
            ui = u["ui"]
            ob = u["ob"]
            O3 = ps[ob][:, 0:260].rearrange("p (h e) -> p h e", e=65)
            if mixer == "A":
                S.op("dve", lambda: dve.reciprocal(out=rcp[ui][:, :].rearrange("p (h e) -> p h e", e=1),
                                                   in_=O3[:, :, 64:65]),
                     reads=[R("ps", ob)], writes=[R("rcp", ui)])
            else:
                S.op("dve", lambda: dve.tensor_tensor(
                    out=den[ui][:, :].rearrange("p (h e) -> p h e", e=1), in0=O3[:, :, 64:65],
                    in1=esink[:, 4 * g:4 * g + 4].rearrange("p (h e) -> p h e", e=1), op=ALU.add),
                    reads=[R("ps", ob), R("esink")], writes=[R("den", ui)])
                S.op("dve", lambda: dve.reciprocal(out=rcp[ui][:, :], in_=den[ui][:, :]),
                     reads=[R("den", ui)], writes=[R("rcp", ui)])
            for hh in range(4):
                S.op("dve", lambda hh=hh: dve.tensor_scalar(
                    out=ytok[ui][:, hh * 64:(hh + 1) * 64], in0=ps[ob][:, hh * 65:hh * 65 + 64],
                    scalar1=rcp[ui][:, hh:hh + 1], scalar2=None, op0=ALU.mult),
                    reads=[R("ps", ob), R("rcp", ui)], writes=[R("ytok", ui)])

        def stage3(u):
            qb, m16, mixer, g = u["spec"]
            ui = u["ui"]
            tb = 6 + ui
            Tb = ps[tb][:, 0:128].bitcast(BF16)
            for i in range(2):
                S.op("pe", lambda i=i: pe.transpose(
                    out=Tb[:, i * 128:(i + 1) * 128], in_=ytok[ui][:, i * 128:(i + 1) * 128],
                    identity=ident_bf[:, :]),
                    reads=[R("ytok", ui), R("ident_bf")], writes=[R("ps", tb)], signal=(i == 1))
            Q = QA if mixer == "A" else QB
            nm = "qa" if mixer == "A" else "qb"
            evac_copy(Q[:, 2 * g:2 * g + 2, qb * 128:(qb + 1) * 128],
                      Tb.rearrange("p (a b) -> p a b", a=2),
                      [R("ps", tb)], [R(nm, 2 * g, qb), R(nm, 2 * g + 1, qb)], force="dve")

        us = [{"spec": sp_} for sp_ in units]
        n = len(us)
        for i in range(n + 2):
            if i < n:
                stage1(us[i])
                for pr in us[i]["pairs"]:
                    pr()
                stage1b(us[i])
            if 0 <= i - 1 < n:
                for hh in range(4):
                    stage2_head(us[i - 1], hh)
                stage2_norm(us[i - 1])
            if 0 <= i - 2 < n:
                stage3(us[i - 2])

    m3_ctr = [0]

    def m3():
        for G in range(4):
            slot = next_group()
            r = ring[slot]
            Wga = r[:, 0:2048].rearrange("p (k n) -> p k n", k=8)
            Wgb = r[:, 2048:4096].rearrange("p (k n) -> p k n", k=8)
            WA = r[:, 4096:5120].rearrange("p (k n) -> p k n", k=4)
            WB = r[:, 5120:6144].rearrange("p (k n) -> p k n", k=4)
            for mm in range(2):
                m = 2 * G + mm
                for s in range(2):
                    par = m3_ctr[0] % 2
                    m3_ctr[0] += 1
                    bga, bgb, bpa, bpb = [4 * par + i for i in range(4)]
                    for k in range(8):
                        for (W, b) in ((Wga, bga), (Wgb, bgb)):
                            S.op("pe", lambda W=W, b=b, k=k: pe.matmul(
                                ps[b][:, :], lhsT=W[:, k, mm * 128:(mm + 1) * 128],
                                rhs=xn[:, k, slab(s)], start=(k == 0), stop=(k == 7)),
                                reads=[R("ring", slot, 0), R("xn", k, s)], writes=[R("ps", b)],
                                signal=(k == 7))
                    for (W, b, Q, nm) in ((WA, bpa, QA, "qa"), (WB, bpb, QB, "qb")):
                        for c in range(4):
                            S.op("pe", lambda W=W, b=b, c=c, Q=Q: pe.matmul(
                                ps[b][:, :], lhsT=W[:, c, mm * 128:(mm + 1) * 128],
                                rhs=Q[:, c, slab(s)], start=(c == 0), stop=(c == 3)),
                                reads=[R("ring", slot, 1)] + [R(nm, c, 4 * s + j) for j in range(4)],
                                writes=[R("ps", b)], signal=(c == 3))
                    sa, sb2 = sg[2 * par], sg[2 * par + 1]
                    S.op("act", lambda: act.activation(out=sa[:, :], in_=ps[bga][:, :],
                                                       func=AF.Sigmoid),
                         reads=[R("ps", bga)], writes=[R("sg", 2 * par)])
                    S.op("act", lambda: act.activation(out=sb2[:, :], in_=ps[bgb][:, :],
                                                       func=AF.Sigmoid),
                         reads=[R("ps", bgb)], writes=[R("sg", 2 * par + 1)])
                    S.op("dve", lambda: dve.tensor_tensor(out=t1[par][:, :], in0=ps[bpa][:, :],
                                                          in1=sa[:, :], op=ALU.mult),
                         reads=[R("ps", bpa), R("sg", 2 * par)], writes=[R("t1", par)])
                    S.op("dve", lambda: dve.tensor_tensor(out=t2[par][:, :], in0=ps[bpb][:, :],
                                                          in1=sb2[:, :], op=ALU.mult),
                         reads=[R("ps", bpb), R("sg", 2 * par + 1)], writes=[R("t2", par)])
                    S.op("pool", lambda m=m, s=s, par=par: pool.tensor_tensor(
                        out=merged[:, m, slab(s)], in0=t1[par][:, :], in1=t2[par][:, :], op=ALU.add),
                        reads=[R("t1", par), R("t2", par)],
                        writes=[R("mg", m, s), R("stg", m // 2)])
        oc = 0
        for H in range(2):
            slot = next_group()
            Wo = ring[slot][:, 0:4096].rearrange("p (k n) -> p k n", k=8)
            for mp in range(4):
                mo = 4 * H + mp
                for s in range(2):
                    b = oc % 4
                    oc += 1
                    for m in range(8):
                        S.op("pe", lambda m=m, b=b, mp=mp, s=s: pe.matmul(
                            ps[b][:, :], lhsT=Wo[:, m, mp * 128:(mp + 1) * 128],
                            rhs=merged[:, m, slab(s)], start=(m == 0), stop=(m == 7)),
                            reads=[R("ring", slot, 0), R("mg", m, s)], writes=[R("ps", b)],
                            signal=(m == 7))
                    S.op("dve", lambda b=b, mo=mo, s=s: dve.tensor_tensor(
                        out=h[:, mo, slab(s)], in0=ps[b][:, :], in1=h[:, mo, slab(s)], op=ALU.add),
                        reads=[R("ps", b), R("h", mo, s)], writes=[R("h", mo, s)])

    def p_items(tok0):
        def dma_p(t):
            st = pstg[t % 2]
            S.dma("sp", [lambda: sp.dma_start(
                out=st[:, :], in_=p_d[tok0 + t * 128:tok0 + (t + 1) * 128, :])],
                "pstg%d" % (t % 2), writes=[R("pstg", t % 2)])

        def xp(t):
            st = pstg[t % 2]
            b = 4 + t % 2
            for j in range(2):
                S.op("pe", lambda j=j: pe.transpose(
                    out=ps[b][:, j * 128:(j + 1) * 128], in_=st[:, j * 128:(j + 1) * 128],
                    identity=ident[:, :]),
                    reads=[R("pstg", t % 2), R("ident")], writes=[R("ps", b)], signal=(j == 1))
            evac_copy(pT[:, 0:2, t * 128:(t + 1) * 128],
                      ps[b][:, 0:256].rearrange("p (a b) -> p a b", a=2),
                      [R("ps", b)], [R("pT", t // 4)])

        def mk(i):
            def it():
                if i >= 1:
                    xp(i - 1)
                if i < 8:
                    dma_p(i)
            return it
        return [mk(i) for i in range(9)]

    def ple(tok0):
        oc = 0
        for H in range(2):
            slot = next_group()
            r = ring[slot]
            Wpg = r[:, 0:4096].rearrange("p (k n) -> p k n", k=8)
            Wpe = r[:, 4096:5120].rearrange("p (k n) -> p k n", k=2)
            for mp in range(4):
                mo = 4 * H + mp
                for s in range(2):
                    par = oc % 2
                    oc += 1
                    bg, bp = par, 2 + par
                    for k in range(8):
                        S.op("pe", lambda k=k, bg=bg, mp=mp, s=s: pe.matmul(
                            ps[bg][:, :], lhsT=Wpg[:, k, mp * 128:(mp + 1) * 128],
                            rhs=xn[:, k, slab(s)], start=(k == 0), stop=(k == 7)),
                            reads=[R("ring", slot, 0), R("xn", k, s)], writes=[R("ps", bg)],
                            signal=(k == 7))
                    for k in range(2):
                        S.op("pe", lambda k=k, bp=bp, mp=mp, s=s: pe.matmul(
                            ps[bp][:, :], lhsT=Wpe[:, k, mp * 128:(mp + 1) * 128],
                            rhs=pT[:, k, slab(s)], start=(k == 0), stop=(k == 1)),
                            reads=[R("ring", slot, 1), R("pT", s)], writes=[R("ps", bp)],
                            signal=(k == 1))
                    S.op("act", lambda bg=bg, par=par: act.activation(
                        out=t2[par][:, :], in_=ps[bg][:, :], func=AF.Sigmoid),
                        reads=[R("ps", bg)], writes=[R("t2", par)])
                    S.op("dve", lambda bp=bp, par=par: dve.tensor_tensor(
                        out=t1[par][:, :], in0=ps[bp][:, :], in1=t2[par][:, :], op=ALU.mult),
                        reads=[R("ps", bp), R("t2", par)], writes=[R("t1", par)])
                    S.op("pool", lambda mo=mo, s=s, par=par: pool.tensor_tensor(
                        out=h[:, mo, slab(s)], in0=h[:, mo, slab(s)], in1=t1[par][:, :], op=ALU.add),
                        reads=[R("t1", par), R("h", mo, s)], writes=[R("h", mo, s)])

    def store_out(tok0):
        for t in range(8):
            oi = t % NOST
            st = ostg[oi]
            for hf in range(2):
                b = (2 * t + hf) % 8
                for j in range(4):
                    k = hf * 4 + j
                    S.op("pe", lambda b=b, j=j, k=k, t=t: pe.transpose(
                        out=ps[b][:, j * 128:(j + 1) * 128], in_=h[:, k, t * 128:(t + 1) * 128],
                        identity=ident[:, :]),
                        reads=[R("h", k, t // 4), R("ident")], writes=[R("ps", b)],
                        signal=(j == 3))
                evac_copy(st[:, hf * 512:(hf + 1) * 512], ps[b][:, :],
                          [R("ps", b)], ostg_res(oi))
            S.dma("sp", [lambda t=t, st=st: sp.dma_start(
                out=out_d[tok0 + t * 128:tok0 + (t + 1) * 128, :], in_=st)],
                "ostg%d" % oi, reads=ostg_res(oi))

    def dump_dbg(what):
        allr = list(S.res.values())
        if what in ("m1", "m2"):
            return dump_h()
        if what in ("m1q", "m2y"):
            fns = [lambda: pool.dma_start(out=dbg_d[:, 0:4, :], in_=QA[:, :, :]),
                   lambda: pool.dma_start(out=dbg_d[:, 4:8, :], in_=QB[:, :, :])]
        elif what == "m1k":
            fns = [lambda: pool.dma_start(out=dbg_d[:, 0:4, :], in_=KA[:, :, 0:PASS]),
                   lambda: pool.dma_start(out=dbg_d[:, 4:6, :], in_=KB[:, :, 0:PASS])]
        elif what == "m1v":
            fns = [lambda: pool.dma_start(
                out=dbg_d[:, 0:6, :].rearrange("p a b -> p (a b)")[:, 0:5200].rearrange(
                    "p (a b) -> p a b", a=8),
                in_=V[:, 0:8].rearrange("p a b c -> p a (b c)"))]
        S.dma("pool", fns, "dbg", reads=allr)
        S.wait_sem("pool", "dbg")

    def dump_h():
        S.dma("sp", [lambda: sp.dma_start(out=dbg_d, in_=h[:, :, :])], "dbg",
              reads=[R("h", k, s) for k in range(8) for s in range(2)])
        S.wait_sem("sp", "dbg")

    def tok_of(pi):
        return (pi // 2) * SEQ + (pi % 2) * PASS

    for t in range(8):
        issue_x(tok_of(0), t)
    for ps_i in range(npass):
        seq, half = ps_i // 2, ps_i % 2
        tok0 = tok_of(ps_i)
        load_x(tok0)
        if stop == "load":
            dump_h(); break
        norm(0)
        ffn(0, extras=build_E_items() if ps_i == 0 else ())
        if stop == "ffn1":
            dump_h(); break
        norm(1)
        m1(half)
        if stop in ("m1", "m1q", "m1k", "m1v"):
            dump_dbg(stop); break
        m2(half)
        if stop in ("m2", "m2y"):
            dump_dbg(stop); break
        m3()
        if ps_i + 1 < npass:
            for t in range(4):
                issue_x(tok_of(ps_i + 1), t)
        if stop == "mix":
            dump_h(); break
        norm(2)
        ffn(1, extras=p_items(tok0))
        if stop == "ffn2":
            dump_h(); break
        norm(3)
        ple(tok0)
        if stop == "ple":
            dump_h(); break
        if ps_i + 1 < npass:
            for t in range(4, 8):
                issue_x(tok_of(ps_i + 1), t)
        store_out(tok0)
    for i in range(NOST):
        S.wait_sem("sp", "ostg%d" % i)
    return nc


def _host_consts():
    kl = np.arange(128)[:, None, None]
    kb = np.arange(5)[None, :, None]
    ql = np.arange(128)[None, None, :]
    rel = ql + 512 - 128 * kb - kl
    idxA = np.clip(rel, -128, 128) + 128
    jc = (128 * kb + kl) // 64
    qc = ql // 64
    validA = (jc >= qc) & (jc <= qc + 8)
    maskA = np.where(validA, 0.0, NEG).astype(np.float32)
    maskA = np.broadcast_to(maskA[:, None], (128, 8, 5, 128)).copy()
    kb2 = np.arange(2)[None, :, None]
    relB = (128 + ql) - (128 * kb2 + kl)
    jcB = (128 * kb2 + kl) // 64
    validB = (jcB >= qc) & (jcB <= qc + 2)
    slopes = np.array([2.0 ** (-8.0 * (hh + 1) / 8) for hh in range(8)], dtype=np.float32)
    biasB = -slopes[None, :, None, None] * np.abs(relB).astype(np.float32)[:, None]
    biasB = np.where(validB[:, None], biasB, NEG).astype(np.float32)
    return idxA, maskA, np.ascontiguousarray(biasB)


def make_in_maps(inputs):
    f = lambda a: np.ascontiguousarray(np.asarray(a, dtype=np.float32))
    x = f(inputs["x"])
    p = f(inputs["p"])[0]
    idxA, maskA, biasB = _host_consts()
    arb = f(inputs["a_rel_bias"])[0]
    biasA = np.ascontiguousarray(np.transpose(arb[:, idxA], (1, 0, 2, 3)))
    gains = np.stack([f(inputs[n])[0].reshape(8, 128).T for n in
                      ("ffn1_norm", "mix_norm", "ffn2_norm", "ple_norm")], axis=1)
    gains = np.ascontiguousarray(gains.reshape(128, 32))
    gqk = np.stack([np.tile(f(inputs[n])[0], 2) for n in
                    ("a_q_norm", "a_k_norm", "b_q_norm", "b_k_norm")], axis=1)
    gqk = np.ascontiguousarray(gqk)
    sinks = np.ascontiguousarray(np.broadcast_to(f(inputs["b_sinks"])[0][None, :], (128, 8)))
    shared = {
        "ffn1_w_gu": f(inputs["ffn1_w_gu"])[0], "ffn2_w_gu": f(inputs["ffn2_w_gu"])[0],
        "ffn1_w_down": f(inputs["ffn1_w_down"])[0], "ffn2_w_down": f(inputs["ffn2_w_down"])[0],
        "w_in": f(inputs["w_in"])[0], "w_gate": f(inputs["w_gate"])[0],
        "w_proj_a": f(inputs["w_proj_a"])[0], "w_proj_b": f(inputs["w_proj_b"])[0],
        "w_out": f(inputs["w_out"])[0], "w_ple_gate": f(inputs["w_ple_gate"])[0],
        "w_ple_proj": f(inputs["w_ple_proj"])[0],
        "gains": gains, "gqk": gqk, "sinks": sinks, "ident": np.eye(128, dtype=np.float32),
        "biasA": biasA, "maskA": maskA, "biasB": biasB,
    }
    in_maps = []
    for c in range(N_CORES):
        m = dict(shared)
        m["x"] = np.ascontiguousarray(x[2 * c:2 * c + 2].reshape(TOK_CORE, D))
        m["p"] = np.ascontiguousarray(p[2 * c:2 * c + 2].reshape(TOK_CORE, 256))
        in_maps.append(m)
    return in_maps


def kernel(**inputs):
    nc = build_nc()
    in_maps = make_in_maps(inputs)
    res = run_bass_kernel_spmd(nc, in_maps, core_ids=list(range(N_CORES)))
    out = np.stack([np.asarray(r["out"]).reshape(2, SEQ, D) for r in res.results], axis=0)
    return out.reshape(16, SEQ, D).astype(np.float32)
```

```python
import numpy as np
import concourse.bass as bass
import concourse.mybir as mybir
from concourse.bass_utils import run_bass_kernel_spmd

F32 = mybir.dt.float32
BF16 = mybir.dt.bfloat16
AF = mybir.ActivationFunctionType
ALU = mybir.AluOpType

N_CORES = 8
D = 1024
KC = 8
DFF = 2816
SEQ = 2048
PASS = 1024
SLAB = 512
TOK_CORE = 4096
EPS = 1e-6
NEG = -30000.0
STRICT = True


class Res:
    __slots__ = ("w", "rs")

    def __init__(self):
        self.w = None
        self.rs = {}


class Sched:
    def __init__(self, nc):
        self.nc = nc
        self.eng = {"pe": nc.tensor, "act": nc.scalar, "dve": nc.vector,
                    "pool": nc.gpsimd, "sp": nc.sync}
        self.sems = {}
        self.cnt = {}
        self.seen = {e: {} for e in self.eng}
        self.res = {}
        for e in ("pe", "act", "dve", "pool"):
            self.newsem(e)

    def newsem(self, key):
        if key not in self.sems:
            self.sems[key] = self.nc.alloc_semaphore("s_" + key)
            self.cnt[key] = 0

    def R(self, *key):
        r = self.res.get(key)
        if r is None:
            r = self.res[key] = Res()
        return r

    def _waits(self, eng, reads, writes):
        waits = {}

        def need(st):
            if st is None:
                return
            k, v = st
            if k == eng and (eng == "pe" or not STRICT):
                return
            if self.seen[eng].get(k, 0) >= v:
                return
            if waits.get(k, 0) < v:
                waits[k] = v

        for r in reads:
            need(r.w)
        for r in writes:
            need(r.w)
            for k, v in r.rs.items():
                need((k, v))
        for k, v in waits.items():
            self.eng[eng].wait_ge(self.sems[k], v)
            self.seen[eng][k] = v

    def _stamp(self, st, reads, writes):
        k, v = st
        for r in reads:
            if r.rs.get(k, 0) < v:
                r.rs[k] = v
        for r in writes:
            r.w = st
            r.rs = {}

    def op(self, eng, fn, reads=(), writes=(), signal=True):
        self._waits(eng, reads, writes)
        inst = fn()
        if eng == "pe" and not signal:
            st = ("pe", self.cnt["pe"] + 1)
        else:
            self.cnt[eng] += 1
            inst.then_inc(self.sems[eng], 1)
            st = (eng, self.cnt[eng])
        self._stamp(st, reads, writes)

    def dma(self, eng, fns, dsem, reads=(), writes=()):
        self.newsem(dsem)
        self._waits(eng, reads, writes)
        for fn in fns:
            inst = fn()
            self.cnt[dsem] += 16
            inst.then_inc(self.sems[dsem], 16)
        self._stamp((dsem, self.cnt[dsem]), reads, writes)

    def wait_sem(self, eng, key):
        if key in self.sems and self.cnt[key] > 0:
            self.eng[eng].wait_ge(self.sems[key], self.cnt[key])


def build_nc(stop=None, npass=4):
    nc = bass.Bass("TRN2", target_bir_lowering=False)
    S = Sched(nc)
    R = S.R

    def din(name, shape):
        return nc.dram_tensor(name, list(shape), F32, kind="ExternalInput").ap()

    x_d = din("x", [TOK_CORE, D])
    p_d = din("p", [TOK_CORE, 256])
    wgu_d = [din("ffn1_w_gu", [D, 2 * DFF]), din("ffn2_w_gu", [D, 2 * DFF])]
    wdn_d = [din("ffn1_w_down", [DFF, D]), din("ffn2_w_down", [DFF, D])]
    win_d = din("w_in", [D, 2304])
    wgate_d = din("w_gate", [D, 2 * D])
    wpa_d = din("w_proj_a", [512, D])
    wpb_d = din("w_proj_b", [512, D])
    wout_d = din("w_out", [D, D])
    wpg_d = din("w_ple_gate", [D, D])
    wpe_d = din("w_ple_proj", [256, D])
    gains_d = din("gains", [128, 32])
    gqk_d = din("gqk", [128, 4])
    sinks_d = din("sinks", [128, 8])
    ident_d = din("ident", [128, 128])
    biasA_d = din("biasA", [128, 8, 5, 128])
    maskA_d = din("maskA", [128, 8, 5, 128])
    biasB_d = din("biasB", [128, 8, 2, 128])
    out_d = nc.dram_tensor("out", [TOK_CORE, D], F32, kind="ExternalOutput").ap()
    dbg_d = None
    if stop is not None:
        dbg_d = nc.dram_tensor("dbg", [128, 8, PASS], F32, kind="ExternalOutput").ap()

    def sb(name, shape, dt):
        return nc.alloc_sbuf_tensor("sb_" + name, list(shape), dt)

    h = sb("h", [128, 8, PASS], F32)
    xn = sb("xn", [128, 8, PASS], BF16)
    QA = sb("QA", [128, 4, PASS], BF16)
    QB = sb("QB", [128, 4, PASS], BF16)
    KA = sb("KA", [128, 4, SEQ], BF16)
    KB = sb("KB", [128, 2, SEQ], BF16)
    V = sb("V", [128, 16, 10, 65], BF16)
    EA = sb("EA", [128, 8, 5, 128], BF16)
    EB = sb("EB", [128, 8, 2, 128], BF16)
    ring = [sb("ring%d" % i, [128, 6144], BF16) for i in range(2)]
    scr = sb("scr", [128, 4096], BF16)
    actb = [scr[:, 1024 * i:1024 * (i + 1)].rearrange("p (a b) -> p a b", a=2) for i in range(2)]
    Pbuf = [sb("P%d" % i, [128, 20 * 128], BF16) for i in range(2)]
    Pbuf.append(scr[:, 0:2560])
    merged = sb("merged", [128, 8, PASS], BF16)
    stg = [merged[:, 2 * i:2 * i + 2, :].bitcast(F32) for i in range(4)]
    stg = [s.rearrange("p a b -> p (a b)") for s in stg]
    ostg = [Pbuf[i][:, 0:2048].bitcast(F32) for i in range(2)]
    for Q_ in (QA, QB):
        for i in range(2):
            ostg.append(Q_[:, 2 * i:2 * i + 2, :].bitcast(F32).rearrange("p a b -> p (a b)"))
    ptmp = [Pbuf[i][:, 0:2560].bitcast(F32) for i in range(2)]

    def stg_res(i):
        return [R("stg", i)] + [R("mg", 2 * i + a, s_) for a in range(2) for s_ in range(2)]

    def p_alias(ui, s0, n):
        if ui != 2:
            return []
        out = []
        for sl in range(s0, s0 + n):
            r = R("act", sl // 8, (sl % 8) // 4) if sl < 16 else R("sg", 0)
            if r not in out:
                out.append(r)
        return out

    def pbuf_res(i):
        return [R("p", i, c) for c in range(6)]

    def ostg_res(i):
        if i < 2:
            return pbuf_res(i)
        if i >= 6:
            nm = "t1" if i == 6 else "t2"
            return [R(nm, 0), R(nm, 1)]
        nm = "qa" if i < 4 else "qb"
        c0 = 2 * (i % 2)
        return [R(nm, c0 + a, qb_) for a in range(2) for qb_ in range(8)]
    sq = [sb("sq%d" % i, [128, 512], BF16) for i in range(4)]
    lnv = sb("lnv", [128, 512], F32)
    rstd = [sb("rstd%d" % i, [128, 512], F32) for i in range(2)]
    sg = [scr[:, 2048 + 512 * i:2048 + 512 * (i + 1)] for i in range(4)]
    t1all = sb("t1all", [128, 2, 512], F32)
    t2all = sb("t2all", [128, 2, 512], F32)
    t1 = [t1all[:, i, :] for i in range(2)]
    t2 = [t2all[:, i, :] for i in range(2)]
    ostg.append(t1all[:, :, :].rearrange("p a b -> p (a b)"))
    ostg.append(t2all[:, :, :].rearrange("p a b -> p (a b)"))
    NOST = len(ostg)
    pT = sb("pT", [128, 2, PASS], BF16)
    pstg = [sb("pstg%d" % i, [128, 256], F32) for i in range(2)]
    ytok = [sb("ytok%d" % i, [128, 256], BF16) for i in range(2)]
    ident = sb("ident", [128, 128], F32)
    ident_bf = sb("ident_bf", [128, 128], BF16)
    ones_bf = sb("ones_bf", [128, 128], BF16)
    bones = sb("bones", [128, 128], BF16)
    gains = sb("gains", [128, 32], F32)
    gqk = sb("gqk", [128, 4], F32)
    esink = sb("esink", [128, 8], F32)
    den = [sb("den%d" % i, [128, 4], F32) for i in range(2)]
    rcp = [sb("rcp%d" % i, [128, 4], F32) for i in range(2)]
    warm = sb("warm", [128, 2], F32)
    ps = [nc.alloc_psum_tensor("ps%d" % i, [128, 512], F32) for i in range(8)]

    pe, act, dve, pool, sp = nc.tensor, nc.scalar, nc.vector, nc.gpsimd, nc.sync

    S.dma("sp", [
        lambda: sp.dma_start(out=ident[:, :], in_=ident_d),
        lambda: sp.dma_start(out=gains[:, :], in_=gains_d),
        lambda: sp.dma_start(out=gqk[:, :], in_=gqk_d),
        lambda: sp.dma_start(out=esink[:, :], in_=sinks_d),
    ], "setup", writes=[R("ident"), R("gains"), R("gqk"), R("esink")])
    S.op("dve", lambda: dve.tensor_copy(out=ident_bf[:, :], in_=ident[:, :]),
         reads=[R("ident")], writes=[R("ident_bf")])
    S.op("dve", lambda: dve.memset(ones_bf[:, :], 1.0), writes=[R("ones")])
    S.op("dve", lambda: dve.memset(warm[:, :], 1.0), writes=[R("warm")])
    S.op("dve", lambda: dve.memset(bones[:, :], 0.0), writes=[R("bones")])
    S.op("dve", lambda: dve.memset(bones[0:64, 0:64], 1.0), writes=[R("bones")])
    S.op("dve", lambda: dve.memset(bones[64:128, 64:128], 1.0), writes=[R("bones")])
    S.op("dve", lambda: dve.memset(V[:, :, :, 64:65], 1.0), writes=[R("vones")])
    S.op("dve", lambda: dve.tensor_scalar(out=gqk[:, 0:1], in0=gqk[:, 0:1], scalar1=0.125,
                                          scalar2=None, op0=ALU.mult),
         reads=[R("gqk")], writes=[R("gqk")])
    S.op("dve", lambda: dve.tensor_scalar(out=gqk[:, 2:3], in0=gqk[:, 2:3], scalar1=0.125,
                                          scalar2=None, op0=ALU.mult),
         reads=[R("gqk")], writes=[R("gqk")])
    S.op("act", lambda: act.activation(out=esink[:, :], in_=esink[:, :], func=AF.Exp),
         reads=[R("esink")], writes=[R("esink")])
    def build_E_items():
        items = []
        for hd in range(8):
            def it(hd=hd):
                i = hd % 2
                a = ptmp[i][:, 0:640]
                b = ptmp[i][:, 640:1280]
                S.dma("sp", [
                    lambda: sp.dma_start(out=a, in_=biasA_d[:, hd].rearrange("p a b -> p (a b)")),
                    lambda: sp.dma_start(out=b, in_=maskA_d[:, hd].rearrange("p a b -> p (a b)")),
                ], "setupE%d" % i, writes=pbuf_res(i))
                S.op("dve", lambda: dve.tensor_tensor(out=a, in0=a, in1=b, op=ALU.add),
                     reads=pbuf_res(i), writes=pbuf_res(i))
                S.op("act", lambda: act.activation(
                    out=EA[:, hd].rearrange("p a b -> p (a b)"), in_=a, func=AF.Exp),
                    reads=pbuf_res(i), writes=[R("EA")])
            items.append(it)
        for hp in range(4):
            def it(hp=hp):
                i = hp % 2
                a = ptmp[i][:, 0:512]
                S.dma("sp", [
                    lambda: sp.dma_start(
                        out=a, in_=biasB_d[:, 2 * hp:2 * hp + 2].rearrange("p h a b -> p (h a b)")),
                ], "setupE%d" % i, writes=pbuf_res(i))
                S.op("act", lambda: act.activation(
                    out=EB[:, 2 * hp:2 * hp + 2].rearrange("p h a b -> p (h a b)"), in_=a, func=AF.Exp),
                    reads=pbuf_res(i), writes=[R("EB")])
            items.append(it)
        return items

    def wv(dram, p=128):
        return dram.rearrange("(kc p) n -> p kc n", p=p)

    def ffn_group(w, g):
        def pieces(slot):
            r = ring[slot]
            return [
                (r[:, 0:2048].rearrange("p (k n) -> p k n", k=8),
                 wv(wgu_d[w])[:, :, 256 * g:256 * g + 256], 0),
                (r[:, 2048:4096].rearrange("p (k n) -> p k n", k=8),
                 wv(wgu_d[w])[:, :, DFF + 256 * g:DFF + 256 * g + 256], 0),
                (r[:, 4096:6144].rearrange("p (k n) -> p k n", k=2),
                 wv(wdn_d[w])[:, 2 * g:2 * g + 2, :], 1),
            ]
        return pieces

    def cols_group(dram, c0, n, kc=8):
        def pieces(slot):
            r = ring[slot]
            return [(r[:, 0:kc * n].rearrange("p (k n) -> p k n", k=kc),
                     wv(dram)[:, :, c0:c0 + n], 0)]
        return pieces

    def kbvb_group():
        def pieces(slot):
            r = ring[slot]
            kd = r[:, 0:2048].rearrange("p (k n) -> p k n", k=8)
            out = []
            for kvh in range(2):
                for dup in range(2):
                    out.append((kd[:, :, kvh * 128 + dup * 64:kvh * 128 + dup * 64 + 64],
                                wv(win_d)[:, :, 2048 + kvh * 64:2048 + kvh * 64 + 64], 0))
            out.append((r[:, 2048:3072].rearrange("p (k n) -> p k n", k=8),
                        wv(win_d)[:, :, 2176:2304], 0))
            return out
        return pieces

    def m3_group(G):
        def pieces(slot):
            r = ring[slot]
            return [
                (r[:, 0:2048].rearrange("p (k n) -> p k n", k=8),
                 wv(wgate_d)[:, :, 256 * G:256 * G + 256], 0),
                (r[:, 2048:4096].rearrange("p (k n) -> p k n", k=8),
                 wv(wgate_d)[:, :, D + 256 * G:D + 256 * G + 256], 0),
                (r[:, 4096:5120].rearrange("p (k n) -> p k n", k=4),
                 wv(wpa_d)[:, :, 256 * G:256 * G + 256], 1),
                (r[:, 5120:6144].rearrange("p (k n) -> p k n", k=4),
                 wv(wpb_d)[:, :, 256 * G:256 * G + 256], 1),
            ]
        return pieces

    def ple_group(H):
        def pieces(slot):
            r = ring[slot]
            return [
                (r[:, 0:4096].rearrange("p (k n) -> p k n", k=8),
                 wv(wpg_d)[:, :, 512 * H:512 * H + 512], 0),
                (r[:, 4096:5120].rearrange("p (k n) -> p k n", k=2),
                 wv(wpe_d)[:, :, 512 * H:512 * H + 512], 1),
            ]
        return pieces

    pass_groups = ([ffn_group(0, g) for g in range(11)]
                   + [cols_group(win_d, 0, 512), cols_group(win_d, 512, 512),
                      cols_group(win_d, 1536, 512), kbvb_group(),
                      cols_group(win_d, 1024, 512)]
                   + [m3_group(G) for G in range(4)]
                   + [cols_group(wout_d, 0, 512), cols_group(wout_d, 512, 512)]
                   + [ffn_group(1, g) for g in range(11)]
                   + [ple_group(0), ple_group(1)])
    NG = len(pass_groups)
    all_groups = pass_groups * npass
    gstate = {"issued": [0, 0], "cur": -1}

    def issue_part(part):
        gi = gstate["issued"][part]
        if gi >= len(all_groups):
            return
        slot = gi % 2
        pcs = [(o, i) for (o, i, pt) in all_groups[gi](slot) if pt == part]
        if pcs:
            S.dma("pool", [(lambda o=o, i=i: pool.dma_start(out=o, in_=i)) for o, i in pcs],
                  "ring%d_%d" % (slot, part), writes=[R("ring", slot, part)])
        gstate["issued"][part] += 1

    def next_group(pf=True):
        gstate["cur"] += 1
        gi = gstate["cur"]
        for part in range(2):
            while gstate["issued"][part] <= gi:
                issue_part(part)
        if pf:
            prefetch(0)
            prefetch(1)
        return gi % 2

    def prefetch(part):
        if gstate["issued"][part] <= gstate["cur"] + 1:
            issue_part(part)

    def slab(s):
        return slice(s * SLAB, (s + 1) * SLAB)

    cp_rr = [0]

    def evac_copy(out, in_, reads, writes):
        cp_rr[0] ^= 1
        if cp_rr[0]:
            S.op("act", lambda: act.activation(out=out, in_=in_, func=AF.Copy),
                 reads=reads, writes=writes)
        else:
            S.op("dve", lambda: dve.tensor_copy(out=out, in_=in_), reads=reads, writes=writes)

    xn_f32 = xn[:, :, :].bitcast(F32).rearrange("p a b -> p (a b)")

    def xstage(t):
        if t < 4:
            return stg[t], stg_res(t), "stg%d" % t
        j = t - 4
        return (xn_f32[:, j * 1024:(j + 1) * 1024],
                [R("xn", 2 * j + a, s_) for a in range(2) for s_ in range(2)], "stgB%d" % j)

    def issue_x(tok0, t):
        ap, res, sem = xstage(t)
        S.dma("sp", [lambda: sp.dma_start(
            out=ap, in_=x_d[tok0 + t * 128:tok0 + (t + 1) * 128, :])],
            sem, writes=res)

    def load_x(tok0):
        for t in range(8):
            st, res, _ = xstage(t)
            for hf in range(2):
                b = (2 * t + hf) % 8
                for j in range(4):
                    k = hf * 4 + j
                    S.op("pe", lambda b=b, j=j, k=k, st=st: pe.transpose(
                        out=ps[b][:, j * 128:(j + 1) * 128], in_=st[:, k * 128:(k + 1) * 128],
                        identity=ident[:, :]),
                        reads=res + [R("ident")], writes=[R("ps", b)], signal=(j == 3))
                evac_copy(h[:, hf * 4:hf * 4 + 4, t * 128:(t + 1) * 128],
                          ps[b][:, :].rearrange("p (a b) -> p a b", a=4),
                          [R("ps", b)], [R("h", hf * 4 + j, t // 4) for j in range(4)])

    def norm(gidx):
        S.op("act", lambda: act.activation(out=warm[:, 1:2], in_=warm[:, 0:1], func=AF.Ln),
             reads=[R("warm")], writes=[R("warm_o")])
        for s in range(2):
            nb = 6 + s
            for k in range(8):
                q = sq[k % 4]
                S.op("act", lambda k=k, q=q: act.activation(out=q[:, :], in_=h[:, k, slab(s)],
                                                            func=AF.Square),
                     reads=[R("h", k, s)], writes=[R("sq", k % 4)])
                S.op("pe", lambda k=k, q=q: pe.matmul(ps[nb][:, :], lhsT=ones_bf[:, :], rhs=q[:, :],
                                                      start=(k == 0), stop=(k == 7)),
                     reads=[R("sq", k % 4), R("ones")], writes=[R("ps", nb)], signal=True)
            S.op("act", lambda: act.activation(out=lnv[:, :], in_=ps[nb][:, :], func=AF.Ln,
                                               bias=EPS, scale=1.0 / D),
                 reads=[R("ps", nb)], writes=[R("lnv")])
            S.op("act", lambda: act.activation(out=ps[nb][:, :], in_=lnv[:, :], func=AF.Exp,
                                               scale=-0.5),
                 reads=[R("lnv")], writes=[R("ps", nb)])
            for k in range(8):
                S.op("dve", lambda k=k: dve.scalar_tensor_tensor(
                    out=xn[:, k, slab(s)], in0=h[:, k, slab(s)],
                    scalar=gains[:, gidx * 8 + k:gidx * 8 + k + 1], in1=ps[nb][:, :],
                    op0=ALU.mult, op1=ALU.mult),
                    reads=[R("h", k, s), R("ps", nb), R("gains")], writes=[R("xn", k, s)])

    step_ctr = [0]

    def ffn(w, extras=()):
        extras = list(extras)
        prev = None
        for g in range(11):
            slot = next_group(pf=False)
            prefetch(0)
            r = ring[slot]
            Wg = r[:, 0:2048].rearrange("p (k n) -> p k n", k=8)
            Wu = r[:, 2048:4096].rearrange("p (k n) -> p k n", k=8)
            Wd = r[:, 4096:6144].rearrange("p (k n) -> p k n", k=2)
            for s in range(2):
                ab = step_ctr[0] % 2
                step_ctr[0] += 1
                mmlist = []
                for jj in range(2):
                    for k in range(8):
                        mmlist.append((Wg, jj, jj, k))
                        mmlist.append((Wu, jj, 2 + jj, k))
                for qi in range(4):
                    for (W, jj, b, k) in mmlist[8 * qi:8 * qi + 8]:
                        S.op("pe", lambda W=W, b=b, k=k, jj=jj: pe.matmul(
                            ps[b][:, :], lhsT=W[:, k, jj * 128:(jj + 1) * 128],
                            rhs=xn[:, k, slab(s)], start=(k == 0), stop=(k == 7)),
                            reads=[R("ring", slot, 0), R("xn", k, s)], writes=[R("ps", b)],
                            signal=(k == 7))
                    jj = qi // 2
                    if qi in (1, 3):
                        sgi = 2 * ab + jj
                        S.op("act", lambda jj=jj, sgi=sgi: act.activation(
                            out=sg[sgi][:, :], in_=ps[jj][:, :], func=AF.Silu),
                            reads=[R("ps", jj)], writes=[R("sg", sgi)])
                        S.op("dve", lambda jj=jj, sgi=sgi, ab=ab: dve.tensor_tensor(
                            out=actb[ab][:, jj, :], in0=ps[2 + jj][:, :], in1=sg[sgi][:, :],
                            op=ALU.mult),
                            reads=[R("ps", 2 + jj), R("sg", sgi)], writes=[R("act", ab, jj)])
                    if prev is not None:
                        prev[2 * qi]()
                        prev[2 * qi + 1]()
                if s == 0:
                    prefetch(1)

                def mk_pair(m, s=s, ab=ab, Wd=Wd, slot=slot):
                    def pair():
                        b = 4 + m % 4
                        for jj in range(2):
                            S.op("pe", lambda jj=jj: pe.matmul(
                                ps[b][:, :], lhsT=Wd[:, jj, m * 128:(m + 1) * 128],
                                rhs=actb[ab][:, jj, :], start=(jj == 0), stop=(jj == 1)),
                                reads=[R("ring", slot, 1), R("act", ab, jj)], writes=[R("ps", b)],
                                signal=(jj == 1))
                        S.op("dve", lambda: dve.scalar_tensor_tensor(
                            out=h[:, m, slab(s)], in0=ps[b][:, :], scalar=0.5,
                            in1=h[:, m, slab(s)], op0=ALU.mult, op1=ALU.add),
                            reads=[R("ps", b), R("h", m, s)], writes=[R("h", m, s)])
                    return pair
                prev = [mk_pair(m) for m in range(8)]
                if s == 1 and extras:
                    extras.pop(0)()
        for pr in prev:
            pr()
        for ex in extras:
            ex()

    qk_ctr = [0]

    def qk_chunks(items):
        work = [(it, s) for it in items for s in range(2)]
        pend = None
        for (it, s) in work:
            lhsT_fn, gcol, dest_fn, dres_fn, slot = it
            i = qk_ctr[0]
            qk_ctr[0] += 1
            b = i % 4
            for k in range(8):
                S.op("pe", lambda k=k, b=b, lhsT_fn=lhsT_fn, s=s: pe.matmul(
                    ps[b][:, :], lhsT=lhsT_fn(k), rhs=xn[:, k, slab(s)],
                    start=(k == 0), stop=(k == 7)),
                    reads=[R("ring", slot, 0), R("xn", k, s)], writes=[R("ps", b)], signal=(k == 7))
            S.op("act", lambda b=b, i=i: act.activation(out=sq[i % 4][:, :], in_=ps[b][:, :],
                                                        func=AF.Square),
                 reads=[R("ps", b)], writes=[R("sq", i % 4)])
            if pend is not None:
                pend()

            def rest(i=i, b=b, gcol=gcol, dest_fn=dest_fn, dres_fn=dres_fn, s=s):
                q = sq[i % 4]
                sb_ = 4 + i % 2
                rs = rstd[i % 2]
                S.op("pe", lambda: pe.matmul(ps[sb_][:, :], lhsT=bones[:, :], rhs=q[:, :],
                                             start=True, stop=True),
                     reads=[R("sq", i % 4), R("bones")], writes=[R("ps", sb_)], signal=True)
                S.op("act", lambda: act.activation(out=lnv[:, :], in_=ps[sb_][:, :], func=AF.Ln,
                                                   bias=EPS, scale=1.0 / 64),
                     reads=[R("ps", sb_)], writes=[R("lnv")])
                S.op("act", lambda: act.activation(out=rs[:, :], in_=lnv[:, :], func=AF.Exp,
                                                   scale=-0.5),
                     reads=[R("lnv")], writes=[R("rstd", i % 2)])
                S.op("dve", lambda: dve.scalar_tensor_tensor(
                    out=dest_fn(s), in0=ps[b][:, :], scalar=gqk[:, gcol:gcol + 1], in1=rs[:, :],
                    op0=ALU.mult, op1=ALU.mult),
                    reads=[R("ps", b), R("rstd", i % 2), R("gqk")], writes=dres_fn(s))
            pend = rest
        pend()

    def m1(half):
        kb0 = half * 8
        t0 = half * PASS
        slot = next_group()
        W = ring[slot][:, 0:4096].rearrange("p (k n) -> p k n", k=8)
        qk_chunks([((lambda k, c=c, W=W: W[:, k, c * 128:(c + 1) * 128]), 0,
                    (lambda s, c=c: QA[:, c, slab(s)]),
                    (lambda s, c=c: [R("qa", c, 4 * s + j) for j in range(4)]), slot)
                   for c in range(4)])
        slot = next_group()
        W = ring[slot][:, 0:4096].rearrange("p (k n) -> p k n", k=8)
        qk_chunks([((lambda k, c=c, W=W: W[:, k, c * 128:(c + 1) * 128]), 1,
                    (lambda s, c=c: KA[:, c, t0 + s * SLAB:t0 + (s + 1) * SLAB]),
                    (lambda s, c=c: [R("ka", c, kb0 + 4 * s + j) for j in range(4)]), slot)
                   for c in range(4)])
        slot = next_group()
        W = ring[slot][:, 0:4096].rearrange("p (k n) -> p k n", k=8)
        qk_chunks([((lambda k, c=c, W=W: W[:, k, c * 128:(c + 1) * 128]), 2,
                    (lambda s, c=c: QB[:, c, slab(s)]),
                    (lambda s, c=c: [R("qb", c, 4 * s + j) for j in range(4)]), slot)
                   for c in range(4)])
        slot = next_group()
        W = ring[slot][:, 0:2048].rearrange("p (k n) -> p k n", k=8)
        Wvb = ring[slot][:, 2048:3072].rearrange("p (k n) -> p k n", k=8)
        qk_chunks([((lambda k, c=c, W=W: W[:, k, c * 128:(c + 1) * 128]), 3,
                    (lambda s, c=c: KB[:, c, t0 + s * SLAB:t0 + (s + 1) * SLAB]),
                    (lambda s, c=c: [R("kb", c, kb0 + 4 * s + j) for j in range(4)]), slot)
                   for c in range(2)])
        for t in range(8):
            b = 6 + t % 2
            for k in range(8):
                S.op("pe", lambda k=k, t=t: pe.matmul(
                    ps[b][:, 0:128], lhsT=xn[:, k, t * 128:(t + 1) * 128], rhs=Wvb[:, k, :],
                    start=(k == 0), stop=(k == 7)),
                    reads=[R("ring", slot, 0), R("xn", k, t // 4)], writes=[R("ps", b)],
                    signal=(k == 7))
            evac_copy(V[:, kb0 + t, 8:10, 0:64],
                      ps[b][:, 0:128].rearrange("p (a b) -> p a b", a=2),
                      [R("ps", b)], [R("vb", kb0 + t)])
        slot = next_group()
        Wva = ring[slot][:, 0:4096].rearrange("p (k n) -> p k n", k=8)
        for t in range(8):
            b = 6 + t % 2
            for k in range(8):
                S.op("pe", lambda k=k, t=t, b=b: pe.matmul(
                    ps[b][:, :], lhsT=xn[:, k, t * 128:(t + 1) * 128], rhs=Wva[:, k, :],
                    start=(k == 0), stop=(k == 7)),
                    reads=[R("ring", slot, 0), R("xn", k, t // 4)], writes=[R("ps", b)],
                    signal=(k == 7))
            evac_copy(V[:, kb0 + t, 0:8, 0:64],
                      ps[b][:, :].rearrange("p (a b) -> p a b", a=8),
                      [R("ps", b)], [R("va", kb0 + t)])

    sbank_ctr = [0]
    unit_ctr = [0]

    def m2(half):
        units = []
        for qb in range(8):
            m16 = half * 8 + qb
            for mixer in ("A", "B"):
                for g in range(2):
                    units.append((qb, m16, mixer, g))

        def stage1(u):
            qb, m16, mixer, g = u["spec"]
            nkb_full = 5 if mixer == "A" else 2
            kbs = [kb for kb in range(nkb_full) if m16 - (nkb_full - 1) + kb >= 0]
            nkb = len(kbs)
            u["kbs"] = kbs
            ui = unit_ctr[0] % 3
            u["par"] = unit_ctr[0] % 2
            unit_ctr[0] += 1
            u["ui"] = ui
            P_ = Pbuf[ui]
            order = [0, 2, 1, 3]
            u["order"] = order
            slots = [(hh, kb) for hh in order for kb in kbs]
            chunks = []
            for gi in range(2):
                base = gi * 2 * nkb
                for off in range(0, 2 * nkb, 4):
                    chunks.append(list(range(base + off, min(base + off + 4, base + 2 * nkb))))
            chunk_of = {}
            for ci, ch in enumerate(chunks):
                for sl in ch:
                    chunk_of[sl] = ci
            u["chunk_of"] = chunk_of
            SB = [0, 1, 2, 7]
            nch = len(chunks) // 2
            pair_emitters = []
            for cp in range(nch):
                def emit_pair(cp=cp):
                    pair = [(cp, chunks[cp]), (nch + cp, chunks[nch + cp])]
                    banks = []
                    for _ in pair:
                        banks.append(SB[sbank_ctr[0] % 4])
                        sbank_ctr[0] += 1
                    n = len(pair[0][1])
                    for j in range(n):
                        for pi, (ci, ch) in enumerate(pair):
                            b = banks[pi]
                            sl = ch[j]
                            hh, kb = slots[sl]
                            hd = 4 * g + hh
                            kblk = m16 - (nkb_full - 1) + kb
                            r0 = (hd % 2) * 64
                            c = hd // 2
                            if mixer == "A":
                                lhsT = KA[r0:r0 + 64, c, kblk * 128:(kblk + 1) * 128]
                                rhs = QA[r0:r0 + 64, c, qb * 128:(qb + 1) * 128]
                                rd = [R("ka", c, kblk), R("qa", c, qb)]
                            else:
                                kvh = hd // 4
                                lhsT = KB[r0:r0 + 64, kvh, kblk * 128:(kblk + 1) * 128]
                                rhs = QB[r0:r0 + 64, c, qb * 128:(qb + 1) * 128]
                                rd = [R("kb", kvh, kblk), R("qb", c, qb)]
                            S.op("pe", lambda b=b, j=j, lhsT=lhsT, rhs=rhs: pe.matmul(
                                ps[b][:, j * 128:(j + 1) * 128], lhsT=lhsT, rhs=rhs,
                                start=True, stop=True),
                                reads=rd, writes=[R("ps", b)], signal=(j == n - 1))
                    for pi, (ci, ch) in enumerate(pair):
                        b = banks[pi]
                        s0 = ch[0]
                        S.op("act", lambda b=b, n=n, s0=s0: act.activation(
                            out=P_[:, s0 * 128:(s0 + n) * 128], in_=ps[b][:, 0:n * 128], func=AF.Exp),
                            reads=[R("ps", b)], writes=[R("p", ui, ci)] + p_alias(ui, s0, n))
                pair_emitters.append(emit_pair)
            u["pairs"] = pair_emitters

        def stage1b(u):
            qb, m16, mixer, g = u["spec"]
            kbs, ui, nkb = u["kbs"], u["ui"], len(u["kbs"])
            order, chunk_of = u["order"], u["chunk_of"]
            P_ = Pbuf[ui]
            E = EA if mixer == "A" else EB
            for pos, hh in enumerate(order):
                hd = 4 * g + hh
                lo = pos * nkb
                segs = sorted(set(chunk_of[sl] for sl in range(lo, lo + nkb)))
                rr = [R("p", ui, sgm) for sgm in segs] + p_alias(ui, lo, nkb)
                en_ = "pool" if pos == 3 else "dve"
                eo_ = pool if pos == 3 else dve
                S.op(en_, lambda lo=lo, hd=hd, P_=P_, E=E, kbs=kbs, nkb=nkb, eo_=eo_: eo_.tensor_tensor(
                    out=P_[:, lo * 128:(lo + nkb) * 128], in0=P_[:, lo * 128:(lo + nkb) * 128],
                    in1=E[:, hd, kbs[0]:kbs[0] + nkb, :].rearrange("p a b -> p (a b)"), op=ALU.mult),
                    reads=rr + [R("EA" if mixer == "A" else "EB")], writes=rr)

        def stage2_head(u, hh):
            qb, m16, mixer, g = u["spec"]
            nkb_full = 5 if mixer == "A" else 2
            kbs, ui, nkb = u["kbs"], u["ui"], len(u["kbs"])
            P_ = Pbuf[ui]
            ob = 3 + u["par"]
            u["ob"] = ob
            hd = 4 * g + hh
            vh = hd if mixer == "A" else 8 + hd // 4
            for i, kb in enumerate(kbs):
                ti = u["order"].index(hh) * nkb + i
                kblk = m16 - (nkb_full - 1) + kb
                S.op("pe", lambda ti=ti, kblk=kblk, i=i: pe.matmul(
                    ps[ob][:, hh * 65:(hh + 1) * 65], lhsT=P_[:, ti * 128:(ti + 1) * 128],
                    rhs=V[:, kblk, vh, :], start=(i == 0), stop=(i == nkb - 1)),
                    reads=[R("p", ui, u["chunk_of"][ti]), R("va" if mixer == "A" else "vb", kblk),
                           R("vones")] + p_alias(ui, ti, 1),
                    writes=[R("ps", ob)], signal=(i == nkb - 1))

        def stage2_norm(u):
            qb, m16, mixer, g = u["spec"]
            ui = u["par"]
            ob = u["ob"]
            O3 = ps[ob][:, 0:260].rearrange("p (h e) -> p h e", e=65)
            if mixer == "A":
                S.op("dve", lambda: dve.reciprocal(out=rcp[ui][:, :].rearrange("p (h e) -> p h e", e=1),
                                                   in_=O3[:, :, 64:65]),
                     reads=[R("ps", ob)], writes=[R("rcp", ui)])
            else:
                S.op("dve", lambda: dve.tensor_tensor(
                    out=den[ui][:, :].rearrange("p (h e) -> p h e", e=1), in0=O3[:, :, 64:65],
                    in1=esink[:, 4 * g:4 * g + 4].rearrange("p (h e) -> p h e", e=1), op=ALU.add),
                    reads=[R("ps", ob), R("esink")], writes=[R("den", ui)])
                S.op("dve", lambda: dve.reciprocal(out=rcp[ui][:, :], in_=den[ui][:, :]),
                     reads=[R("den", ui)], writes=[R("rcp", ui)])
            for hh in range(4):
                S.op("dve", lambda hh=hh: dve.tensor_scalar(
                    out=ytok[ui][:, hh * 64:(hh + 1) * 64], in0=ps[ob][:, hh * 65:hh * 65 + 64],
                    scalar1=rcp[ui][:, hh:hh + 1], scalar2=None, op0=ALU.mult),
                    reads=[R("ps", ob), R("rcp", ui)], writes=[R("ytok", ui)])

        def stage3(u):
            qb, m16, mixer, g = u["spec"]
            ui = u["par"]
            tb = 5 + ui
            Tb = ps[tb][:, 0:128].bitcast(BF16)
            for i in range(2):
                S.op("pe", lambda i=i: pe.transpose(
                    out=Tb[:, i * 128:(i + 1) * 128], in_=ytok[ui][:, i * 128:(i + 1) * 128],
                    identity=ident_bf[:, :]),
                    reads=[R("ytok", ui), R("ident_bf")], writes=[R("ps", tb)], signal=(i == 1))
            Q = QA if mixer == "A" else QB
            nm = "qa" if mixer == "A" else "qb"
            evac_copy(Q[:, 2 * g:2 * g + 2, qb * 128:(qb + 1) * 128],
                      Tb.rearrange("p (a b) -> p a b", a=2),
                      [R("ps", tb)], [R(nm, 2 * g, qb), R(nm, 2 * g + 1, qb)])

        us = [{"spec": sp_} for sp_ in units]
        n = len(us)
        for i in range(n + 3):
            if i < n:
                stage1(us[i])
                for pr in us[i]["pairs"]:
                    pr()
                stage1b(us[i])
            if 0 <= i - 2 < n:
                for hh in range(4):
                    stage2_head(us[i - 2], hh)
                stage2_norm(us[i - 2])
            if 0 <= i - 3 < n:
                stage3(us[i - 3])

    m3_ctr = [0]

    def m3():
        for G in range(4):
            slot = next_group()
            r = ring[slot]
            Wga = r[:, 0:2048].rearrange("p (k n) -> p k n", k=8)
            Wgb = r[:, 2048:4096].rearrange("p (k n) -> p k n", k=8)
            WA = r[:, 4096:5120].rearrange("p (k n) -> p k n", k=4)
            WB = r[:, 5120:6144].rearrange("p (k n) -> p k n", k=4)
            for mm in range(2):
                m = 2 * G + mm
                for s in range(2):
                    par = m3_ctr[0] % 2
                    m3_ctr[0] += 1
                    bga, bgb, bpa, bpb = [4 * par + i for i in range(4)]
                    for k in range(8):
                        for (W, b) in ((Wga, bga), (Wgb, bgb)):
                            S.op("pe", lambda W=W, b=b, k=k: pe.matmul(
                                ps[b][:, :], lhsT=W[:, k, mm * 128:(mm + 1) * 128],
                                rhs=xn[:, k, slab(s)], start=(k == 0), stop=(k == 7)),
                                reads=[R("ring", slot, 0), R("xn", k, s)], writes=[R("ps", b)],
                                signal=(k == 7))
                    for (W, b, Q, nm) in ((WA, bpa, QA, "qa"), (WB, bpb, QB, "qb")):
                        for c in range(4):
                            S.op("pe", lambda W=W, b=b, c=c, Q=Q: pe.matmul(
                                ps[b][:, :], lhsT=W[:, c, mm * 128:(mm + 1) * 128],
                                rhs=Q[:, c, slab(s)], start=(c == 0), stop=(c == 3)),
                                reads=[R("ring", slot, 1)] + [R(nm, c, 4 * s + j) for j in range(4)],
                                writes=[R("ps", b)], signal=(c == 3))
                    sa, sb2 = sg[2 * par], sg[2 * par + 1]
                    S.op("act", lambda: act.activation(out=sa[:, :], in_=ps[bga][:, :],
                                                       func=AF.Sigmoid),
                         reads=[R("ps", bga)], writes=[R("sg", 2 * par)])
                    S.op("act", lambda: act.activation(out=sb2[:, :], in_=ps[bgb][:, :],
                                                       func=AF.Sigmoid),
                         reads=[R("ps", bgb)], writes=[R("sg", 2 * par + 1)])
                    S.op("dve", lambda: dve.tensor_tensor(out=t1[par][:, :], in0=ps[bpa][:, :],
                                                          in1=sa[:, :], op=ALU.mult),
                         reads=[R("ps", bpa), R("sg", 2 * par)], writes=[R("t1", par)])
                    S.op("dve", lambda: dve.tensor_tensor(out=t2[par][:, :], in0=ps[bpb][:, :],
                                                          in1=sb2[:, :], op=ALU.mult),
                         reads=[R("ps", bpb), R("sg", 2 * par + 1)], writes=[R("t2", par)])
                    S.op("pool", lambda m=m, s=s, par=par: pool.tensor_tensor(
                        out=merged[:, m, slab(s)], in0=t1[par][:, :], in1=t2[par][:, :], op=ALU.add),
                        reads=[R("t1", par), R("t2", par)],
                        writes=[R("mg", m, s), R("stg", m // 2)])
        oc = 0
        for H in range(2):
            slot = next_group()
            Wo = ring[slot][:, 0:4096].rearrange("p (k n) -> p k n", k=8)
            for mp in range(4):
                mo = 4 * H + mp
                for s in range(2):
                    b = oc % 4
                    oc += 1
                    for m in range(8):
                        S.op("pe", lambda m=m, b=b, mp=mp, s=s: pe.matmul(
                            ps[b][:, :], lhsT=Wo[:, m, mp * 128:(mp + 1) * 128],
                            rhs=merged[:, m, slab(s)], start=(m == 0), stop=(m == 7)),
                            reads=[R("ring", slot, 0), R("mg", m, s)], writes=[R("ps", b)],
                            signal=(m == 7))
                    S.op("dve", lambda b=b, mo=mo, s=s: dve.tensor_tensor(
                        out=h[:, mo, slab(s)], in0=ps[b][:, :], in1=h[:, mo, slab(s)], op=ALU.add),
                        reads=[R("ps", b), R("h", mo, s)], writes=[R("h", mo, s)])

    def p_items(tok0):
        def dma_p(t):
            st = pstg[t % 2]
            S.dma("sp", [lambda: sp.dma_start(
                out=st[:, :], in_=p_d[tok0 + t * 128:tok0 + (t + 1) * 128, :])],
                "pstg%d" % (t % 2), writes=[R("pstg", t % 2)])

        def xp(t):
            st = pstg[t % 2]
            b = 4 + t % 2
            for j in range(2):
                S.op("pe", lambda j=j: pe.transpose(
                    out=ps[b][:, j * 128:(j + 1) * 128], in_=st[:, j * 128:(j + 1) * 128],
                    identity=ident[:, :]),
                    reads=[R("pstg", t % 2), R("ident")], writes=[R("ps", b)], signal=(j == 1))
            evac_copy(pT[:, 0:2, t * 128:(t + 1) * 128],
                      ps[b][:, 0:256].rearrange("p (a b) -> p a b", a=2),
                      [R("ps", b)], [R("pT", t // 4)])

        def mk(i):
            def it():
                if i >= 1:
                    xp(i - 1)
                if i < 8:
                    dma_p(i)
            return it
        return [mk(i) for i in range(9)]

    def ple(tok0):
        oc = 0
        for H in range(2):
            slot = next_group()
            r = ring[slot]
            Wpg = r[:, 0:4096].rearrange("p (k n) -> p k n", k=8)
            Wpe = r[:, 4096:5120].rearrange("p (k n) -> p k n", k=2)
            for mp in range(4):
                mo = 4 * H + mp
                for s in range(2):
                    par = oc % 2
                    oc += 1
                    bg, bp = par, 2 + par
                    for k in range(8):
                        S.op("pe", lambda k=k, bg=bg, mp=mp, s=s: pe.matmul(
                            ps[bg][:, :], lhsT=Wpg[:, k, mp * 128:(mp + 1) * 128],
                            rhs=xn[:, k, slab(s)], start=(k == 0), stop=(k == 7)),
                            reads=[R("ring", slot, 0), R("xn", k, s)], writes=[R("ps", bg)],
                            signal=(k == 7))
                    for k in range(2):
                        S.op("pe", lambda k=k, bp=bp, mp=mp, s=s: pe.matmul(
                            ps[bp][:, :], lhsT=Wpe[:, k, mp * 128:(mp + 1) * 128],
                            rhs=pT[:, k, slab(s)], start=(k == 0), stop=(k == 1)),
                            reads=[R("ring", slot, 1), R("pT", s)], writes=[R("ps", bp)],
                            signal=(k == 1))
                    S.op("act", lambda bg=bg, par=par: act.activation(
                        out=t2[par][:, :], in_=ps[bg][:, :], func=AF.Sigmoid),
                        reads=[R("ps", bg)], writes=[R("t2", par)])
                    S.op("dve", lambda bp=bp, par=par: dve.tensor_tensor(
                        out=t1[par][:, :], in0=ps[bp][:, :], in1=t2[par][:, :], op=ALU.mult),
                        reads=[R("ps", bp), R("t2", par)], writes=[R("t1", par)])
                    S.op("pool", lambda mo=mo, s=s, par=par: pool.tensor_tensor(
                        out=h[:, mo, slab(s)], in0=h[:, mo, slab(s)], in1=t1[par][:, :], op=ALU.add),
                        reads=[R("t1", par), R("h", mo, s)], writes=[R("h", mo, s)])

    def store_out(tok0):
        for t in range(8):
            oi = t % NOST
            st = ostg[oi]
            for hf in range(2):
                b = (2 * t + hf) % 8
                for j in range(4):
                    k = hf * 4 + j
                    S.op("pe", lambda b=b, j=j, k=k, t=t: pe.transpose(
                        out=ps[b][:, j * 128:(j + 1) * 128], in_=h[:, k, t * 128:(t + 1) * 128],
                        identity=ident[:, :]),
                        reads=[R("h", k, t // 4), R("ident")], writes=[R("ps", b)],
                        signal=(j == 3))
                evac_copy(st[:, hf * 512:(hf + 1) * 512], ps[b][:, :],
                          [R("ps", b)], ostg_res(oi))
            S.dma("sp", [lambda t=t, st=st: sp.dma_start(
                out=out_d[tok0 + t * 128:tok0 + (t + 1) * 128, :], in_=st)],
                "ostg%d" % oi, reads=ostg_res(oi))

    def dump_dbg(what):
        allr = list(S.res.values())
        if what in ("m1", "m2"):
            return dump_h()
        if what in ("m1q", "m2y"):
            fns = [lambda: pool.dma_start(out=dbg_d[:, 0:4, :], in_=QA[:, :, :]),
                   lambda: pool.dma_start(out=dbg_d[:, 4:8, :], in_=QB[:, :, :])]
        elif what == "m1k":
            fns = [lambda: pool.dma_start(out=dbg_d[:, 0:4, :], in_=KA[:, :, 0:PASS]),
                   lambda: pool.dma_start(out=dbg_d[:, 4:6, :], in_=KB[:, :, 0:PASS])]
        elif what == "m1v":
            fns = [lambda: pool.dma_start(
                out=dbg_d[:, 0:6, :].rearrange("p a b -> p (a b)")[:, 0:5200].rearrange(
                    "p (a b) -> p a b", a=8),
                in_=V[:, 0:8].rearrange("p a b c -> p a (b c)"))]
        S.dma("pool", fns, "dbg", reads=allr)
        S.wait_sem("pool", "dbg")

    def dump_h():
        S.dma("sp", [lambda: sp.dma_start(out=dbg_d, in_=h[:, :, :])], "dbg",
              reads=[R("h", k, s) for k in range(8) for s in range(2)])
        S.wait_sem("sp", "dbg")

    def tok_of(pi):
        return (pi // 2) * SEQ + (pi % 2) * PASS

    for t in range(8):
        issue_x(tok_of(0), t)
    for ps_i in range(npass):
        seq, half = ps_i // 2, ps_i % 2
        tok0 = tok_of(ps_i)
        load_x(tok0)
        if stop == "load":
            dump_h(); break
        norm(0)
        ffn(0, extras=build_E_items() if ps_i == 0 else ())
        if stop == "ffn1":
            dump_h(); break
        norm(1)
        m1(half)
        if stop in ("m1", "m1q", "m1k", "m1v"):
            dump_dbg(stop); break
        m2(half)
        if stop in ("m2", "m2y"):
            dump_dbg(stop); break
        m3()
        if ps_i + 1 < npass:
            for t in range(4):
                issue_x(tok_of(ps_i + 1), t)
        if stop == "mix":
            dump_h(); break
        norm(2)
        ffn(1, extras=p_items(tok0))
        if stop == "ffn2":
            dump_h(); break
        norm(3)
        ple(tok0)
        if stop == "ple":
            dump_h(); break
        if ps_i + 1 < npass:
            for t in range(4, 8):
                issue_x(tok_of(ps_i + 1), t)
        store_out(tok0)
    for i in range(NOST):
        S.wait_sem("sp", "ostg%d" % i)
    return nc


def _host_consts():
    kl = np.arange(128)[:, None, None]
    kb = np.arange(5)[None, :, None]
    ql = np.arange(128)[None, None, :]
    rel = ql + 512 - 128 * kb - kl
    idxA = np.clip(rel, -128, 128) + 128
    jc = (128 * kb + kl) // 64
    qc = ql // 64
    validA = (jc >= qc) & (jc <= qc + 8)
    maskA = np.where(validA, 0.0, NEG).astype(np.float32)
    maskA = np.broadcast_to(maskA[:, None], (128, 8, 5, 128)).copy()
    kb2 = np.arange(2)[None, :, None]
    relB = (128 + ql) - (128 * kb2 + kl)
    jcB = (128 * kb2 + kl) // 64
    validB = (jcB >= qc) & (jcB <= qc + 2)
    slopes = np.array([2.0 ** (-8.0 * (hh + 1) / 8) for hh in range(8)], dtype=np.float32)
    biasB = -slopes[None, :, None, None] * np.abs(relB).astype(np.float32)[:, None]
    biasB = np.where(validB[:, None], biasB, NEG).astype(np.float32)
    return idxA, maskA, np.ascontiguousarray(biasB)


def make_in_maps(inputs):
    f = lambda a: np.ascontiguousarray(np.asarray(a, dtype=np.float32))
    x = f(inputs["x"])
    p = f(inputs["p"])[0]
    idxA, maskA, biasB = _host_consts()
    arb = f(inputs["a_rel_bias"])[0]
    biasA = np.ascontiguousarray(np.transpose(arb[:, idxA], (1, 0, 2, 3)))
    gains = np.stack([f(inputs[n])[0].reshape(8, 128).T for n in
                      ("ffn1_norm", "mix_norm", "ffn2_norm", "ple_norm")], axis=1)
    gains = np.ascontiguousarray(gains.reshape(128, 32))
    gqk = np.stack([np.tile(f(inputs[n])[0], 2) for n in
                    ("a_q_norm", "a_k_norm", "b_q_norm", "b_k_norm")], axis=1)
    gqk = np.ascontiguousarray(gqk)
    sinks = np.ascontiguousarray(np.broadcast_to(f(inputs["b_sinks"])[0][None, :], (128, 8)))
    shared = {
        "ffn1_w_gu": f(inputs["ffn1_w_gu"])[0], "ffn2_w_gu": f(inputs["ffn2_w_gu"])[0],
        "ffn1_w_down": f(inputs["ffn1_w_down"])[0], "ffn2_w_down": f(inputs["ffn2_w_down"])[0],
        "w_in": f(inputs["w_in"])[0], "w_gate": f(inputs["w_gate"])[0],
        "w_proj_a": f(inputs["w_proj_a"])[0], "w_proj_b": f(inputs["w_proj_b"])[0],
        "w_out": f(inputs["w_out"])[0], "w_ple_gate": f(inputs["w_ple_gate"])[0],
        "w_ple_proj": f(inputs["w_ple_proj"])[0],
        "gains": gains, "gqk": gqk, "sinks": sinks, "ident": np.eye(128, dtype=np.float32),
        "biasA": biasA, "maskA": maskA, "biasB": biasB,
    }
    in_maps = []
    for c in range(N_CORES):
        m = dict(shared)
        m["x"] = np.ascontiguousarray(x[2 * c:2 * c + 2].reshape(TOK_CORE, D))
        m["p"] = np.ascontiguousarray(p[2 * c:2 * c + 2].reshape(TOK_CORE, 256))
        in_maps.append(m)
    return in_maps


def kernel(**inputs):
    nc = build_nc()
    in_maps = make_in_maps(inputs)
    res = run_bass_kernel_spmd(nc, in_maps, core_ids=list(range(N_CORES)))
    out = np.stack([np.asarray(r["out"]).reshape(2, SEQ, D) for r in res.results], axis=0)
    return out.reshape(16, SEQ, D).astype(np.float32)
```

```python
import numpy as np
import concourse.bass as bass
import concourse.mybir as mybir
from concourse.bass_utils import run_bass_kernel_spmd

F32 = mybir.dt.float32
BF16 = mybir.dt.bfloat16
AF = mybir.ActivationFunctionType
ALU = mybir.AluOpType

N_CORES = 8
D = 1024
KC = 8
DFF = 2816
SEQ = 2048
PASS = 1024
SLAB = 512
TOK_CORE = 4096
EPS = 1e-6
NEG = -30000.0
STRICT = True


class Res:
    __slots__ = ("w", "rs")

    def __init__(self):
        self.w = None
        self.rs = {}


class Sched:
    def __init__(self, nc):
        self.nc = nc
        self.eng = {"pe": nc.tensor, "act": nc.scalar, "dve": nc.vector,
                    "pool": nc.gpsimd, "sp": nc.sync}
        self.sems = {}
        self.cnt = {}
        self.seen = {e: {} for e in self.eng}
        self.res = {}
        for e in ("pe", "act", "dve", "pool"):
            self.newsem(e)

    def newsem(self, key):
        if key not in self.sems:
            self.sems[key] = self.nc.alloc_semaphore("s_" + key)
            self.cnt[key] = 0

    def R(self, *key):
        r = self.res.get(key)
        if r is None:
            r = self.res[key] = Res()
        return r

    def _waits(self, eng, reads, writes):
        waits = {}

        def need(st):
            if st is None:
                return
            k, v = st
            if k == eng and (eng == "pe" or not STRICT):
                return
            if self.seen[eng].get(k, 0) >= v:
                return
            if waits.get(k, 0) < v:
                waits[k] = v

        for r in reads:
            need(r.w)
        for r in writes:
            need(r.w)
            for k, v in r.rs.items():
                need((k, v))
        for k, v in waits.items():
            self.eng[eng].wait_ge(self.sems[k], v)
            self.seen[eng][k] = v

    def _stamp(self, st, reads, writes):
        k, v = st
        for r in reads:
            if r.rs.get(k, 0) < v:
                r.rs[k] = v
        for r in writes:
            r.w = st
            r.rs = {}

    def op(self, eng, fn, reads=(), writes=(), signal=True):
        self._waits(eng, reads, writes)
        inst = fn()
        if eng == "pe" and not signal:
            st = ("pe", self.cnt["pe"] + 1)
        else:
            self.cnt[eng] += 1
            inst.then_inc(self.sems[eng], 1)
            st = (eng, self.cnt[eng])
        self._stamp(st, reads, writes)

    def dma(self, eng, fns, dsem, reads=(), writes=()):
        self.newsem(dsem)
        self._waits(eng, reads, writes)
        for fn in fns:
            inst = fn()
            self.cnt[dsem] += 16
            inst.then_inc(self.sems[dsem], 16)
        self._stamp((dsem, self.cnt[dsem]), reads, writes)

    def wait_sem(self, eng, key):
        if key in self.sems and self.cnt[key] > 0:
            self.eng[eng].wait_ge(self.sems[key], self.cnt[key])


def build_nc(stop=None, npass=4):
    nc = bass.Bass("TRN2", target_bir_lowering=False)
    S = Sched(nc)
    R = S.R

    def din(name, shape):
        return nc.dram_tensor(name, list(shape), F32, kind="ExternalInput").ap()

    x_d = din("x", [TOK_CORE, D])
    p_d = din("p", [TOK_CORE, 256])
    wgu_d = [din("ffn1_w_gu", [D, 2 * DFF]), din("ffn2_w_gu", [D, 2 * DFF])]
    wdn_d = [din("ffn1_w_down", [DFF, D]), din("ffn2_w_down", [DFF, D])]
    win_d = din("w_in", [D, 2304])
    wgate_d = din("w_gate", [D, 2 * D])
    wpa_d = din("w_proj_a", [512, D])
    wpb_d = din("w_proj_b", [512, D])
    wout_d = din("w_out", [D, D])
    wpg_d = din("w_ple_gate", [D, D])
    wpe_d = din("w_ple_proj", [256, D])
    gains_d = din("gains", [128, 32])
    gqk_d = din("gqk", [128, 4])
    sinks_d = din("sinks", [128, 8])
    ident_d = din("ident", [128, 128])
    biasA_d = din("biasA", [128, 8, 5, 128])
    maskA_d = din("maskA", [128, 8, 5, 128])
    biasB_d = din("biasB", [128, 8, 2, 128])
    out_d = nc.dram_tensor("out", [TOK_CORE, D], F32, kind="ExternalOutput").ap()
    dbg_d = None
    if stop is not None:
        dbg_d = nc.dram_tensor("dbg", [128, 8, PASS], F32, kind="ExternalOutput").ap()

    def sb(name, shape, dt):
        return nc.alloc_sbuf_tensor("sb_" + name, list(shape), dt)

    h = sb("h", [128, 8, PASS], F32)
    xn = sb("xn", [128, 8, PASS], BF16)
    QA = sb("QA", [128, 4, PASS], BF16)
    QB = sb("QB", [128, 4, PASS], BF16)
    KA = sb("KA", [128, 4, SEQ], BF16)
    KB = sb("KB", [128, 2, SEQ], BF16)
    V = sb("V", [128, 16, 10, 65], BF16)
    EA = sb("EA", [128, 8, 5, 128], BF16)
    EB = sb("EB", [128, 8, 2, 128], BF16)
    ring = [sb("ring%d" % i, [128, 6144], BF16) for i in range(2)]
    actb = [sb("act%d" % i, [128, 2, 512], BF16) for i in range(2)]
    Pbuf = [sb("P%d" % i, [128, 20 * 128], BF16) for i in range(2)]
    merged = sb("merged", [128, 8, PASS], BF16)
    stg = [merged[:, 2 * i:2 * i + 2, :].bitcast(F32) for i in range(4)]
    stg = [s.rearrange("p a b -> p (a b)") for s in stg]
    ostg = [Pbuf[i][:, 0:2048].bitcast(F32) for i in range(2)]
    for Q_ in (QA, QB):
        for i in range(2):
            ostg.append(Q_[:, 2 * i:2 * i + 2, :].bitcast(F32).rearrange("p a b -> p (a b)"))
    ptmp = [Pbuf[i][:, 0:2560].bitcast(F32) for i in range(2)]

    def stg_res(i):
        return [R("stg", i)] + [R("mg", 2 * i + a, s_) for a in range(2) for s_ in range(2)]

    def pbuf_res(i):
        return [R("p", i, c) for c in range(6)]

    def ostg_res(i):
        if i < 2:
            return pbuf_res(i)
        if i >= 6:
            nm = "t1" if i == 6 else "t2"
            return [R(nm, 0), R(nm, 1)]
        nm = "qa" if i < 4 else "qb"
        c0 = 2 * (i % 2)
        return [R(nm, c0 + a, qb_) for a in range(2) for qb_ in range(8)]
    sq = [sb("sq%d" % i, [128, 512], BF16) for i in range(4)]
    lnv = sb("lnv", [128, 512], F32)
    rstd = [sb("rstd%d" % i, [128, 512], F32) for i in range(2)]
    sg = [sb("sg%d" % i, [128, 512], BF16) for i in range(4)]
    t1all = sb("t1all", [128, 2, 512], F32)
    t2all = sb("t2all", [128, 2, 512], F32)
    t1 = [t1all[:, i, :] for i in range(2)]
    t2 = [t2all[:, i, :] for i in range(2)]
    ostg.append(t1all[:, :, :].rearrange("p a b -> p (a b)"))
    ostg.append(t2all[:, :, :].rearrange("p a b -> p (a b)"))
    NOST = len(ostg)
    pT = sb("pT", [128, 2, PASS], BF16)
    pstg = [sb("pstg%d" % i, [128, 256], F32) for i in range(2)]
    ytok = [sb("ytok%d" % i, [128, 256], BF16) for i in range(2)]
    ident = sb("ident", [128, 128], F32)
    ident_bf = sb("ident_bf", [128, 128], BF16)
    ones_bf = sb("ones_bf", [128, 128], BF16)
    bones = sb("bones", [128, 128], BF16)
    gains = sb("gains", [128, 32], F32)
    gqk = sb("gqk", [128, 4], F32)
    esink = sb("esink", [128, 8], F32)
    den = [sb("den%d" % i, [128, 4], F32) for i in range(2)]
    rcp = [sb("rcp%d" % i, [128, 4], F32) for i in range(2)]
    warm = sb("warm", [128, 2], F32)
    ps = [nc.alloc_psum_tensor("ps%d" % i, [128, 512], F32) for i in range(8)]

    pe, act, dve, pool, sp = nc.tensor, nc.scalar, nc.vector, nc.gpsimd, nc.sync

    S.dma("sp", [
        lambda: sp.dma_start(out=ident[:, :], in_=ident_d),
        lambda: sp.dma_start(out=gains[:, :], in_=gains_d),
        lambda: sp.dma_start(out=gqk[:, :], in_=gqk_d),
        lambda: sp.dma_start(out=esink[:, :], in_=sinks_d),
    ], "setup", writes=[R("ident"), R("gains"), R("gqk"), R("esink")])
    S.op("dve", lambda: dve.tensor_copy(out=ident_bf[:, :], in_=ident[:, :]),
         reads=[R("ident")], writes=[R("ident_bf")])
    S.op("dve", lambda: dve.memset(ones_bf[:, :], 1.0), writes=[R("ones")])
    S.op("dve", lambda: dve.memset(warm[:, :], 1.0), writes=[R("warm")])
    S.op("dve", lambda: dve.memset(bones[:, :], 0.0), writes=[R("bones")])
    S.op("dve", lambda: dve.memset(bones[0:64, 0:64], 1.0), writes=[R("bones")])
    S.op("dve", lambda: dve.memset(bones[64:128, 64:128], 1.0), writes=[R("bones")])
    S.op("dve", lambda: dve.memset(V[:, :, :, 64:65], 1.0), writes=[R("vones")])
    S.op("dve", lambda: dve.tensor_scalar(out=gqk[:, 0:1], in0=gqk[:, 0:1], scalar1=0.125,
                                          scalar2=None, op0=ALU.mult),
         reads=[R("gqk")], writes=[R("gqk")])
    S.op("dve", lambda: dve.tensor_scalar(out=gqk[:, 2:3], in0=gqk[:, 2:3], scalar1=0.125,
                                          scalar2=None, op0=ALU.mult),
         reads=[R("gqk")], writes=[R("gqk")])
    S.op("act", lambda: act.activation(out=esink[:, :], in_=esink[:, :], func=AF.Exp),
         reads=[R("esink")], writes=[R("esink")])
    def build_E_items():
        items = []
        for hd in range(8):
            def it(hd=hd):
                i = hd % 2
                a = ptmp[i][:, 0:640]
                b = ptmp[i][:, 640:1280]
                S.dma("sp", [
                    lambda: sp.dma_start(out=a, in_=biasA_d[:, hd].rearrange("p a b -> p (a b)")),
                    lambda: sp.dma_start(out=b, in_=maskA_d[:, hd].rearrange("p a b -> p (a b)")),
                ], "setupE%d" % i, writes=pbuf_res(i))
                S.op("dve", lambda: dve.tensor_tensor(out=a, in0=a, in1=b, op=ALU.add),
                     reads=pbuf_res(i), writes=pbuf_res(i))
                S.op("act", lambda: act.activation(
                    out=EA[:, hd].rearrange("p a b -> p (a b)"), in_=a, func=AF.Exp),
                    reads=pbuf_res(i), writes=[R("EA")])
            items.append(it)
        for hp in range(4):
            def it(hp=hp):
                i = hp % 2
                a = ptmp[i][:, 0:512]
                S.dma("sp", [
                    lambda: sp.dma_start(
                        out=a, in_=biasB_d[:, 2 * hp:2 * hp + 2].rearrange("p h a b -> p (h a b)")),
                ], "setupE%d" % i, writes=pbuf_res(i))
                S.op("act", lambda: act.activation(
                    out=EB[:, 2 * hp:2 * hp + 2].rearrange("p h a b -> p (h a b)"), in_=a, func=AF.Exp),
                    reads=pbuf_res(i), writes=[R("EB")])
            items.append(it)
        return items

    def wv(dram, p=128):
        return dram.rearrange("(kc p) n -> p kc n", p=p)

    def ffn_group(w, g):
        def pieces(slot):
            r = ring[slot]
            return [
                (r[:, 0:2048].rearrange("p (k n) -> p k n", k=8),
                 wv(wgu_d[w])[:, :, 256 * g:256 * g + 256], 0),
                (r[:, 2048:4096].rearrange("p (k n) -> p k n", k=8),
                 wv(wgu_d[w])[:, :, DFF + 256 * g:DFF + 256 * g + 256], 0),
                (r[:, 4096:6144].rearrange("p (k n) -> p k n", k=2),
                 wv(wdn_d[w])[:, 2 * g:2 * g + 2, :], 1),
            ]
        return pieces

    def cols_group(dram, c0, n, kc=8):
        def pieces(slot):
            r = ring[slot]
            return [(r[:, 0:kc * n].rearrange("p (k n) -> p k n", k=kc),
                     wv(dram)[:, :, c0:c0 + n], 0)]
        return pieces

    def kbvb_group():
        def pieces(slot):
            r = ring[slot]
            kd = r[:, 0:2048].rearrange("p (k n) -> p k n", k=8)
            out = []
            for kvh in range(2):
                for dup in range(2):
                    out.append((kd[:, :, kvh * 128 + dup * 64:kvh * 128 + dup * 64 + 64],
                                wv(win_d)[:, :, 2048 + kvh * 64:2048 + kvh * 64 + 64], 0))
            out.append((r[:, 2048:3072].rearrange("p (k n) -> p k n", k=8),
                        wv(win_d)[:, :, 2176:2304], 0))
            return out
        return pieces

    def m3_group(G):
        def pieces(slot):
            r = ring[slot]
            return [
                (r[:, 0:2048].rearrange("p (k n) -> p k n", k=8),
                 wv(wgate_d)[:, :, 256 * G:256 * G + 256], 0),
                (r[:, 2048:4096].rearrange("p (k n) -> p k n", k=8),
                 wv(wgate_d)[:, :, D + 256 * G:D + 256 * G + 256], 0),
                (r[:, 4096:5120].rearrange("p (k n) -> p k n", k=4),
                 wv(wpa_d)[:, :, 256 * G:256 * G + 256], 1),
                (r[:, 5120:6144].rearrange("p (k n) -> p k n", k=4),
                 wv(wpb_d)[:, :, 256 * G:256 * G + 256], 1),
            ]
        return pieces

    def ple_group(H):
        def pieces(slot):
            r = ring[slot]
            return [
                (r[:, 0:4096].rearrange("p (k n) -> p k n", k=8),
                 wv(wpg_d)[:, :, 512 * H:512 * H + 512], 0),
                (r[:, 4096:5120].rearrange("p (k n) -> p k n", k=2),
                 wv(wpe_d)[:, :, 512 * H:512 * H + 512], 1),
            ]
        return pieces

    pass_groups = ([ffn_group(0, g) for g in range(11)]
                   + [cols_group(win_d, 0, 512), cols_group(win_d, 512, 512),
                      cols_group(win_d, 1536, 512), kbvb_group(),
                      cols_group(win_d, 1024, 512)]
                   + [m3_group(G) for G in range(4)]
                   + [cols_group(wout_d, 0, 512), cols_group(wout_d, 512, 512)]
                   + [ffn_group(1, g) for g in range(11)]
                   + [ple_group(0), ple_group(1)])
    NG = len(pass_groups)
    all_groups = pass_groups * npass
    gstate = {"issued": [0, 0], "cur": -1}

    def issue_part(part):
        gi = gstate["issued"][part]
        if gi >= len(all_groups):
            return
        slot = gi % 2
        pcs = [(o, i) for (o, i, pt) in all_groups[gi](slot) if pt == part]
        if pcs:
            S.dma("pool", [(lambda o=o, i=i: pool.dma_start(out=o, in_=i)) for o, i in pcs],
                  "ring%d_%d" % (slot, part), writes=[R("ring", slot, part)])
        gstate["issued"][part] += 1

    def next_group(pf=True):
        gstate["cur"] += 1
        gi = gstate["cur"]
        for part in range(2):
            while gstate["issued"][part] <= gi:
                issue_part(part)
        if pf:
            prefetch(0)
            prefetch(1)
        return gi % 2

    def prefetch(part):
        if gstate["issued"][part] <= gstate["cur"] + 1:
            issue_part(part)

    def slab(s):
        return slice(s * SLAB, (s + 1) * SLAB)

    cp_rr = [0]

    def evac_copy(out, in_, reads, writes):
        cp_rr[0] ^= 1
        if cp_rr[0]:
            S.op("act", lambda: act.activation(out=out, in_=in_, func=AF.Copy),
                 reads=reads, writes=writes)
        else:
            S.op("dve", lambda: dve.tensor_copy(out=out, in_=in_), reads=reads, writes=writes)

    xn_f32 = xn[:, :, :].bitcast(F32).rearrange("p a b -> p (a b)")

    def xstage(t):
        if t < 4:
            return stg[t], stg_res(t), "stg%d" % t
        j = t - 4
        return (xn_f32[:, j * 1024:(j + 1) * 1024],
                [R("xn", 2 * j + a, s_) for a in range(2) for s_ in range(2)], "stgB%d" % j)

    def issue_x(tok0, t):
        ap, res, sem = xstage(t)
        S.dma("sp", [lambda: sp.dma_start(
            out=ap, in_=x_d[tok0 + t * 128:tok0 + (t + 1) * 128, :])],
            sem, writes=res)

    def load_x(tok0):
        for t in range(8):
            st, res, _ = xstage(t)
            for hf in range(2):
                b = (2 * t + hf) % 8
                for j in range(4):
                    k = hf * 4 + j
                    S.op("pe", lambda b=b, j=j, k=k, st=st: pe.transpose(
                        out=ps[b][:, j * 128:(j + 1) * 128], in_=st[:, k * 128:(k + 1) * 128],
                        identity=ident[:, :]),
                        reads=res + [R("ident")], writes=[R("ps", b)], signal=(j == 3))
                evac_copy(h[:, hf * 4:hf * 4 + 4, t * 128:(t + 1) * 128],
                          ps[b][:, :].rearrange("p (a b) -> p a b", a=4),
                          [R("ps", b)], [R("h", hf * 4 + j, t // 4) for j in range(4)])

    def norm(gidx):
        S.op("act", lambda: act.activation(out=warm[:, 1:2], in_=warm[:, 0:1], func=AF.Ln),
             reads=[R("warm")], writes=[R("warm_o")])
        for s in range(2):
            nb = 6 + s
            for k in range(8):
                q = sq[k % 4]
                S.op("act", lambda k=k, q=q: act.activation(out=q[:, :], in_=h[:, k, slab(s)],
                                                            func=AF.Square),
                     reads=[R("h", k, s)], writes=[R("sq", k % 4)])
                S.op("pe", lambda k=k, q=q: pe.matmul(ps[nb][:, :], lhsT=ones_bf[:, :], rhs=q[:, :],
                                                      start=(k == 0), stop=(k == 7)),
                     reads=[R("sq", k % 4), R("ones")], writes=[R("ps", nb)], signal=True)
            S.op("act", lambda: act.activation(out=lnv[:, :], in_=ps[nb][:, :], func=AF.Ln,
                                               bias=EPS, scale=1.0 / D),
                 reads=[R("ps", nb)], writes=[R("lnv")])
            S.op("act", lambda: act.activation(out=ps[nb][:, :], in_=lnv[:, :], func=AF.Exp,
                                               scale=-0.5),
                 reads=[R("lnv")], writes=[R("ps", nb)])
            for k in range(8):
                S.op("dve", lambda k=k: dve.scalar_tensor_tensor(
                    out=xn[:, k, slab(s)], in0=h[:, k, slab(s)],
                    scalar=gains[:, gidx * 8 + k:gidx * 8 + k + 1], in1=ps[nb][:, :],
                    op0=ALU.mult, op1=ALU.mult),
                    reads=[R("h", k, s), R("ps", nb), R("gains")], writes=[R("xn", k, s)])

    step_ctr = [0]

    def ffn(w, extras=()):
        extras = list(extras)
        prev = None
        for g in range(11):
            slot = next_group(pf=False)
            prefetch(0)
            r = ring[slot]
            Wg = r[:, 0:2048].rearrange("p (k n) -> p k n", k=8)
            Wu = r[:, 2048:4096].rearrange("p (k n) -> p k n", k=8)
            Wd = r[:, 4096:6144].rearrange("p (k n) -> p k n", k=2)
            for s in range(2):
                ab = step_ctr[0] % 2
                step_ctr[0] += 1
                mmlist = []
                for jj in range(2):
                    for k in range(8):
                        mmlist.append((Wg, jj, jj, k))
                        mmlist.append((Wu, jj, 2 + jj, k))
                for qi in range(4):
                    for (W, jj, b, k) in mmlist[8 * qi:8 * qi + 8]:
                        S.op("pe", lambda W=W, b=b, k=k, jj=jj: pe.matmul(
                            ps[b][:, :], lhsT=W[:, k, jj * 128:(jj + 1) * 128],
                            rhs=xn[:, k, slab(s)], start=(k == 0), stop=(k == 7)),
                            reads=[R("ring", slot, 0), R("xn", k, s)], writes=[R("ps", b)],
                            signal=(k == 7))
                    jj = qi // 2
                    if qi in (1, 3):
                        sgi = 2 * ab + jj
                        S.op("act", lambda jj=jj, sgi=sgi: act.activation(
                            out=sg[sgi][:, :], in_=ps[jj][:, :], func=AF.Silu),
                            reads=[R("ps", jj)], writes=[R("sg", sgi)])
                        S.op("dve", lambda jj=jj, sgi=sgi, ab=ab: dve.tensor_tensor(
                            out=actb[ab][:, jj, :], in0=ps[2 + jj][:, :], in1=sg[sgi][:, :],
                            op=ALU.mult),
                            reads=[R("ps", 2 + jj), R("sg", sgi)], writes=[R("act", ab, jj)])
                    if prev is not None:
                        prev[2 * qi]()
                        prev[2 * qi + 1]()
                if s == 0:
                    prefetch(1)

                def mk_pair(m, s=s, ab=ab, Wd=Wd, slot=slot):
                    def pair():
                        b = 4 + m % 4
                        for jj in range(2):
                            S.op("pe", lambda jj=jj: pe.matmul(
                                ps[b][:, :], lhsT=Wd[:, jj, m * 128:(m + 1) * 128],
                                rhs=actb[ab][:, jj, :], start=(jj == 0), stop=(jj == 1)),
                                reads=[R("ring", slot, 1), R("act", ab, jj)], writes=[R("ps", b)],
                                signal=(jj == 1))
                        S.op("dve", lambda: dve.scalar_tensor_tensor(
                            out=h[:, m, slab(s)], in0=ps[b][:, :], scalar=0.5,
                            in1=h[:, m, slab(s)], op0=ALU.mult, op1=ALU.add),
                            reads=[R("ps", b), R("h", m, s)], writes=[R("h", m, s)])
                    return pair
                prev = [mk_pair(m) for m in range(8)]
                if s == 1 and extras:
                    extras.pop(0)()
        for pr in prev:
            pr()
        for ex in extras:
            ex()

    qk_ctr = [0]

    def qk_chunks(items):
        work = [(it, s) for it in items for s in range(2)]
        pend = None
        for (it, s) in work:
            lhsT_fn, gcol, dest_fn, dres_fn, slot = it
            i = qk_ctr[0]
            qk_ctr[0] += 1
            b = i % 4
            for k in range(8):
                S.op("pe", lambda k=k, b=b, lhsT_fn=lhsT_fn, s=s: pe.matmul(
                    ps[b][:, :], lhsT=lhsT_fn(k), rhs=xn[:, k, slab(s)],
                    start=(k == 0), stop=(k == 7)),
                    reads=[R("ring", slot, 0), R("xn", k, s)], writes=[R("ps", b)], signal=(k == 7))
            S.op("act", lambda b=b, i=i: act.activation(out=sq[i % 4][:, :], in_=ps[b][:, :],
                                                        func=AF.Square),
                 reads=[R("ps", b)], writes=[R("sq", i % 4)])
            if pend is not None:
                pend()

            def rest(i=i, b=b, gcol=gcol, dest_fn=dest_fn, dres_fn=dres_fn, s=s):
                q = sq[i % 4]
                sb_ = 4 + i % 2
                rs = rstd[i % 2]
                S.op("pe", lambda: pe.matmul(ps[sb_][:, :], lhsT=bones[:, :], rhs=q[:, :],
                                             start=True, stop=True),
                     reads=[R("sq", i % 4), R("bones")], writes=[R("ps", sb_)], signal=True)
                S.op("act", lambda: act.activation(out=lnv[:, :], in_=ps[sb_][:, :], func=AF.Ln,
                                                   bias=EPS, scale=1.0 / 64),
                     reads=[R("ps", sb_)], writes=[R("lnv")])
                S.op("act", lambda: act.activation(out=rs[:, :], in_=lnv[:, :], func=AF.Exp,
                                                   scale=-0.5),
                     reads=[R("lnv")], writes=[R("rstd", i % 2)])
                S.op("dve", lambda: dve.scalar_tensor_tensor(
                    out=dest_fn(s), in0=ps[b][:, :], scalar=gqk[:, gcol:gcol + 1], in1=rs[:, :],
                    op0=ALU.mult, op1=ALU.mult),
                    reads=[R("ps", b), R("rstd", i % 2), R("gqk")], writes=dres_fn(s))
            pend = rest
        pend()

    def m1(half):
        kb0 = half * 8
        t0 = half * PASS
        slot = next_group()
        W = ring[slot][:, 0:4096].rearrange("p (k n) -> p k n", k=8)
        qk_chunks([((lambda k, c=c, W=W: W[:, k, c * 128:(c + 1) * 128]), 0,
                    (lambda s, c=c: QA[:, c, slab(s)]),
                    (lambda s, c=c: [R("qa", c, 4 * s + j) for j in range(4)]), slot)
                   for c in range(4)])
        slot = next_group()
        W = ring[slot][:, 0:4096].rearrange("p (k n) -> p k n", k=8)
        qk_chunks([((lambda k, c=c, W=W: W[:, k, c * 128:(c + 1) * 128]), 1,
                    (lambda s, c=c: KA[:, c, t0 + s * SLAB:t0 + (s + 1) * SLAB]),
                    (lambda s, c=c: [R("ka", c, kb0 + 4 * s + j) for j in range(4)]), slot)
                   for c in range(4)])
        slot = next_group()
        W = ring[slot][:, 0:4096].rearrange("p (k n) -> p k n", k=8)
        qk_chunks([((lambda k, c=c, W=W: W[:, k, c * 128:(c + 1) * 128]), 2,
                    (lambda s, c=c: QB[:, c, slab(s)]),
                    (lambda s, c=c: [R("qb", c, 4 * s + j) for j in range(4)]), slot)
                   for c in range(4)])
        slot = next_group()
        W = ring[slot][:, 0:2048].rearrange("p (k n) -> p k n", k=8)
        Wvb = ring[slot][:, 2048:3072].rearrange("p (k n) -> p k n", k=8)
        qk_chunks([((lambda k, c=c, W=W: W[:, k, c * 128:(c + 1) * 128]), 3,
                    (lambda s, c=c: KB[:, c, t0 + s * SLAB:t0 + (s + 1) * SLAB]),
                    (lambda s, c=c: [R("kb", c, kb0 + 4 * s + j) for j in range(4)]), slot)
                   for c in range(2)])
        for t in range(8):
            b = 6 + t % 2
            for k in range(8):
                S.op("pe", lambda k=k, t=t: pe.matmul(
                    ps[b][:, 0:128], lhsT=xn[:, k, t * 128:(t + 1) * 128], rhs=Wvb[:, k, :],
                    start=(k == 0), stop=(k == 7)),
                    reads=[R("ring", slot, 0), R("xn", k, t // 4)], writes=[R("ps", b)],
                    signal=(k == 7))
            evac_copy(V[:, kb0 + t, 8:10, 0:64],
                      ps[b][:, 0:128].rearrange("p (a b) -> p a b", a=2),
                      [R("ps", b)], [R("vb", kb0 + t)])
        slot = next_group()
        Wva = ring[slot][:, 0:4096].rearrange("p (k n) -> p k n", k=8)
        for t in range(8):
            b = 6 + t % 2
            for k in range(8):
                S.op("pe", lambda k=k, t=t, b=b: pe.matmul(
                    ps[b][:, :], lhsT=xn[:, k, t * 128:(t + 1) * 128], rhs=Wva[:, k, :],
                    start=(k == 0), stop=(k == 7)),
                    reads=[R("ring", slot, 0), R("xn", k, t // 4)], writes=[R("ps", b)],
                    signal=(k == 7))
            evac_copy(V[:, kb0 + t, 0:8, 0:64],
                      ps[b][:, :].rearrange("p (a b) -> p a b", a=8),
                      [R("ps", b)], [R("va", kb0 + t)])

    sbank_ctr = [0]
    unit_ctr = [0]

    def m2(half):
        units = []
        for qb in range(8):
            m16 = half * 8 + qb
            for mixer in ("A", "B"):
                for g in range(2):
                    units.append((qb, m16, mixer, g))

        def stage1(u):
            qb, m16, mixer, g = u["spec"]
            nkb_full = 5 if mixer == "A" else 2
            kbs = [kb for kb in range(nkb_full) if m16 - (nkb_full - 1) + kb >= 0]
            nkb = len(kbs)
            u["kbs"] = kbs
            ui = unit_ctr[0] % 2
            unit_ctr[0] += 1
            u["ui"] = ui
            P_ = Pbuf[ui]
            order = [0, 2, 1, 3]
            u["order"] = order
            slots = [(hh, kb) for hh in order for kb in kbs]
            chunks = []
            for gi in range(2):
                base = gi * 2 * nkb
                for off in range(0, 2 * nkb, 4):
                    chunks.append(list(range(base + off, min(base + off + 4, base + 2 * nkb))))
            chunk_of = {}
            for ci, ch in enumerate(chunks):
                for sl in ch:
                    chunk_of[sl] = ci
            u["chunk_of"] = chunk_of
            SB = [0, 1, 2, 7]
            nch = len(chunks) // 2
            pair_emitters = []
            for cp in range(nch):
                def emit_pair(cp=cp):
                    pair = [(cp, chunks[cp]), (nch + cp, chunks[nch + cp])]
                    banks = []
                    for _ in pair:
                        banks.append(SB[sbank_ctr[0] % 4])
                        sbank_ctr[0] += 1
                    n = len(pair[0][1])
                    for j in range(n):
                        for pi, (ci, ch) in enumerate(pair):
                            b = banks[pi]
                            sl = ch[j]
                            hh, kb = slots[sl]
                            hd = 4 * g + hh
                            kblk = m16 - (nkb_full - 1) + kb
                            r0 = (hd % 2) * 64
                            c = hd // 2
                            if mixer == "A":
                                lhsT = KA[r0:r0 + 64, c, kblk * 128:(kblk + 1) * 128]
                                rhs = QA[r0:r0 + 64, c, qb * 128:(qb + 1) * 128]
                                rd = [R("ka", c, kblk), R("qa", c, qb)]
                            else:
                                kvh = hd // 4
                                lhsT = KB[r0:r0 + 64, kvh, kblk * 128:(kblk + 1) * 128]
                                rhs = QB[r0:r0 + 64, c, qb * 128:(qb + 1) * 128]
                                rd = [R("kb", kvh, kblk), R("qb", c, qb)]
                            S.op("pe", lambda b=b, j=j, lhsT=lhsT, rhs=rhs: pe.matmul(
                                ps[b][:, j * 128:(j + 1) * 128], lhsT=lhsT, rhs=rhs,
                                start=True, stop=True),
                                reads=rd, writes=[R("ps", b)], signal=(j == n - 1))
                    for pi, (ci, ch) in enumerate(pair):
                        b = banks[pi]
                        s0 = ch[0]
                        S.op("act", lambda b=b, n=n, s0=s0: act.activation(
                            out=P_[:, s0 * 128:(s0 + n) * 128], in_=ps[b][:, 0:n * 128], func=AF.Exp),
                            reads=[R("ps", b)], writes=[R("p", ui, ci)])
                pair_emitters.append(emit_pair)
            u["pairs"] = pair_emitters

        def stage1b(u):
            qb, m16, mixer, g = u["spec"]
            kbs, ui, nkb = u["kbs"], u["ui"], len(u["kbs"])
            order, chunk_of = u["order"], u["chunk_of"]
            P_ = Pbuf[ui]
            E = EA if mixer == "A" else EB
            for pos, hh in enumerate(order):
                hd = 4 * g + hh
                lo = pos * nkb
                segs = sorted(set(chunk_of[sl] for sl in range(lo, lo + nkb)))
                rr = [R("p", ui, sgm) for sgm in segs]
                en_ = "pool" if pos == 3 else "dve"
                eo_ = pool if pos == 3 else dve
                S.op(en_, lambda lo=lo, hd=hd, P_=P_, E=E, kbs=kbs, nkb=nkb, eo_=eo_: eo_.tensor_tensor(
                    out=P_[:, lo * 128:(lo + nkb) * 128], in0=P_[:, lo * 128:(lo + nkb) * 128],
                    in1=E[:, hd, kbs[0]:kbs[0] + nkb, :].rearrange("p a b -> p (a b)"), op=ALU.mult),
                    reads=rr + [R("EA" if mixer == "A" else "EB")], writes=rr)

        def stage2_head(u, hh):
            qb, m16, mixer, g = u["spec"]
            nkb_full = 5 if mixer == "A" else 2
            kbs, ui, nkb = u["kbs"], u["ui"], len(u["kbs"])
            P_ = Pbuf[ui]
            ob = 3 + ui
            u["ob"] = ob
            hd = 4 * g + hh
            vh = hd if mixer == "A" else 8 + hd // 4
            for i, kb in enumerate(kbs):
                ti = u["order"].index(hh) * nkb + i
                kblk = m16 - (nkb_full - 1) + kb
                S.op("pe", lambda ti=ti, kblk=kblk, i=i: pe.matmul(
                    ps[ob][:, hh * 65:(hh + 1) * 65], lhsT=P_[:, ti * 128:(ti + 1) * 128],
                    rhs=V[:, kblk, vh, :], start=(i == 0), stop=(i == nkb - 1)),
                    reads=[R("p", ui, u["chunk_of"][ti]), R("va" if mixer == "A" else "vb", kblk),
                           R("vones")],
                    writes=[R("ps", ob)], signal=(i == nkb - 1))

        def stage2_norm(u):
            qb, m16, mixer, g = u["spec"]
            ui = u["ui"]
            ob = u["ob"]
            O3 = ps[ob][:, 0:260].rearrange("p (h e) -> p h e", e=65)
            if mixer == "A":
                S.op("dve", lambda: dve.reciprocal(out=rcp[ui][:, :].rearrange("p (h e) -> p h e", e=1),
                                                   in_=O3[:, :, 64:65]),
                     reads=[R("ps", ob)], writes=[R("rcp", ui)])
            else:
                S.op("dve", lambda: dve.tensor_tensor(
                    out=den[ui][:, :].rearrange("p (h e) -> p h e", e=1), in0=O3[:, :, 64:65],
                    in1=esink[:, 4 * g:4 * g + 4].rearrange("p (h e) -> p h e", e=1), op=ALU.add),
                    reads=[R("ps", ob), R("esink")], writes=[R("den", ui)])
                S.op("dve", lambda: dve.reciprocal(out=rcp[ui][:, :], in_=den[ui][:, :]),
                     reads=[R("den", ui)], writes=[R("rcp", ui)])
            S.op("dve", lambda: dve.tensor_tensor(
                out=ytok[ui][:, :].rearrange("p (h d) -> p h d", h=4), in0=O3[:, :, 0:64],
                in1=rcp[ui][:, :].rearrange("p (h e) -> p h e", e=1).broadcast_to([128, 4, 64]),
                op=ALU.mult),
                reads=[R("ps", ob), R("rcp", ui)], writes=[R("ytok", ui)])

        def stage3(u):
            qb, m16, mixer, g = u["spec"]
            ui = u["ui"]
            tb = 5 + ui
            Tb = ps[tb][:, 0:128].bitcast(BF16)
            for i in range(2):
                S.op("pe", lambda i=i: pe.transpose(
                    out=Tb[:, i * 128:(i + 1) * 128], in_=ytok[ui][:, i * 128:(i + 1) * 128],
                    identity=ident_bf[:, :]),
                    reads=[R("ytok", ui), R("ident_bf")], writes=[R("ps", tb)], signal=(i == 1))
            Q = QA if mixer == "A" else QB
            nm = "qa" if mixer == "A" else "qb"
            evac_copy(Q[:, 2 * g:2 * g + 2, qb * 128:(qb + 1) * 128],
                      Tb.rearrange("p (a b) -> p a b", a=2),
                      [R("ps", tb)], [R(nm, 2 * g, qb), R(nm, 2 * g + 1, qb)])

        us = [{"spec": sp_} for sp_ in units]
        n = len(us)
        for i in range(n + 2):
            if i < n:
                stage1(us[i])
                for pr in us[i]["pairs"]:
                    pr()
                stage1b(us[i])
            if 0 <= i - 1 < n:
                for hh in range(4):
                    stage2_head(us[i - 1], hh)
                stage2_norm(us[i - 1])
            if 0 <= i - 2 < n:
                stage3(us[i - 2])

    m3_ctr = [0]

    def m3():
        for G in range(4):
            slot = next_group()
            r = ring[slot]
            Wga = r[:, 0:2048].rearrange("p (k n) -> p k n", k=8)
            Wgb = r[:, 2048:4096].rearrange("p (k n) -> p k n", k=8)
            WA = r[:, 4096:5120].rearrange("p (k n) -> p k n", k=4)
            WB = r[:, 5120:6144].rearrange("p (k n) -> p k n", k=4)
            for mm in range(2):
                m = 2 * G + mm
                for s in range(2):
                    par = m3_ctr[0] % 2
                    m3_ctr[0] += 1
                    bga, bgb, bpa, bpb = [4 * par + i for i in range(4)]
                    for k in range(8):
                        for (W, b) in ((Wga, bga), (Wgb, bgb)):
                            S.op("pe", lambda W=W, b=b, k=k: pe.matmul(
                                ps[b][:, :], lhsT=W[:, k, mm * 128:(mm + 1) * 128],
                                rhs=xn[:, k, slab(s)], start=(k == 0), stop=(k == 7)),
                                reads=[R("ring", slot, 0), R("xn", k, s)], writes=[R("ps", b)],
                                signal=(k == 7))
                    for (W, b, Q, nm) in ((WA, bpa, QA, "qa"), (WB, bpb, QB, "qb")):
                        for c in range(4):
                            S.op("pe", lambda W=W, b=b, c=c, Q=Q: pe.matmul(
                                ps[b][:, :], lhsT=W[:, c, mm * 128:(mm + 1) * 128],
                                rhs=Q[:, c, slab(s)], start=(c == 0), stop=(c == 3)),
                                reads=[R("ring", slot, 1)] + [R(nm, c, 4 * s + j) for j in range(4)],
                                writes=[R("ps", b)], signal=(c == 3))
                    sa, sb2 = sg[2 * par], sg[2 * par + 1]
                    S.op("act", lambda: act.activation(out=sa[:, :], in_=ps[bga][:, :],
                                                       func=AF.Sigmoid),
                         reads=[R("ps", bga)], writes=[R("sg", 2 * par)])
                    S.op("act", lambda: act.activation(out=sb2[:, :], in_=ps[bgb][:, :],
                                                       func=AF.Sigmoid),
                         reads=[R("ps", bgb)], writes=[R("sg", 2 * par + 1)])
                    S.op("dve", lambda: dve.tensor_tensor(out=t1[par][:, :], in0=ps[bpa][:, :],
                                                          in1=sa[:, :], op=ALU.mult),
                         reads=[R("ps", bpa), R("sg", 2 * par)], writes=[R("t1", par)])
                    S.op("dve", lambda: dve.tensor_tensor(out=t2[par][:, :], in0=ps[bpb][:, :],
                                                          in1=sb2[:, :], op=ALU.mult),
                         reads=[R("ps", bpb), R("sg", 2 * par + 1)], writes=[R("t2", par)])
                    S.op("pool", lambda m=m, s=s, par=par: pool.tensor_tensor(
                        out=merged[:, m, slab(s)], in0=t1[par][:, :], in1=t2[par][:, :], op=ALU.add),
                        reads=[R("t1", par), R("t2", par)],
                        writes=[R("mg", m, s), R("stg", m // 2)])
        oc = 0
        for H in range(2):
            slot = next_group()
            Wo = ring[slot][:, 0:4096].rearrange("p (k n) -> p k n", k=8)
            for mp in range(4):
                mo = 4 * H + mp
                for s in range(2):
                    b = oc % 4
                    oc += 1
                    for m in range(8):
                        S.op("pe", lambda m=m, b=b, mp=mp, s=s: pe.matmul(
                            ps[b][:, :], lhsT=Wo[:, m, mp * 128:(mp + 1) * 128],
                            rhs=merged[:, m, slab(s)], start=(m == 0), stop=(m == 7)),
                            reads=[R("ring", slot, 0), R("mg", m, s)], writes=[R("ps", b)],
                            signal=(m == 7))
                    S.op("dve", lambda b=b, mo=mo, s=s: dve.tensor_tensor(
                        out=h[:, mo, slab(s)], in0=ps[b][:, :], in1=h[:, mo, slab(s)], op=ALU.add),
                        reads=[R("ps", b), R("h", mo, s)], writes=[R("h", mo, s)])

    def p_items(tok0):
        def dma_p(t):
            st = pstg[t % 2]
            S.dma("sp", [lambda: sp.dma_start(
                out=st[:, :], in_=p_d[tok0 + t * 128:tok0 + (t + 1) * 128, :])],
                "pstg%d" % (t % 2), writes=[R("pstg", t % 2)])

        def xp(t):
            st = pstg[t % 2]
            b = 4 + t % 2
            for j in range(2):
                S.op("pe", lambda j=j: pe.transpose(
                    out=ps[b][:, j * 128:(j + 1) * 128], in_=st[:, j * 128:(j + 1) * 128],
                    identity=ident[:, :]),
                    reads=[R("pstg", t % 2), R("ident")], writes=[R("ps", b)], signal=(j == 1))
            evac_copy(pT[:, 0:2, t * 128:(t + 1) * 128],
                      ps[b][:, 0:256].rearrange("p (a b) -> p a b", a=2),
                      [R("ps", b)], [R("pT", t // 4)])

        def mk(i):
            def it():
                if i >= 1:
                    xp(i - 1)
                if i < 8:
                    dma_p(i)
            return it
        return [mk(i) for i in range(9)]

    def ple(tok0):
        oc = 0
        for H in range(2):
            slot = next_group()
            r = ring[slot]
            Wpg = r[:, 0:4096].rearrange("p (k n) -> p k n", k=8)
            Wpe = r[:, 4096:5120].rearrange("p (k n) -> p k n", k=2)
            for mp in range(4):
                mo = 4 * H + mp
                for s in range(2):
                    par = oc % 2
                    oc += 1
                    bg, bp = par, 2 + par
                    for k in range(8):
                        S.op("pe", lambda k=k, bg=bg, mp=mp, s=s: pe.matmul(
                            ps[bg][:, :], lhsT=Wpg[:, k, mp * 128:(mp + 1) * 128],
                            rhs=xn[:, k, slab(s)], start=(k == 0), stop=(k == 7)),
                            reads=[R("ring", slot, 0), R("xn", k, s)], writes=[R("ps", bg)],
                            signal=(k == 7))
                    for k in range(2):
                        S.op("pe", lambda k=k, bp=bp, mp=mp, s=s: pe.matmul(
                            ps[bp][:, :], lhsT=Wpe[:, k, mp * 128:(mp + 1) * 128],
                            rhs=pT[:, k, slab(s)], start=(k == 0), stop=(k == 1)),
                            reads=[R("ring", slot, 1), R("pT", s)], writes=[R("ps", bp)],
                            signal=(k == 1))
                    S.op("act", lambda bg=bg, par=par: act.activation(
                        out=t2[par][:, :], in_=ps[bg][:, :], func=AF.Sigmoid),
                        reads=[R("ps", bg)], writes=[R("t2", par)])
                    S.op("dve", lambda bp=bp, par=par: dve.tensor_tensor(
                        out=t1[par][:, :], in0=ps[bp][:, :], in1=t2[par][:, :], op=ALU.mult),
                        reads=[R("ps", bp), R("t2", par)], writes=[R("t1", par)])
                    S.op("pool", lambda mo=mo, s=s, par=par: pool.tensor_tensor(
                        out=h[:, mo, slab(s)], in0=h[:, mo, slab(s)], in1=t1[par][:, :], op=ALU.add),
                        reads=[R("t1", par), R("h", mo, s)], writes=[R("h", mo, s)])

    def store_out(tok0):
        for t in range(8):
            oi = t % NOST
            st = ostg[oi]
            for hf in range(2):
                b = (2 * t + hf) % 8
                for j in range(4):
                    k = hf * 4 + j
                    S.op("pe", lambda b=b, j=j, k=k, t=t: pe.transpose(
                        out=ps[b][:, j * 128:(j + 1) * 128], in_=h[:, k, t * 128:(t + 1) * 128],
                        identity=ident[:, :]),
                        reads=[R("h", k, t // 4), R("ident")], writes=[R("ps", b)],
                        signal=(j == 3))
                evac_copy(st[:, hf * 512:(hf + 1) * 512], ps[b][:, :],
                          [R("ps", b)], ostg_res(oi))
            S.dma("sp", [lambda t=t, st=st: sp.dma_start(
                out=out_d[tok0 + t * 128:tok0 + (t + 1) * 128, :], in_=st)],
                "ostg%d" % oi, reads=ostg_res(oi))

    def dump_dbg(what):
        allr = list(S.res.values())
        if what in ("m1", "m2"):
            return dump_h()
        if what in ("m1q", "m2y"):
            fns = [lambda: pool.dma_start(out=dbg_d[:, 0:4, :], in_=QA[:, :, :]),
                   lambda: pool.dma_start(out=dbg_d[:, 4:8, :], in_=QB[:, :, :])]
        elif what == "m1k":
            fns = [lambda: pool.dma_start(out=dbg_d[:, 0:4, :], in_=KA[:, :, 0:PASS]),
                   lambda: pool.dma_start(out=dbg_d[:, 4:6, :], in_=KB[:, :, 0:PASS])]
        elif what == "m1v":
            fns = [lambda: pool.dma_start(
                out=dbg_d[:, 0:6, :].rearrange("p a b -> p (a b)")[:, 0:5200].rearrange(
                    "p (a b) -> p a b", a=8),
                in_=V[:, 0:8].rearrange("p a b c -> p a (b c)"))]
        S.dma("pool", fns, "dbg", reads=allr)
        S.wait_sem("pool", "dbg")

    def dump_h():
        S.dma("sp", [lambda: sp.dma_start(out=dbg_d, in_=h[:, :, :])], "dbg",
              reads=[R("h", k, s) for k in range(8) for s in range(2)])
        S.wait_sem("sp", "dbg")

    def tok_of(pi):
        return (pi // 2) * SEQ + (pi % 2) * PASS

    for t in range(8):
        issue_x(tok_of(0), t)
    for ps_i in range(npass):
        seq, half = ps_i // 2, ps_i % 2
        tok0 = tok_of(ps_i)
        load_x(tok0)
        if stop == "load":
            dump_h(); break
        norm(0)
        ffn(0, extras=build_E_items() if ps_i == 0 else ())
        if stop == "ffn1":
            dump_h(); break
        norm(1)
        m1(half)
        if stop in ("m1", "m1q", "m1k", "m1v"):
            dump_dbg(stop); break
        m2(half)
        if stop in ("m2", "m2y"):
            dump_dbg(stop); break
        m3()
        if ps_i + 1 < npass:
            for t in range(4):
                issue_x(tok_of(ps_i + 1), t)
        if stop == "mix":
            dump_h(); break
        norm(2)
        ffn(1, extras=p_items(tok0))
        if stop == "ffn2":
            dump_h(); break
        norm(3)
        ple(tok0)
        if stop == "ple":
            dump_h(); break
        if ps_i + 1 < npass:
            for t in range(4, 8):
                issue_x(tok_of(ps_i + 1), t)
        store_out(tok0)
    for i in range(NOST):
        S.wait_sem("sp", "ostg%d" % i)
    return nc


def _host_consts():
    kl = np.arange(128)[:, None, None]
    kb = np.arange(5)[None, :, None]
    ql = np.arange(128)[None, None, :]
    rel = ql + 512 - 128 * kb - kl
    idxA = np.clip(rel, -128, 128) + 128
    jc = (128 * kb + kl) // 64
    qc = ql // 64
    validA = (jc >= qc) & (jc <= qc + 8)
    maskA = np.where(validA, 0.0, NEG).astype(np.float32)
    maskA = np.broadcast_to(maskA[:, None], (128, 8, 5, 128)).copy()
    kb2 = np.arange(2)[None, :, None]
    relB = (128 + ql) - (128 * kb2 + kl)
    jcB = (128 * kb2 + kl) // 64
    validB = (jcB >= qc) & (jcB <= qc + 2)
    slopes = np.array([2.0 ** (-8.0 * (hh + 1) / 8) for hh in range(8)], dtype=np.float32)
    biasB = -slopes[None, :, None, None] * np.abs(relB).astype(np.float32)[:, None]
    biasB = np.where(validB[:, None], biasB, NEG).astype(np.float32)
    return idxA, maskA, np.ascontiguousarray(biasB)


def make_in_maps(inputs):
    f = lambda a: np.ascontiguousarray(np.asarray(a, dtype=np.float32))
    x = f(inputs["x"])
    p = f(inputs["p"])[0]
    idxA, maskA, biasB = _host_consts()
    arb = f(inputs["a_rel_bias"])[0]
    biasA = np.ascontiguousarray(np.transpose(arb[:, idxA], (1, 0, 2, 3)))
    gains = np.stack([f(inputs[n])[0].reshape(8, 128).T for n in
                      ("ffn1_norm", "mix_norm", "ffn2_norm", "ple_norm")], axis=1)
    gains = np.ascontiguousarray(gains.reshape(128, 32))
    gqk = np.stack([np.tile(f(inputs[n])[0], 2) for n in
                    ("a_q_norm", "a_k_norm", "b_q_norm", "b_k_norm")], axis=1)
    gqk = np.ascontiguousarray(gqk)
    sinks = np.ascontiguousarray(np.broadcast_to(f(inputs["b_sinks"])[0][None, :], (128, 8)))
    shared = {
        "ffn1_w_gu": f(inputs["ffn1_w_gu"])[0], "ffn2_w_gu": f(inputs["ffn2_w_gu"])[0],
        "ffn1_w_down": f(inputs["ffn1_w_down"])[0], "ffn2_w_down": f(inputs["ffn2_w_down"])[0],
        "w_in": f(inputs["w_in"])[0], "w_gate": f(inputs["w_gate"])[0],
        "w_proj_a": f(inputs["w_proj_a"])[0], "w_proj_b": f(inputs["w_proj_b"])[0],
        "w_out": f(inputs["w_out"])[0], "w_ple_gate": f(inputs["w_ple_gate"])[0],
        "w_ple_proj": f(inputs["w_ple_proj"])[0],
        "gains": gains, "gqk": gqk, "sinks": sinks, "ident": np.eye(128, dtype=np.float32),
        "biasA": biasA, "maskA": maskA, "biasB": biasB,
    }
    in_maps = []
    for c in range(N_CORES):
        m = dict(shared)
        m["x"] = np.ascontiguousarray(x[2 * c:2 * c + 2].reshape(TOK_CORE, D))
        m["p"] = np.ascontiguousarray(p[2 * c:2 * c + 2].reshape(TOK_CORE, 256))
        in_maps.append(m)
    return in_maps


def kernel(**inputs):
    nc = build_nc()
    in_maps = make_in_maps(inputs)
    res = run_bass_kernel_spmd(nc, in_maps, core_ids=list(range(N_CORES)))
    out = np.stack([np.asarray(r["out"]).reshape(2, SEQ, D) for r in res.results], axis=0)
    return out.reshape(16, SEQ, D).astype(np.float32)
```

```python
import numpy as np
import concourse.bass as bass
import concourse.mybir as mybir
from concourse.bass_utils import run_bass_kernel_spmd

F32 = mybir.dt.float32
BF16 = mybir.dt.bfloat16
AF = mybir.ActivationFunctionType
ALU = mybir.AluOpType

N_CORES = 8
D = 1024
KC = 8
DFF = 2816
SEQ = 2048
PASS = 1024
SLAB = 512
TOK_CORE = 4096
EPS = 1e-6
NEG = -30000.0
STRICT = True


class Res:
    __slots__ = ("w", "rs")

    def __init__(self):
        self.w = None
        self.rs = {}


class Sched:
    def __init__(self, nc):
        self.nc = nc
        self.eng = {"pe": nc.tensor, "act": nc.scalar, "dve": nc.vector,
                    "pool": nc.gpsimd, "sp": nc.sync}
        self.sems = {}
        self.cnt = {}
        self.seen = {e: {} for e in self.eng}
        self.res = {}
        for e in ("pe", "act", "dve", "pool"):
            self.newsem(e)

    def newsem(self, key):
        if key not in self.sems:
            self.sems[key] = self.nc.alloc_semaphore("s_" + key)
            self.cnt[key] = 0

    def R(self, *key):
        r = self.res.get(key)
        if r is None:
            r = self.res[key] = Res()
        return r

    def _waits(self, eng, reads, writes):
        waits = {}

        def need(st):
            if st is None:
                return
            k, v = st
            if k == eng and (eng == "pe" or not STRICT):
                return
            if self.seen[eng].get(k, 0) >= v:
                return
            if waits.get(k, 0) < v:
                waits[k] = v

        for r in reads:
            need(r.w)
        for r in writes:
            need(r.w)
            for k, v in r.rs.items():
                need((k, v))
        for k, v in waits.items():
            self.eng[eng].wait_ge(self.sems[k], v)
            self.seen[eng][k] = v

    def _stamp(self, st, reads, writes):
        k, v = st
        for r in reads:
            if r.rs.get(k, 0) < v:
                r.rs[k] = v
        for r in writes:
            r.w = st
            r.rs = {}

    def op(self, eng, fn, reads=(), writes=(), signal=True):
        self._waits(eng, reads, writes)
        inst = fn()
        if eng == "pe" and not signal:
            st = ("pe", self.cnt["pe"] + 1)
        else:
            self.cnt[eng] += 1
            inst.then_inc(self.sems[eng], 1)
            st = (eng, self.cnt[eng])
        self._stamp(st, reads, writes)

    def dma(self, eng, fns, dsem, reads=(), writes=()):
        self.newsem(dsem)
        self._waits(eng, reads, writes)
        for fn in fns:
            inst = fn()
            self.cnt[dsem] += 16
            inst.then_inc(self.sems[dsem], 16)
        self._stamp((dsem, self.cnt[dsem]), reads, writes)

    def wait_sem(self, eng, key):
        if key in self.sems and self.cnt[key] > 0:
            self.eng[eng].wait_ge(self.sems[key], self.cnt[key])


def build_nc(stop=None, npass=4):
    nc = bass.Bass("TRN2", target_bir_lowering=False)
    S = Sched(nc)
    R = S.R

    def din(name, shape):
        return nc.dram_tensor(name, list(shape), F32, kind="ExternalInput").ap()

    x_d = din("x", [TOK_CORE, D])
    p_d = din("p", [TOK_CORE, 256])
    wgu_d = [din("ffn1_w_gu", [D, 2 * DFF]), din("ffn2_w_gu", [D, 2 * DFF])]
    wdn_d = [din("ffn1_w_down", [DFF, D]), din("ffn2_w_down", [DFF, D])]
    win_d = din("w_in", [D, 2304])
    wgate_d = din("w_gate", [D, 2 * D])
    wpa_d = din("w_proj_a", [512, D])
    wpb_d = din("w_proj_b", [512, D])
    wout_d = din("w_out", [D, D])
    wpg_d = din("w_ple_gate", [D, D])
    wpe_d = din("w_ple_proj", [256, D])
    gains_d = din("gains", [128, 32])
    gqk_d = din("gqk", [128, 4])
    sinks_d = din("sinks", [128, 8])
    ident_d = din("ident", [128, 128])
    biasA_d = din("biasA", [128, 8, 5, 128])
    maskA_d = din("maskA", [128, 8, 5, 128])
    biasB_d = din("biasB", [128, 8, 2, 128])
    out_d = nc.dram_tensor("out", [TOK_CORE, D], F32, kind="ExternalOutput").ap()
    dbg_d = None
    if stop is not None:
        dbg_d = nc.dram_tensor("dbg", [128, 8, PASS], F32, kind="ExternalOutput").ap()

    def sb(name, shape, dt):
        return nc.alloc_sbuf_tensor("sb_" + name, list(shape), dt)

    h = sb("h", [128, 8, PASS], F32)
    xn = sb("xn", [128, 8, PASS], BF16)
    QA = sb("QA", [128, 4, PASS], BF16)
    QB = sb("QB", [128, 4, PASS], BF16)
    KA = sb("KA", [128, 4, SEQ], BF16)
    KB = sb("KB", [128, 2, SEQ], BF16)
    V = sb("V", [128, 16, 10, 65], BF16)
    EA = sb("EA", [128, 8, 5, 128], BF16)
    EB = sb("EB", [128, 8, 2, 128], BF16)
    ring = [sb("ring%d" % i, [128, 6144], BF16) for i in range(2)]
    actb = [sb("act%d" % i, [128, 2, 512], BF16) for i in range(2)]
    Pbuf = [sb("P%d" % i, [128, 20 * 128], BF16) for i in range(2)]
    merged = sb("merged", [128, 8, PASS], BF16)
    stg = [merged[:, 2 * i:2 * i + 2, :].bitcast(F32) for i in range(4)]
    stg = [s.rearrange("p a b -> p (a b)") for s in stg]
    ostg = [Pbuf[i][:, 0:2048].bitcast(F32) for i in range(2)]
    for Q_ in (QA, QB):
        for i in range(2):
            ostg.append(Q_[:, 2 * i:2 * i + 2, :].bitcast(F32).rearrange("p a b -> p (a b)"))
    ptmp = [Pbuf[i][:, 0:2560].bitcast(F32) for i in range(2)]

    def stg_res(i):
        return [R("stg", i)] + [R("mg", 2 * i + a, s_) for a in range(2) for s_ in range(2)]

    def pbuf_res(i):
        return [R("p", i, c) for c in range(6)]

    def ostg_res(i):
        if i < 2:
            return pbuf_res(i)
        if i >= 6:
            nm = "t1" if i == 6 else "t2"
            return [R(nm, 0), R(nm, 1)]
        nm = "qa" if i < 4 else "qb"
        c0 = 2 * (i % 2)
        return [R(nm, c0 + a, qb_) for a in range(2) for qb_ in range(8)]
    sq = [sb("sq%d" % i, [128, 512], BF16) for i in range(4)]
    lnv = sb("lnv", [128, 512], F32)
    rstd = [sb("rstd%d" % i, [128, 512], F32) for i in range(2)]
    sg = [sb("sg%d" % i, [128, 512], BF16) for i in range(4)]
    t1all = sb("t1all", [128, 2, 512], F32)
    t2all = sb("t2all", [128, 2, 512], F32)
    t1 = [t1all[:, i, :] for i in range(2)]
    t2 = [t2all[:, i, :] for i in range(2)]
    ostg.append(t1all[:, :, :].rearrange("p a b -> p (a b)"))
    ostg.append(t2all[:, :, :].rearrange("p a b -> p (a b)"))
    NOST = len(ostg)
    pT = sb("pT", [128, 2, PASS], BF16)
    pstg = [sb("pstg%d" % i, [128, 256], F32) for i in range(2)]
    ytok = [sb("ytok%d" % i, [128, 256], BF16) for i in range(2)]
    ident = sb("ident", [128, 128], F32)
    ident_bf = sb("ident_bf", [128, 128], BF16)
    ones_bf = sb("ones_bf", [128, 128], BF16)
    bones = sb("bones", [128, 128], BF16)
    gains = sb("gains", [128, 32], F32)
    gqk = sb("gqk", [128, 4], F32)
    esink = sb("esink", [128, 8], F32)
    den = [sb("den%d" % i, [128, 4], F32) for i in range(2)]
    rcp = [sb("rcp%d" % i, [128, 4], F32) for i in range(2)]
    warm = sb("warm", [128, 2], F32)
    ps = [nc.alloc_psum_tensor("ps%d" % i, [128, 512], F32) for i in range(8)]

    pe, act, dve, pool, sp = nc.tensor, nc.scalar, nc.vector, nc.gpsimd, nc.sync

    S.dma("sp", [
        lambda: sp.dma_start(out=ident[:, :], in_=ident_d),
        lambda: sp.dma_start(out=gains[:, :], in_=gains_d),
        lambda: sp.dma_start(out=gqk[:, :], in_=gqk_d),
        lambda: sp.dma_start(out=esink[:, :], in_=sinks_d),
    ], "setup", writes=[R("ident"), R("gains"), R("gqk"), R("esink")])
    S.op("dve", lambda: dve.tensor_copy(out=ident_bf[:, :], in_=ident[:, :]),
         reads=[R("ident")], writes=[R("ident_bf")])
    S.op("dve", lambda: dve.memset(ones_bf[:, :], 1.0), writes=[R("ones")])
    S.op("dve", lambda: dve.memset(warm[:, :], 1.0), writes=[R("warm")])
    S.op("dve", lambda: dve.memset(bones[:, :], 0.0), writes=[R("bones")])
    S.op("dve", lambda: dve.memset(bones[0:64, 0:64], 1.0), writes=[R("bones")])
    S.op("dve", lambda: dve.memset(bones[64:128, 64:128], 1.0), writes=[R("bones")])
    S.op("dve", lambda: dve.memset(V[:, :, :, 64:65], 1.0), writes=[R("vones")])
    S.op("dve", lambda: dve.tensor_scalar(out=gqk[:, 0:1], in0=gqk[:, 0:1], scalar1=0.125,
                                          scalar2=None, op0=ALU.mult),
         reads=[R("gqk")], writes=[R("gqk")])
    S.op("dve", lambda: dve.tensor_scalar(out=gqk[:, 2:3], in0=gqk[:, 2:3], scalar1=0.125,
                                          scalar2=None, op0=ALU.mult),
         reads=[R("gqk")], writes=[R("gqk")])
    S.op("act", lambda: act.activation(out=esink[:, :], in_=esink[:, :], func=AF.Exp),
         reads=[R("esink")], writes=[R("esink")])
    def epos(hd):
        return 4 * (hd // 4) + [0, 2, 1, 3].index(hd % 4)

    def build_E_items():
        items = []
        for hd in range(8):
            def it(hd=hd):
                i = hd % 2
                a = ptmp[i][:, 0:640]
                b = ptmp[i][:, 640:1280]
                S.dma("sp", [
                    lambda: sp.dma_start(out=a, in_=biasA_d[:, hd].rearrange("p a b -> p (a b)")),
                    lambda: sp.dma_start(out=b, in_=maskA_d[:, hd].rearrange("p a b -> p (a b)")),
                ], "setupE%d" % i, writes=pbuf_res(i))
                S.op("dve", lambda: dve.tensor_tensor(out=a, in0=a, in1=b, op=ALU.add),
                     reads=pbuf_res(i), writes=pbuf_res(i))
                S.op("act", lambda: act.activation(
                    out=EA[:, epos(hd)].rearrange("p a b -> p (a b)"), in_=a, func=AF.Exp),
                    reads=pbuf_res(i), writes=[R("EA")])
            items.append(it)
        for hp in range(4):
            def it(hp=hp):
                i = hp % 2
                a = ptmp[i][:, 0:512]
                S.dma("sp", [
                    lambda: sp.dma_start(
                        out=a, in_=biasB_d[:, 2 * hp:2 * hp + 2].rearrange("p h a b -> p (h a b)")),
                ], "setupE%d" % i, writes=pbuf_res(i))
                for q_ in range(2):
                    S.op("act", lambda q_=q_: act.activation(
                        out=EB[:, epos(2 * hp + q_)].rearrange("p a b -> p (a b)"),
                        in_=a[:, q_ * 256:(q_ + 1) * 256], func=AF.Exp),
                        reads=pbuf_res(i), writes=[R("EB")])
            items.append(it)
        return items

    def wv(dram, p=128):
        return dram.rearrange("(kc p) n -> p kc n", p=p)

    def ffn_group(w, g):
        def pieces(slot):
            r = ring[slot]
            return [
                (r[:, 0:2048].rearrange("p (k n) -> p k n", k=8),
                 wv(wgu_d[w])[:, :, 256 * g:256 * g + 256], 0),
                (r[:, 2048:4096].rearrange("p (k n) -> p k n", k=8),
                 wv(wgu_d[w])[:, :, DFF + 256 * g:DFF + 256 * g + 256], 0),
                (r[:, 4096:6144].rearrange("p (k n) -> p k n", k=2),
                 wv(wdn_d[w])[:, 2 * g:2 * g + 2, :], 1),
            ]
        return pieces

    def cols_group(dram, c0, n, kc=8):
        def pieces(slot):
            r = ring[slot]
            return [(r[:, 0:kc * n].rearrange("p (k n) -> p k n", k=kc),
                     wv(dram)[:, :, c0:c0 + n], 0)]
        return pieces

    def kbvb_group():
        def pieces(slot):
            r = ring[slot]
            kd = r[:, 0:2048].rearrange("p (k n) -> p k n", k=8)
            out = []
            for kvh in range(2):
                for dup in range(2):
                    out.append((kd[:, :, kvh * 128 + dup * 64:kvh * 128 + dup * 64 + 64],
                                wv(win_d)[:, :, 2048 + kvh * 64:2048 + kvh * 64 + 64], 0))
            out.append((r[:, 2048:3072].rearrange("p (k n) -> p k n", k=8),
                        wv(win_d)[:, :, 2176:2304], 0))
            return out
        return pieces

    def m3_group(G):
        def pieces(slot):
            r = ring[slot]
            return [
                (r[:, 0:2048].rearrange("p (k n) -> p k n", k=8),
                 wv(wgate_d)[:, :, 256 * G:256 * G + 256], 0),
                (r[:, 2048:4096].rearrange("p (k n) -> p k n", k=8),
                 wv(wgate_d)[:, :, D + 256 * G:D + 256 * G + 256], 0),
                (r[:, 4096:5120].rearrange("p (k n) -> p k n", k=4),
                 wv(wpa_d)[:, :, 256 * G:256 * G + 256], 1),
                (r[:, 5120:6144].rearrange("p (k n) -> p k n", k=4),
                 wv(wpb_d)[:, :, 256 * G:256 * G + 256], 1),
            ]
        return pieces

    def ple_group(H):
        def pieces(slot):
            r = ring[slot]
            return [
                (r[:, 0:4096].rearrange("p (k n) -> p k n", k=8),
                 wv(wpg_d)[:, :, 512 * H:512 * H + 512], 0),
                (r[:, 4096:5120].rearrange("p (k n) -> p k n", k=2),
                 wv(wpe_d)[:, :, 512 * H:512 * H + 512], 1),
            ]
        return pieces

    pass_groups = ([ffn_group(0, g) for g in range(11)]
                   + [cols_group(win_d, 0, 512), cols_group(win_d, 512, 512),
                      cols_group(win_d, 1536, 512), kbvb_group(),
                      cols_group(win_d, 1024, 512)]
                   + [m3_group(G) for G in range(4)]
                   + [cols_group(wout_d, 0, 512), cols_group(wout_d, 512, 512)]
                   + [ffn_group(1, g) for g in range(11)]
                   + [ple_group(0), ple_group(1)])
    NG = len(pass_groups)
    all_groups = pass_groups * npass
    gstate = {"issued": [0, 0], "cur": -1}

    def issue_part(part):
        gi = gstate["issued"][part]
        if gi >= len(all_groups):
            return
        slot = gi % 2
        pcs = [(o, i) for (o, i, pt) in all_groups[gi](slot) if pt == part]
        if pcs:
            S.dma("pool", [(lambda o=o, i=i: pool.dma_start(out=o, in_=i)) for o, i in pcs],
                  "ring%d_%d" % (slot, part), writes=[R("ring", slot, part)])
        gstate["issued"][part] += 1

    def next_group(pf=True):
        gstate["cur"] += 1
        gi = gstate["cur"]
        for part in range(2):
            while gstate["issued"][part] <= gi:
                issue_part(part)
        if pf:
            prefetch(0)
            prefetch(1)
        return gi % 2

    def prefetch(part):
        if gstate["issued"][part] <= gstate["cur"] + 1:
            issue_part(part)

    def slab(s):
        return slice(s * SLAB, (s + 1) * SLAB)

    cp_rr = [0]

    def evac_copy(out, in_, reads, writes):
        cp_rr[0] ^= 1
        if cp_rr[0]:
            S.op("act", lambda: act.activation(out=out, in_=in_, func=AF.Copy),
                 reads=reads, writes=writes)
        else:
            S.op("dve", lambda: dve.tensor_copy(out=out, in_=in_), reads=reads, writes=writes)

    xn_f32 = xn[:, :, :].bitcast(F32).rearrange("p a b -> p (a b)")

    def xstage(t):
        if t < 4:
            return stg[t], stg_res(t), "stg%d" % t
        j = t - 4
        return (xn_f32[:, j * 1024:(j + 1) * 1024],
                [R("xn", 2 * j + a, s_) for a in range(2) for s_ in range(2)], "stgB%d" % j)

    def issue_x(tok0, t):
        ap, res, sem = xstage(t)
        S.dma("sp", [lambda: sp.dma_start(
            out=ap, in_=x_d[tok0 + t * 128:tok0 + (t + 1) * 128, :])],
            sem, writes=res)

    def load_x(tok0):
        for t in range(8):
            st, res, _ = xstage(t)
            for hf in range(2):
                b = (2 * t + hf) % 8
                for j in range(4):
                    k = hf * 4 + j
                    S.op("pe", lambda b=b, j=j, k=k, st=st: pe.transpose(
                        out=ps[b][:, j * 128:(j + 1) * 128], in_=st[:, k * 128:(k + 1) * 128],
                        identity=ident[:, :]),
                        reads=res + [R("ident")], writes=[R("ps", b)], signal=(j == 3))
                evac_copy(h[:, hf * 4:hf * 4 + 4, t * 128:(t + 1) * 128],
                          ps[b][:, :].rearrange("p (a b) -> p a b", a=4),
                          [R("ps", b)], [R("h", hf * 4 + j, t // 4) for j in range(4)])

    def norm(gidx):
        S.op("act", lambda: act.activation(out=warm[:, 1:2], in_=warm[:, 0:1], func=AF.Ln),
             reads=[R("warm")], writes=[R("warm_o")])
        for s in range(2):
            nb = 6 + s
            for k in range(8):
                q = sq[k % 4]
                S.op("act", lambda k=k, q=q: act.activation(out=q[:, :], in_=h[:, k, slab(s)],
                                                            func=AF.Square),
                     reads=[R("h", k, s)], writes=[R("sq", k % 4)])
                S.op("pe", lambda k=k, q=q: pe.matmul(ps[nb][:, :], lhsT=ones_bf[:, :], rhs=q[:, :],
                                                      start=(k == 0), stop=(k == 7)),
                     reads=[R("sq", k % 4), R("ones")], writes=[R("ps", nb)], signal=True)
            S.op("act", lambda: act.activation(out=lnv[:, :], in_=ps[nb][:, :], func=AF.Ln,
                                               bias=EPS, scale=1.0 / D),
                 reads=[R("ps", nb)], writes=[R("lnv")])
            S.op("act", lambda: act.activation(out=ps[nb][:, :], in_=lnv[:, :], func=AF.Exp,
                                               scale=-0.5),
                 reads=[R("lnv")], writes=[R("ps", nb)])
            for k in range(8):
                S.op("dve", lambda k=k: dve.scalar_tensor_tensor(
                    out=xn[:, k, slab(s)], in0=h[:, k, slab(s)],
                    scalar=gains[:, gidx * 8 + k:gidx * 8 + k + 1], in1=ps[nb][:, :],
                    op0=ALU.mult, op1=ALU.mult),
                    reads=[R("h", k, s), R("ps", nb), R("gains")], writes=[R("xn", k, s)])

    step_ctr = [0]

    def ffn(w, extras=()):
        extras = list(extras)
        prev = None
        for g in range(11):
            slot = next_group(pf=False)
            prefetch(0)
            r = ring[slot]
            Wg = r[:, 0:2048].rearrange("p (k n) -> p k n", k=8)
            Wu = r[:, 2048:4096].rearrange("p (k n) -> p k n", k=8)
            Wd = r[:, 4096:6144].rearrange("p (k n) -> p k n", k=2)
            for s in range(2):
                ab = step_ctr[0] % 2
                step_ctr[0] += 1
                mmlist = []
                for jj in range(2):
                    for k in range(8):
                        mmlist.append((Wg, jj, jj, k))
                        mmlist.append((Wu, jj, 2 + jj, k))
                for qi in range(4):
                    for (W, jj, b, k) in mmlist[8 * qi:8 * qi + 8]:
                        S.op("pe", lambda W=W, b=b, k=k, jj=jj: pe.matmul(
                            ps[b][:, :], lhsT=W[:, k, jj * 128:(jj + 1) * 128],
                            rhs=xn[:, k, slab(s)], start=(k == 0), stop=(k == 7)),
                            reads=[R("ring", slot, 0), R("xn", k, s)], writes=[R("ps", b)],
                            signal=(k == 7))
                    jj = qi // 2
                    if qi in (1, 3):
                        sgi = 2 * ab + jj
                        S.op("act", lambda jj=jj, sgi=sgi: act.activation(
                            out=sg[sgi][:, :], in_=ps[jj][:, :], func=AF.Silu),
                            reads=[R("ps", jj)], writes=[R("sg", sgi)])
                        S.op("dve", lambda jj=jj, sgi=sgi, ab=ab: dve.tensor_tensor(
                            out=actb[ab][:, jj, :], in0=ps[2 + jj][:, :], in1=sg[sgi][:, :],
                            op=ALU.mult),
                            reads=[R("ps", 2 + jj), R("sg", sgi)], writes=[R("act", ab, jj)])
                    if prev is not None:
                        prev[2 * qi]()
                        prev[2 * qi + 1]()
                if s == 0:
                    prefetch(1)

                def mk_pair(m, s=s, ab=ab, Wd=Wd, slot=slot):
                    def pair():
                        b = 4 + m % 4
                        for jj in range(2):
                            S.op("pe", lambda jj=jj: pe.matmul(
                                ps[b][:, :], lhsT=Wd[:, jj, m * 128:(m + 1) * 128],
                                rhs=actb[ab][:, jj, :], start=(jj == 0), stop=(jj == 1)),
                                reads=[R("ring", slot, 1), R("act", ab, jj)], writes=[R("ps", b)],
                                signal=(jj == 1))
                        S.op("dve", lambda: dve.scalar_tensor_tensor(
                            out=h[:, m, slab(s)], in0=ps[b][:, :], scalar=0.5,
                            in1=h[:, m, slab(s)], op0=ALU.mult, op1=ALU.add),
                            reads=[R("ps", b), R("h", m, s)], writes=[R("h", m, s)])
                    return pair
                prev = [mk_pair(m) for m in range(8)]
                if s == 1 and extras:
                    extras.pop(0)()
        for pr in prev:
            pr()
        for ex in extras:
            ex()

    qk_ctr = [0]

    def qk_chunks(items):
        work = [(it, s) for it in items for s in range(2)]
        pend = None
        for (it, s) in work:
            lhsT_fn, gcol, dest_fn, dres_fn, slot = it
            i = qk_ctr[0]
            qk_ctr[0] += 1
            b = i % 4
            for k in range(8):
                S.op("pe", lambda k=k, b=b, lhsT_fn=lhsT_fn, s=s: pe.matmul(
                    ps[b][:, :], lhsT=lhsT_fn(k), rhs=xn[:, k, slab(s)],
                    start=(k == 0), stop=(k == 7)),
                    reads=[R("ring", slot, 0), R("xn", k, s)], writes=[R("ps", b)], signal=(k == 7))
            S.op("act", lambda b=b, i=i: act.activation(out=sq[i % 4][:, :], in_=ps[b][:, :],
                                                        func=AF.Square),
                 reads=[R("ps", b)], writes=[R("sq", i % 4)])
            if pend is not None:
                pend()

            def rest(i=i, b=b, gcol=gcol, dest_fn=dest_fn, dres_fn=dres_fn, s=s):
                q = sq[i % 4]
                sb_ = 4 + i % 2
                rs = rstd[i % 2]
                S.op("pe", lambda: pe.matmul(ps[sb_][:, :], lhsT=bones[:, :], rhs=q[:, :],
                                             start=True, stop=True),
                     reads=[R("sq", i % 4), R("bones")], writes=[R("ps", sb_)], signal=True)
                S.op("act", lambda: act.activation(out=lnv[:, :], in_=ps[sb_][:, :], func=AF.Ln,
                                                   bias=EPS, scale=1.0 / 64),
                     reads=[R("ps", sb_)], writes=[R("lnv")])
                S.op("act", lambda: act.activation(out=rs[:, :], in_=lnv[:, :], func=AF.Exp,
                                                   scale=-0.5),
                     reads=[R("lnv")], writes=[R("rstd", i % 2)])
                S.op("dve", lambda: dve.scalar_tensor_tensor(
                    out=dest_fn(s), in0=ps[b][:, :], scalar=gqk[:, gcol:gcol + 1], in1=rs[:, :],
                    op0=ALU.mult, op1=ALU.mult),
                    reads=[R("ps", b), R("rstd", i % 2), R("gqk")], writes=dres_fn(s))
            pend = rest
        pend()

    def m1(half):
        kb0 = half * 8
        t0 = half * PASS
        slot = next_group()
        W = ring[slot][:, 0:4096].rearrange("p (k n) -> p k n", k=8)
        qk_chunks([((lambda k, c=c, W=W: W[:, k, c * 128:(c + 1) * 128]), 0,
                    (lambda s, c=c: QA[:, c, slab(s)]),
                    (lambda s, c=c: [R("qa", c, 4 * s + j) for j in range(4)]), slot)
                   for c in range(4)])
        slot = next_group()
        W = ring[slot][:, 0:4096].rearrange("p (k n) -> p k n", k=8)
        qk_chunks([((lambda k, c=c, W=W: W[:, k, c * 128:(c + 1) * 128]), 1,
                    (lambda s, c=c: KA[:, c, t0 + s * SLAB:t0 + (s + 1) * SLAB]),
                    (lambda s, c=c: [R("ka", c, kb0 + 4 * s + j) for j in range(4)]), slot)
                   for c in range(4)])
        slot = next_group()
        W = ring[slot][:, 0:4096].rearrange("p (k n) -> p k n", k=8)
        qk_chunks([((lambda k, c=c, W=W: W[:, k, c * 128:(c + 1) * 128]), 2,
                    (lambda s, c=c: QB[:, c, slab(s)]),
                    (lambda s, c=c: [R("qb", c, 4 * s + j) for j in range(4)]), slot)
                   for c in range(4)])
        slot = next_group()
        W = ring[slot][:, 0:2048].rearrange("p (k n) -> p k n", k=8)
        Wvb = ring[slot][:, 2048:3072].rearrange("p (k n) -> p k n", k=8)
        qk_chunks([((lambda k, c=c, W=W: W[:, k, c * 128:(c + 1) * 128]), 3,
                    (lambda s, c=c: KB[:, c, t0 + s * SLAB:t0 + (s + 1) * SLAB]),
                    (lambda s, c=c: [R("kb", c, kb0 + 4 * s + j) for j in range(4)]), slot)
                   for c in range(2)])
        for t in range(8):
            b = 6 + t % 2
            for k in range(8):
                S.op("pe", lambda k=k, t=t: pe.matmul(
                    ps[b][:, 0:128], lhsT=xn[:, k, t * 128:(t + 1) * 128], rhs=Wvb[:, k, :],
                    start=(k == 0), stop=(k == 7)),
                    reads=[R("ring", slot, 0), R("xn", k, t // 4)], writes=[R("ps", b)],
                    signal=(k == 7))
            evac_copy(V[:, kb0 + t, 8:10, 0:64],
                      ps[b][:, 0:128].rearrange("p (a b) -> p a b", a=2),
                      [R("ps", b)], [R("vb", kb0 + t)])
        slot = next_group()
        Wva = ring[slot][:, 0:4096].rearrange("p (k n) -> p k n", k=8)
        for t in range(8):
            b = 6 + t % 2
            for k in range(8):
                S.op("pe", lambda k=k, t=t, b=b: pe.matmul(
                    ps[b][:, :], lhsT=xn[:, k, t * 128:(t + 1) * 128], rhs=Wva[:, k, :],
                    start=(k == 0), stop=(k == 7)),
                    reads=[R("ring", slot, 0), R("xn", k, t // 4)], writes=[R("ps", b)],
                    signal=(k == 7))
            evac_copy(V[:, kb0 + t, 0:8, 0:64],
                      ps[b][:, :].rearrange("p (a b) -> p a b", a=8),
                      [R("ps", b)], [R("va", kb0 + t)])

    sbank_ctr = [0]
    unit_ctr = [0]

    def m2(half):
        units = []
        for qb in range(8):
            m16 = half * 8 + qb
            for mixer in ("A", "B"):
                for g in range(2):
                    units.append((qb, m16, mixer, g))

        def stage1(u):
            qb, m16, mixer, g = u["spec"]
            nkb_full = 5 if mixer == "A" else 2
            kbs = [kb for kb in range(nkb_full) if m16 - (nkb_full - 1) + kb >= 0]
            nkb = len(kbs)
            u["kbs"] = kbs
            ui = unit_ctr[0] % 2
            unit_ctr[0] += 1
            u["ui"] = ui
            P_ = Pbuf[ui]
            order = [0, 2, 1, 3]
            u["order"] = order
            slots = [(hh, kb) for hh in order for kb in kbs]
            chunks = []
            for gi in range(2):
                base = gi * 2 * nkb
                for off in range(0, 2 * nkb, 4):
                    chunks.append(list(range(base + off, min(base + off + 4, base + 2 * nkb))))
            chunk_of = {}
            for ci, ch in enumerate(chunks):
                for sl in ch:
                    chunk_of[sl] = ci
            u["chunk_of"] = chunk_of
            SB = [0, 1, 2, 7]
            nch = len(chunks) // 2
            pair_emitters = []
            for cp in range(nch):
                def emit_pair(cp=cp):
                    pair = [(cp, chunks[cp]), (nch + cp, chunks[nch + cp])]
                    banks = []
                    for _ in pair:
                        banks.append(SB[sbank_ctr[0] % 4])
                        sbank_ctr[0] += 1
                    n = len(pair[0][1])
                    for j in range(n):
                        for pi, (ci, ch) in enumerate(pair):
                            b = banks[pi]
                            sl = ch[j]
                            hh, kb = slots[sl]
                            hd = 4 * g + hh
                            kblk = m16 - (nkb_full - 1) + kb
                            r0 = (hd % 2) * 64
                            c = hd // 2
                            if mixer == "A":
                                lhsT = KA[r0:r0 + 64, c, kblk * 128:(kblk + 1) * 128]
                                rhs = QA[r0:r0 + 64, c, qb * 128:(qb + 1) * 128]
                                rd = [R("ka", c, kblk), R("qa", c, qb)]
                            else:
                                kvh = hd // 4
                                lhsT = KB[r0:r0 + 64, kvh, kblk * 128:(kblk + 1) * 128]
                                rhs = QB[r0:r0 + 64, c, qb * 128:(qb + 1) * 128]
                                rd = [R("kb", kvh, kblk), R("qb", c, qb)]
                            S.op("pe", lambda b=b, j=j, lhsT=lhsT, rhs=rhs: pe.matmul(
                                ps[b][:, j * 128:(j + 1) * 128], lhsT=lhsT, rhs=rhs,
                                start=True, stop=True),
                                reads=rd, writes=[R("ps", b)], signal=(j == n - 1))
                    for pi, (ci, ch) in enumerate(pair):
                        b = banks[pi]
                        s0 = ch[0]
                        S.op("act", lambda b=b, n=n, s0=s0: act.activation(
                            out=P_[:, s0 * 128:(s0 + n) * 128], in_=ps[b][:, 0:n * 128], func=AF.Exp),
                            reads=[R("ps", b)], writes=[R("p", ui, ci)])
                pair_emitters.append(emit_pair)
            u["pairs"] = pair_emitters

        def stage1b(u):
            qb, m16, mixer, g = u["spec"]
            kbs, ui, nkb = u["kbs"], u["ui"], len(u["kbs"])
            order, chunk_of = u["order"], u["chunk_of"]
            P_ = Pbuf[ui]
            E = EA if mixer == "A" else EB
            nsl = 4 * nkb
            rr = [R("p", ui, c_) for c_ in sorted(set(chunk_of.values()))]
            S.op("dve", lambda: dve.tensor_tensor(
                out=P_[:, 0:nsl * 128].rearrange("p (h x) -> p h x", h=4),
                in0=P_[:, 0:nsl * 128].rearrange("p (h x) -> p h x", h=4),
                in1=E[:, 4 * g:4 * g + 4, kbs[0]:kbs[0] + nkb, :].rearrange("p h a b -> p h (a b)"),
                op=ALU.mult),
                reads=rr + [R("EA" if mixer == "A" else "EB")], writes=rr)

        def stage2_head(u, hh):
            qb, m16, mixer, g = u["spec"]
            nkb_full = 5 if mixer == "A" else 2
            kbs, ui, nkb = u["kbs"], u["ui"], len(u["kbs"])
            P_ = Pbuf[ui]
            ob = 3 + ui
            u["ob"] = ob
            hd = 4 * g + hh
            vh = hd if mixer == "A" else 8 + hd // 4
            for i, kb in enumerate(kbs):
                ti = u["order"].index(hh) * nkb + i
                kblk = m16 - (nkb_full - 1) + kb
                S.op("pe", lambda ti=ti, kblk=kblk, i=i: pe.matmul(
                    ps[ob][:, hh * 65:(hh + 1) * 65], lhsT=P_[:, ti * 128:(ti + 1) * 128],
                    rhs=V[:, kblk, vh, :], start=(i == 0), stop=(i == nkb - 1)),
                    reads=[R("p", ui, u["chunk_of"][ti]), R("va" if mixer == "A" else "vb", kblk),
                           R("vones")],
                    writes=[R("ps", ob)], signal=(i == nkb - 1))

        def stage2_norm(u):
            qb, m16, mixer, g = u["spec"]
            ui = u["ui"]
            ob = u["ob"]
            O3 = ps[ob][:, 0:260].rearrange("p (h e) -> p h e", e=65)
            if mixer == "A":
                S.op("dve", lambda: dve.reciprocal(out=rcp[ui][:, :].rearrange("p (h e) -> p h e", e=1),
                                                   in_=O3[:, :, 64:65]),
                     reads=[R("ps", ob)], writes=[R("rcp", ui)])
            else:
                S.op("dve", lambda: dve.tensor_tensor(
                    out=den[ui][:, :].rearrange("p (h e) -> p h e", e=1), in0=O3[:, :, 64:65],
                    in1=esink[:, 4 * g:4 * g + 4].rearrange("p (h e) -> p h e", e=1), op=ALU.add),
                    reads=[R("ps", ob), R("esink")], writes=[R("den", ui)])
                S.op("dve", lambda: dve.reciprocal(out=rcp[ui][:, :], in_=den[ui][:, :]),
                     reads=[R("den", ui)], writes=[R("rcp", ui)])
            S.op("dve", lambda: dve.tensor_tensor(
                out=ytok[ui][:, :].rearrange("p (h d) -> p h d", h=4), in0=O3[:, :, 0:64],
                in1=rcp[ui][:, :].rearrange("p (h e) -> p h e", e=1).broadcast_to([128, 4, 64]),
                op=ALU.mult),
                reads=[R("ps", ob), R("rcp", ui)], writes=[R("ytok", ui)])

        def stage3(u):
            qb, m16, mixer, g = u["spec"]
            ui = u["ui"]
            tb = 5 + ui
            Tb = ps[tb][:, 0:128].bitcast(BF16)
            for i in range(2):
                S.op("pe", lambda i=i: pe.transpose(
                    out=Tb[:, i * 128:(i + 1) * 128], in_=ytok[ui][:, i * 128:(i + 1) * 128],
                    identity=ident_bf[:, :]),
                    reads=[R("ytok", ui), R("ident_bf")], writes=[R("ps", tb)], signal=(i == 1))
            Q = QA if mixer == "A" else QB
            nm = "qa" if mixer == "A" else "qb"
            evac_copy(Q[:, 2 * g:2 * g + 2, qb * 128:(qb + 1) * 128],
                      Tb.rearrange("p (a b) -> p a b", a=2),
                      [R("ps", tb)], [R(nm, 2 * g, qb), R(nm, 2 * g + 1, qb)])

        us = [{"spec": sp_} for sp_ in units]
        n = len(us)
        for i in range(n + 2):
            if i < n:
                stage1(us[i])
                for pr in us[i]["pairs"]:
                    pr()
                stage1b(us[i])
            if 0 <= i - 1 < n:
                for hh in range(4):
                    stage2_head(us[i - 1], hh)
                stage2_norm(us[i - 1])
            if 0 <= i - 2 < n:
                stage3(us[i - 2])

    m3_ctr = [0]

    def m3():
        for G in range(4):
            slot = next_group()
            r = ring[slot]
            Wga = r[:, 0:2048].rearrange("p (k n) -> p k n", k=8)
            Wgb = r[:, 2048:4096].rearrange("p (k n) -> p k n", k=8)
            WA = r[:, 4096:5120].rearrange("p (k n) -> p k n", k=4)
            WB = r[:, 5120:6144].rearrange("p (k n) -> p k n", k=4)
            for mm in range(2):
                m = 2 * G + mm
                for s in range(2):
                    par = m3_ctr[0] % 2
                    m3_ctr[0] += 1
                    bga, bgb, bpa, bpb = [4 * par + i for i in range(4)]
                    for k in range(8):
                        for (W, b) in ((Wga, bga), (Wgb, bgb)):
                            S.op("pe", lambda W=W, b=b, k=k: pe.matmul(
                                ps[b][:, :], lhsT=W[:, k, mm * 128:(mm + 1) * 128],
                                rhs=xn[:, k, slab(s)], start=(k == 0), stop=(k == 7)),
                                reads=[R("ring", slot, 0), R("xn", k, s)], writes=[R("ps", b)],
                                signal=(k == 7))
                    for (W, b, Q, nm) in ((WA, bpa, QA, "qa"), (WB, bpb, QB, "qb")):
                        for c in range(4):
                            S.op("pe", lambda W=W, b=b, c=c, Q=Q: pe.matmul(
                                ps[b][:, :], lhsT=W[:, c, mm * 128:(mm + 1) * 128],
                                rhs=Q[:, c, slab(s)], start=(c == 0), stop=(c == 3)),
                                reads=[R("ring", slot, 1)] + [R(nm, c, 4 * s + j) for j in range(4)],
                                writes=[R("ps", b)], signal=(c == 3))
                    sa, sb2 = sg[2 * par], sg[2 * par + 1]
                    S.op("act", lambda: act.activation(out=sa[:, :], in_=ps[bga][:, :],
                                                       func=AF.Sigmoid),
                         reads=[R("ps", bga)], writes=[R("sg", 2 * par)])
                    S.op("act", lambda: act.activation(out=sb2[:, :], in_=ps[bgb][:, :],
                                                       func=AF.Sigmoid),
                         reads=[R("ps", bgb)], writes=[R("sg", 2 * par + 1)])
                    S.op("dve", lambda: dve.tensor_tensor(out=t1[par][:, :], in0=ps[bpa][:, :],
                                                          in1=sa[:, :], op=ALU.mult),
                         reads=[R("ps", bpa), R("sg", 2 * par)], writes=[R("t1", par)])
                    S.op("dve", lambda: dve.tensor_tensor(out=t2[par][:, :], in0=ps[bpb][:, :],
                                                          in1=sb2[:, :], op=ALU.mult),
                         reads=[R("ps", bpb), R("sg", 2 * par + 1)], writes=[R("t2", par)])
                    S.op("pool", lambda m=m, s=s, par=par: pool.tensor_tensor(
                        out=merged[:, m, slab(s)], in0=t1[par][:, :], in1=t2[par][:, :], op=ALU.add),
                        reads=[R("t1", par), R("t2", par)],
                        writes=[R("mg", m, s), R("stg", m // 2)])
        oc = 0
        for H in range(2):
            slot = next_group()
            Wo = ring[slot][:, 0:4096].rearrange("p (k n) -> p k n", k=8)
            for mp in range(4):
                mo = 4 * H + mp
                for s in range(2):
                    b = oc % 4
                    oc += 1
                    for m in range(8):
                        S.op("pe", lambda m=m, b=b, mp=mp, s=s: pe.matmul(
                            ps[b][:, :], lhsT=Wo[:, m, mp * 128:(mp + 1) * 128],
                            rhs=merged[:, m, slab(s)], start=(m == 0), stop=(m == 7)),
                            reads=[R("ring", slot, 0), R("mg", m, s)], writes=[R("ps", b)],
                            signal=(m == 7))
                    S.op("dve", lambda b=b, mo=mo, s=s: dve.tensor_tensor(
                        out=h[:, mo, slab(s)], in0=ps[b][:, :], in1=h[:, mo, slab(s)], op=ALU.add),
                        reads=[R("ps", b), R("h", mo, s)], writes=[R("h", mo, s)])

    def p_items(tok0):
        def dma_p(t):
            st = pstg[t % 2]
            S.dma("sp", [lambda: sp.dma_start(
                out=st[:, :], in_=p_d[tok0 + t * 128:tok0 + (t + 1) * 128, :])],
                "pstg%d" % (t % 2), writes=[R("pstg", t % 2)])

        def xp(t):
            st = pstg[t % 2]
            b = 4 + t % 2
            for j in range(2):
                S.op("pe", lambda j=j: pe.transpose(
                    out=ps[b][:, j * 128:(j + 1) * 128], in_=st[:, j * 128:(j + 1) * 128],
                    identity=ident[:, :]),
                    reads=[R("pstg", t % 2), R("ident")], writes=[R("ps", b)], signal=(j == 1))
            evac_copy(pT[:, 0:2, t * 128:(t + 1) * 128],
                      ps[b][:, 0:256].rearrange("p (a b) -> p a b", a=2),
                      [R("ps", b)], [R("pT", t // 4)])

        def mk(i):
            def it():
                if i >= 1:
                    xp(i - 1)
                if i < 8:
                    dma_p(i)
            return it
        return [mk(i) for i in range(9)]

    def ple(tok0):
        oc = 0
        for H in range(2):
            slot = next_group()
            r = ring[slot]
            Wpg = r[:, 0:4096].rearrange("p (k n) -> p k n", k=8)
            Wpe = r[:, 4096:5120].rearrange("p (k n) -> p k n", k=2)
            for mp in range(4):
                mo = 4 * H + mp
                for s in range(2):
                    par = oc % 2
                    oc += 1
                    bg, bp = par, 2 + par
                    for k in range(8):
                        S.op("pe", lambda k=k, bg=bg, mp=mp, s=s: pe.matmul(
                            ps[bg][:, :], lhsT=Wpg[:, k, mp * 128:(mp + 1) * 128],
                            rhs=xn[:, k, slab(s)], start=(k == 0), stop=(k == 7)),
                            reads=[R("ring", slot, 0), R("xn", k, s)], writes=[R("ps", bg)],
                            signal=(k == 7))
                    for k in range(2):
                        S.op("pe", lambda k=k, bp=bp, mp=mp, s=s: pe.matmul(
                            ps[bp][:, :], lhsT=Wpe[:, k, mp * 128:(mp + 1) * 128],
                            rhs=pT[:, k, slab(s)], start=(k == 0), stop=(k == 1)),
                            reads=[R("ring", slot, 1), R("pT", s)], writes=[R("ps", bp)],
                            signal=(k == 1))
                    S.op("act", lambda bg=bg, par=par: act.activation(
                        out=t2[par][:, :], in_=ps[bg][:, :], func=AF.Sigmoid),
                        reads=[R("ps", bg)], writes=[R("t2", par)])
                    S.op("dve", lambda bp=bp, par=par: dve.tensor_tensor(
                        out=t1[par][:, :], in0=ps[bp][:, :], in1=t2[par][:, :], op=ALU.mult),
                        reads=[R("ps", bp), R("t2", par)], writes=[R("t1", par)])
                    S.op("pool", lambda mo=mo, s=s, par=par: pool.tensor_tensor(
                        out=h[:, mo, slab(s)], in0=h[:, mo, slab(s)], in1=t1[par][:, :], op=ALU.add),
                        reads=[R("t1", par), R("h", mo, s)], writes=[R("h", mo, s)])

    def store_out(tok0):
        for t in range(8):
            oi = t % NOST
            st = ostg[oi]
            for hf in range(2):
                b = (2 * t + hf) % 8
                for j in range(4):
                    k = hf * 4 + j
                    S.op("pe", lambda b=b, j=j, k=k, t=t: pe.transpose(
                        out=ps[b][:, j * 128:(j + 1) * 128], in_=h[:, k, t * 128:(t + 1) * 128],
                        identity=ident[:, :]),
                        reads=[R("h", k, t // 4), R("ident")], writes=[R("ps", b)],
                        signal=(j == 3))
                evac_copy(st[:, hf * 512:(hf + 1) * 512], ps[b][:, :],
                          [R("ps", b)], ostg_res(oi))
            S.dma("sp", [lambda t=t, st=st: sp.dma_start(
                out=out_d[tok0 + t * 128:tok0 + (t + 1) * 128, :], in_=st)],
                "ostg%d" % oi, reads=ostg_res(oi))

    def dump_dbg(what):
        allr = list(S.res.values())
        if what in ("m1", "m2"):
            return dump_h()
        if what in ("m1q", "m2y"):
            fns = [lambda: pool.dma_start(out=dbg_d[:, 0:4, :], in_=QA[:, :, :]),
                   lambda: pool.dma_start(out=dbg_d[:, 4:8, :], in_=QB[:, :, :])]
        elif what == "m1k":
            fns = [lambda: pool.dma_start(out=dbg_d[:, 0:4, :], in_=KA[:, :, 0:PASS]),
                   lambda: pool.dma_start(out=dbg_d[:, 4:6, :], in_=KB[:, :, 0:PASS])]
        elif what == "m1v":
            fns = [lambda: pool.dma_start(
                out=dbg_d[:, 0:6, :].rearrange("p a b -> p (a b)")[:, 0:5200].rearrange(
                    "p (a b) -> p a b", a=8),
                in_=V[:, 0:8].rearrange("p a b c -> p a (b c)"))]
        S.dma("pool", fns, "dbg", reads=allr)
        S.wait_sem("pool", "dbg")

    def dump_h():
        S.dma("sp", [lambda: sp.dma_start(out=dbg_d, in_=h[:, :, :])], "dbg",
              reads=[R("h", k, s) for k in range(8) for s in range(2)])
        S.wait_sem("sp", "dbg")

    def tok_of(pi):
        return (pi // 2) * SEQ + (pi % 2) * PASS

    for t in range(8):
        issue_x(tok_of(0), t)
    for ps_i in range(npass):
        seq, half = ps_i // 2, ps_i % 2
        tok0 = tok_of(ps_i)
        load_x(tok0)
        if stop == "load":
            dump_h(); break
        norm(0)
        ffn(0, extras=build_E_items() if ps_i == 0 else ())
        if stop == "ffn1":
            dump_h(); break
        norm(1)
        m1(half)
        if stop in ("m1", "m1q", "m1k", "m1v"):
            dump_dbg(stop); break
        m2(half)
        if stop in ("m2", "m2y"):
            dump_dbg(stop); break
        m3()
        if ps_i + 1 < npass:
            for t in range(4):
                issue_x(tok_of(ps_i + 1), t)
        if stop == "mix":
            dump_h(); break
        norm(2)
        ffn(1, extras=p_items(tok0))
        if stop == "ffn2":
            dump_h(); break
        norm(3)
        ple(tok0)
        if stop == "ple":
            dump_h(); break
        if ps_i + 1 < npass:
            for t in range(4, 8):
                issue_x(tok_of(ps_i + 1), t)
        store_out(tok0)
    for i in range(NOST):
        S.wait_sem("sp", "ostg%d" % i)
    return nc


def _host_consts():
    kl = np.arange(128)[:, None, None]
    kb = np.arange(5)[None, :, None]
    ql = np.arange(128)[None, None, :]
    rel = ql + 512 - 128 * kb - kl
    idxA = np.clip(rel, -128, 128) + 128
    jc = (128 * kb + kl) // 64
    qc = ql // 64
    validA = (jc >= qc) & (jc <= qc + 8)
    maskA = np.where(validA, 0.0, NEG).astype(np.float32)
    maskA = np.broadcast_to(maskA[:, None], (128, 8, 5, 128)).copy()
    kb2 = np.arange(2)[None, :, None]
    relB = (128 + ql) - (128 * kb2 + kl)
    jcB = (128 * kb2 + kl) // 64
    validB = (jcB >= qc) & (jcB <= qc + 2)
    slopes = np.array([2.0 ** (-8.0 * (hh + 1) / 8) for hh in range(8)], dtype=np.float32)
    biasB = -slopes[None, :, None, None] * np.abs(relB).astype(np.float32)[:, None]
    biasB = np.where(validB[:, None], biasB, NEG).astype(np.float32)
    return idxA, maskA, np.ascontiguousarray(biasB)


def make_in_maps(inputs):
    f = lambda a: np.ascontiguousarray(np.asarray(a, dtype=np.float32))
    x = f(inputs["x"])
    p = f(inputs["p"])[0]
    idxA, maskA, biasB = _host_consts()
    arb = f(inputs["a_rel_bias"])[0]
    biasA = np.ascontiguousarray(np.transpose(arb[:, idxA], (1, 0, 2, 3)))
    gains = np.stack([f(inputs[n])[0].reshape(8, 128).T for n in
                      ("ffn1_norm", "mix_norm", "ffn2_norm", "ple_norm")], axis=1)
    gains = np.ascontiguousarray(gains.reshape(128, 32))
    gqk = np.stack([np.tile(f(inputs[n])[0], 2) for n in
                    ("a_q_norm", "a_k_norm", "b_q_norm", "b_k_norm")], axis=1)
    gqk = np.ascontiguousarray(gqk)
    sinks = np.ascontiguousarray(np.broadcast_to(f(inputs["b_sinks"])[0][None, :], (128, 8)))
    shared = {
        "ffn1_w_gu": f(inputs["ffn1_w_gu"])[0], "ffn2_w_gu": f(inputs["ffn2_w_gu"])[0],
        "ffn1_w_down": f(inputs["ffn1_w_down"])[0], "ffn2_w_down": f(inputs["ffn2_w_down"])[0],
        "w_in": f(inputs["w_in"])[0], "w_gate": f(inputs["w_gate"])[0],
        "w_proj_a": f(inputs["w_proj_a"])[0], "w_proj_b": f(inputs["w_proj_b"])[0],
        "w_out": f(inputs["w_out"])[0], "w_ple_gate": f(inputs["w_ple_gate"])[0],
        "w_ple_proj": f(inputs["w_ple_proj"])[0],
        "gains": gains, "gqk": gqk, "sinks": sinks, "ident": np.eye(128, dtype=np.float32),
        "biasA": biasA, "maskA": maskA, "biasB": biasB,
    }
    in_maps = []
    for c in range(N_CORES):
        m = dict(shared)
        m["x"] = np.ascontiguousarray(x[2 * c:2 * c + 2].reshape(TOK_CORE, D))
        m["p"] = np.ascontiguousarray(p[2 * c:2 * c + 2].reshape(TOK_CORE, 256))
        in_maps.append(m)
    return in_maps


def kernel(**inputs):
    nc = build_nc()
    in_maps = make_in_maps(inputs)
    res = run_bass_kernel_spmd(nc, in_maps, core_ids=list(range(N_CORES)))
    out = np.stack([np.asarray(r["out"]).reshape(2, SEQ, D) for r in res.results], axis=0)
    return out.reshape(16, SEQ, D).astype(np.float32)
```

```python
import numpy as np
import concourse.bass as bass
import concourse.mybir as mybir
from concourse.bass_utils import run_bass_kernel_spmd

F32 = mybir.dt.float32
BF16 = mybir.dt.bfloat16
AF = mybir.ActivationFunctionType
ALU = mybir.AluOpType

N_CORES = 8
D = 1024
KC = 8
DFF = 2816
SEQ = 2048
PASS = 1024
SLAB = 512
TOK_CORE = 4096
EPS = 1e-6
NEG = -30000.0
STRICT = True


class Res:
    __slots__ = ("w", "rs")

    def __init__(self):
        self.w = None
        self.rs = {}


class Sched:
    def __init__(self, nc):
        self.nc = nc
        self.eng = {"pe": nc.tensor, "act": nc.scalar, "dve": nc.vector,
                    "pool": nc.gpsimd, "sp": nc.sync}
        self.sems = {}
        self.cnt = {}
        self.seen = {e: {} for e in self.eng}
        self.res = {}
        for e in ("pe", "act", "dve", "pool"):
            self.newsem(e)

    def newsem(self, key):
        if key not in self.sems:
            self.sems[key] = self.nc.alloc_semaphore("s_" + key)
            self.cnt[key] = 0

    def R(self, *key):
        r = self.res.get(key)
        if r is None:
            r = self.res[key] = Res()
        return r

    def _waits(self, eng, reads, writes):
        waits = {}

        def need(st):
            if st is None:
                return
            k, v = st
            if k == eng and (eng == "pe" or not STRICT):
                return
            if self.seen[eng].get(k, 0) >= v:
                return
            if waits.get(k, 0) < v:
                waits[k] = v

        for r in reads:
            need(r.w)
        for r in writes:
            need(r.w)
            for k, v in r.rs.items():
                need((k, v))
        for k, v in waits.items():
            self.eng[eng].wait_ge(self.sems[k], v)
            self.seen[eng][k] = v

    def _stamp(self, st, reads, writes):
        k, v = st
        for r in reads:
            if r.rs.get(k, 0) < v:
                r.rs[k] = v
        for r in writes:
            r.w = st
            r.rs = {}

    def op(self, eng, fn, reads=(), writes=(), signal=True):
        self._waits(eng, reads, writes)
        inst = fn()
        if eng == "pe" and not signal:
            st = ("pe", self.cnt["pe"] + 1)
        else:
            self.cnt[eng] += 1
            inst.then_inc(self.sems[eng], 1)
            st = (eng, self.cnt[eng])
        self._stamp(st, reads, writes)

    def dma(self, eng, fns, dsem, reads=(), writes=()):
        self.newsem(dsem)
        self._waits(eng, reads, writes)
        for fn in fns:
            inst = fn()
            self.cnt[dsem] += 16
            inst.then_inc(self.sems[dsem], 16)
        self._stamp((dsem, self.cnt[dsem]), reads, writes)

    def wait_sem(self, eng, key):
        if key in self.sems and self.cnt[key] > 0:
            self.eng[eng].wait_ge(self.sems[key], self.cnt[key])


def build_nc(stop=None, npass=4):
    nc = bass.Bass("TRN2", target_bir_lowering=False)
    S = Sched(nc)
    R = S.R

    def din(name, shape):
        return nc.dram_tensor(name, list(shape), F32, kind="ExternalInput").ap()

    x_d = din("x", [TOK_CORE, D])
    p_d = din("p", [TOK_CORE, 256])
    wgu_d = [din("ffn1_w_gu", [D, 2 * DFF]), din("ffn2_w_gu", [D, 2 * DFF])]
    wdn_d = [din("ffn1_w_down", [DFF, D]), din("ffn2_w_down", [DFF, D])]
    win_d = din("w_in", [D, 2304])
    wgate_d = din("w_gate", [D, 2 * D])
    wpa_d = din("w_proj_a", [512, D])
    wpb_d = din("w_proj_b", [512, D])
    wout_d = din("w_out", [D, D])
    wpg_d = din("w_ple_gate", [D, D])
    wpe_d = din("w_ple_proj", [256, D])
    gains_d = din("gains", [128, 32])
    gqk_d = din("gqk", [128, 4])
    sinks_d = din("sinks", [128, 8])
    ident_d = din("ident", [128, 128])
    biasA_d = din("biasA", [128, 8, 5, 128])
    maskA_d = din("maskA", [128, 8, 5, 128])
    biasB_d = din("biasB", [128, 8, 2, 128])
    out_d = nc.dram_tensor("out", [TOK_CORE, D], F32, kind="ExternalOutput").ap()
    dbg_d = None
    if stop is not None:
        dbg_d = nc.dram_tensor("dbg", [128, 8, PASS], F32, kind="ExternalOutput").ap()

    def sb(name, shape, dt):
        return nc.alloc_sbuf_tensor("sb_" + name, list(shape), dt)

    h = sb("h", [128, 8, PASS], F32)
    xn = sb("xn", [128, 8, PASS], BF16)
    QA = sb("QA", [128, 4, PASS], BF16)
    QB = sb("QB", [128, 4, PASS], BF16)
    KA = sb("KA", [128, 4, SEQ], BF16)
    KB = sb("KB", [128, 2, SEQ], BF16)
    V = sb("V", [128, 16, 10, 65], BF16)
    EA = sb("EA", [128, 8, 5, 128], BF16)
    EB = sb("EB", [128, 8, 2, 128], BF16)
    ring = [sb("ring%d" % i, [128, 6144], BF16) for i in range(2)]
    actb = [sb("act%d" % i, [128, 2, 512], BF16) for i in range(2)]
    Pbuf = [sb("P%d" % i, [128, 20 * 128], BF16) for i in range(2)]
    merged = sb("merged", [128, 8, PASS], BF16)
    stg = [merged[:, 2 * i:2 * i + 2, :].bitcast(F32) for i in range(4)]
    stg = [s.rearrange("p a b -> p (a b)") for s in stg]
    ostg = [Pbuf[i][:, 0:2048].bitcast(F32) for i in range(2)]
    for Q_ in (QA, QB):
        for i in range(2):
            ostg.append(Q_[:, 2 * i:2 * i + 2, :].bitcast(F32).rearrange("p a b -> p (a b)"))
    ptmp = [Pbuf[i][:, 0:2560].bitcast(F32) for i in range(2)]

    def stg_res(i):
        return [R("stg", i)] + [R("mg", 2 * i + a, s_) for a in range(2) for s_ in range(2)]

    def pbuf_res(i):
        return [R("p", i, c) for c in range(6)]

    def ostg_res(i):
        if i < 2:
            return pbuf_res(i)
        if i >= 6:
            nm = "t1" if i == 6 else "t2"
            return [R(nm, 0), R(nm, 1)]
        nm = "qa" if i < 4 else "qb"
        c0 = 2 * (i % 2)
        return [R(nm, c0 + a, qb_) for a in range(2) for qb_ in range(8)]
    sq = [sb("sq%d" % i, [128, 512], BF16) for i in range(4)]
    lnv = sb("lnv", [128, 512], F32)
    rstd = [sb("rstd%d" % i, [128, 512], F32) for i in range(2)]
    sg = [sb("sg%d" % i, [128, 512], BF16) for i in range(4)]
    t1all = sb("t1all", [128, 2, 512], F32)
    t2all = sb("t2all", [128, 2, 512], F32)
    t1 = [t1all[:, i, :] for i in range(2)]
    t2 = [t2all[:, i, :] for i in range(2)]
    ostg.append(t1all[:, :, :].rearrange("p a b -> p (a b)"))
    ostg.append(t2all[:, :, :].rearrange("p a b -> p (a b)"))
    NOST = len(ostg)
    pT = sb("pT", [128, 2, PASS], BF16)
    pstg = [sb("pstg%d" % i, [128, 256], F32) for i in range(2)]
    ytok = [sb("ytok%d" % i, [128, 256], BF16) for i in range(2)]
    ident = sb("ident", [128, 128], F32)
    ident_bf = sb("ident_bf", [128, 128], BF16)
    ones_bf = sb("ones_bf", [128, 128], BF16)
    bones = sb("bones", [128, 128], BF16)
    gains = sb("gains", [128, 32], F32)
    gqk = sb("gqk", [128, 4], F32)
    esink = sb("esink", [128, 8], F32)
    den = [sb("den%d" % i, [128, 4], F32) for i in range(2)]
    rcp = [sb("rcp%d" % i, [128, 4], F32) for i in range(2)]
    warm = sb("warm", [128, 2], F32)
    ps_all = nc.alloc_psum_tensor("ps_all", [128, 8, 512], F32)
    ps = [ps_all[:, i, :] for i in range(8)]

    pe, act, dve, pool, sp = nc.tensor, nc.scalar, nc.vector, nc.gpsimd, nc.sync

    S.dma("sp", [
        lambda: sp.dma_start(out=ident[:, :], in_=ident_d),
        lambda: sp.dma_start(out=gains[:, :], in_=gains_d),
        lambda: sp.dma_start(out=gqk[:, :], in_=gqk_d),
        lambda: sp.dma_start(out=esink[:, :], in_=sinks_d),
    ], "setup", writes=[R("ident"), R("gains"), R("gqk"), R("esink")])
    S.op("dve", lambda: dve.tensor_copy(out=ident_bf[:, :], in_=ident[:, :]),
         reads=[R("ident")], writes=[R("ident_bf")])
    S.op("dve", lambda: dve.memset(ones_bf[:, :], 1.0), writes=[R("ones")])
    S.op("dve", lambda: dve.memset(warm[:, :], 1.0), writes=[R("warm")])
    S.op("dve", lambda: dve.memset(bones[:, :], 0.0), writes=[R("bones")])
    S.op("dve", lambda: dve.memset(bones[0:64, 0:64], 1.0), writes=[R("bones")])
    S.op("dve", lambda: dve.memset(bones[64:128, 64:128], 1.0), writes=[R("bones")])
    S.op("dve", lambda: dve.memset(V[:, :, :, 64:65], 1.0), writes=[R("vones")])
    S.op("dve", lambda: dve.tensor_scalar(out=gqk[:, 0:1], in0=gqk[:, 0:1], scalar1=0.125,
                                          scalar2=None, op0=ALU.mult),
         reads=[R("gqk")], writes=[R("gqk")])
    S.op("dve", lambda: dve.tensor_scalar(out=gqk[:, 2:3], in0=gqk[:, 2:3], scalar1=0.125,
                                          scalar2=None, op0=ALU.mult),
         reads=[R("gqk")], writes=[R("gqk")])
    S.op("act", lambda: act.activation(out=esink[:, :], in_=esink[:, :], func=AF.Exp),
         reads=[R("esink")], writes=[R("esink")])
    def epos(hd):
        return 4 * (hd // 4) + [0, 2, 1, 3].index(hd % 4)

    def build_E_items():
        items = []
        for hd in range(8):
            def it(hd=hd):
                i = hd % 2
                a = ptmp[i][:, 0:640]
                b = ptmp[i][:, 640:1280]
                S.dma("sp", [
                    lambda: sp.dma_start(out=a, in_=biasA_d[:, hd].rearrange("p a b -> p (a b)")),
                    lambda: sp.dma_start(out=b, in_=maskA_d[:, hd].rearrange("p a b -> p (a b)")),
                ], "setupE%d" % i, writes=pbuf_res(i))
                S.op("dve", lambda: dve.tensor_tensor(out=a, in0=a, in1=b, op=ALU.add),
                     reads=pbuf_res(i), writes=pbuf_res(i))
                S.op("act", lambda: act.activation(
                    out=EA[:, epos(hd)].rearrange("p a b -> p (a b)"), in_=a, func=AF.Exp),
                    reads=pbuf_res(i), writes=[R("EA")])
            items.append(it)
        for hp in range(4):
            def it(hp=hp):
                i = hp % 2
                a = ptmp[i][:, 0:512]
                S.dma("sp", [
                    lambda: sp.dma_start(
                        out=a, in_=biasB_d[:, 2 * hp:2 * hp + 2].rearrange("p h a b -> p (h a b)")),
                ], "setupE%d" % i, writes=pbuf_res(i))
                for q_ in range(2):
                    S.op("act", lambda q_=q_: act.activation(
                        out=EB[:, epos(2 * hp + q_)].rearrange("p a b -> p (a b)"),
                        in_=a[:, q_ * 256:(q_ + 1) * 256], func=AF.Exp),
                        reads=pbuf_res(i), writes=[R("EB")])
            items.append(it)
        return items

    def wv(dram, p=128):
        return dram.rearrange("(kc p) n -> p kc n", p=p)

    def ffn_group(w, g):
        def pieces(slot):
            r = ring[slot]
            return [
                (r[:, 0:2048].rearrange("p (k n) -> p k n", k=8),
                 wv(wgu_d[w])[:, :, 256 * g:256 * g + 256], 0),
                (r[:, 2048:4096].rearrange("p (k n) -> p k n", k=8),
                 wv(wgu_d[w])[:, :, DFF + 256 * g:DFF + 256 * g + 256], 0),
                (r[:, 4096:6144].rearrange("p (k n) -> p k n", k=2),
                 wv(wdn_d[w])[:, 2 * g:2 * g + 2, :], 1),
            ]
        return pieces

    def cols_group(dram, c0, n, kc=8):
        def pieces(slot):
            r = ring[slot]
            return [(r[:, 0:kc * n].rearrange("p (k n) -> p k n", k=kc),
                     wv(dram)[:, :, c0:c0 + n], 0)]
        return pieces

    def kbvb_group():
        def pieces(slot):
            r = ring[slot]
            kd = r[:, 0:2048].rearrange("p (k n) -> p k n", k=8)
            out = []
            for kvh in range(2):
                for dup in range(2):
                    out.append((kd[:, :, kvh * 128 + dup * 64:kvh * 128 + dup * 64 + 64],
                                wv(win_d)[:, :, 2048 + kvh * 64:2048 + kvh * 64 + 64], 0))
            out.append((r[:, 2048:3072].rearrange("p (k n) -> p k n", k=8),
                        wv(win_d)[:, :, 2176:2304], 0))
            return out
        return pieces

    def m3_group(G):
        def pieces(slot):
            r = ring[slot]
            return [
                (r[:, 0:2048].rearrange("p (k n) -> p k n", k=8),
                 wv(wgate_d)[:, :, 256 * G:256 * G + 256], 0),
                (r[:, 2048:4096].rearrange("p (k n) -> p k n", k=8),
                 wv(wgate_d)[:, :, D + 256 * G:D + 256 * G + 256], 0),
                (r[:, 4096:5120].rearrange("p (k n) -> p k n", k=4),
                 wv(wpa_d)[:, :, 256 * G:256 * G + 256], 1),
                (r[:, 5120:6144].rearrange("p (k n) -> p k n", k=4),
                 wv(wpb_d)[:, :, 256 * G:256 * G + 256], 1),
            ]
        return pieces

    def ple_group(H):
        def pieces(slot):
            r = ring[slot]
            return [
                (r[:, 0:4096].rearrange("p (k n) -> p k n", k=8),
                 wv(wpg_d)[:, :, 512 * H:512 * H + 512], 0),
                (r[:, 4096:5120].rearrange("p (k n) -> p k n", k=2),
                 wv(wpe_d)[:, :, 512 * H:512 * H + 512], 1),
            ]
        return pieces

    pass_groups = ([ffn_group(0, g) for g in range(11)]
                   + [cols_group(win_d, 0, 512), cols_group(win_d, 512, 512),
                      cols_group(win_d, 1536, 512), kbvb_group(),
                      cols_group(win_d, 1024, 512)]
                   + [m3_group(G) for G in range(4)]
                   + [cols_group(wout_d, 0, 512), cols_group(wout_d, 512, 512)]
                   + [ffn_group(1, g) for g in range(11)]
                   + [ple_group(0), ple_group(1)])
    NG = len(pass_groups)
    all_groups = pass_groups * npass
    gstate = {"issued": [0, 0], "cur": -1}

    def issue_part(part):
        gi = gstate["issued"][part]
        if gi >= len(all_groups):
            return
        slot = gi % 2
        pcs = [(o, i) for (o, i, pt) in all_groups[gi](slot) if pt == part]
        if pcs:
            S.dma("pool", [(lambda o=o, i=i: pool.dma_start(out=o, in_=i)) for o, i in pcs],
                  "ring%d_%d" % (slot, part), writes=[R("ring", slot, part)])
        gstate["issued"][part] += 1

    def next_group(pf=True):
        gstate["cur"] += 1
        gi = gstate["cur"]
        for part in range(2):
            while gstate["issued"][part] <= gi:
                issue_part(part)
        if pf:
            prefetch(0)
            prefetch(1)
        return gi % 2

    def prefetch(part):
        if gstate["issued"][part] <= gstate["cur"] + 1:
            issue_part(part)

    def slab(s):
        return slice(s * SLAB, (s + 1) * SLAB)

    cp_rr = [0]

    def evac_copy(out, in_, reads, writes, force=None):
        if force is None:
            cp_rr[0] ^= 1
        if (force == "act") or (force is None and cp_rr[0]):
            S.op("act", lambda: act.activation(out=out, in_=in_, func=AF.Copy),
                 reads=reads, writes=writes)
        else:
            S.op("dve", lambda: dve.tensor_copy(out=out, in_=in_), reads=reads, writes=writes)

    xn_f32 = xn[:, :, :].bitcast(F32).rearrange("p a b -> p (a b)")

    def xstage(t):
        if t < 4:
            return stg[t], stg_res(t), "stg%d" % t
        j = t - 4
        return (xn_f32[:, j * 1024:(j + 1) * 1024],
                [R("xn", 2 * j + a, s_) for a in range(2) for s_ in range(2)], "stgB%d" % j)

    def issue_x(tok0, t):
        ap, res, sem = xstage(t)
        S.dma("sp", [lambda: sp.dma_start(
            out=ap, in_=x_d[tok0 + t * 128:tok0 + (t + 1) * 128, :])],
            sem, writes=res)

    def load_x(tok0):
        for t in range(8):
            st, res, _ = xstage(t)
            for hf in range(2):
                b = (2 * t + hf) % 8
                for j in range(4):
                    k = hf * 4 + j
                    S.op("pe", lambda b=b, j=j, k=k, st=st: pe.transpose(
                        out=ps[b][:, j * 128:(j + 1) * 128], in_=st[:, k * 128:(k + 1) * 128],
                        identity=ident[:, :]),
                        reads=res + [R("ident")], writes=[R("ps", b)], signal=(j == 3))
                evac_copy(h[:, hf * 4:hf * 4 + 4, t * 128:(t + 1) * 128],
                          ps[b][:, :].rearrange("p (a b) -> p a b", a=4),
                          [R("ps", b)], [R("h", hf * 4 + j, t // 4) for j in range(4)])

    def norm(gidx):
        S.op("act", lambda: act.activation(out=warm[:, 1:2], in_=warm[:, 0:1], func=AF.Ln),
             reads=[R("warm")], writes=[R("warm_o")])
        for s in range(2):
            nb = 6 + s
            for k in range(8):
                q = sq[k % 4]
                S.op("act", lambda k=k, q=q: act.activation(out=q[:, :], in_=h[:, k, slab(s)],
                                                            func=AF.Square),
                     reads=[R("h", k, s)], writes=[R("sq", k % 4)])
                S.op("pe", lambda k=k, q=q: pe.matmul(ps[nb][:, :], lhsT=ones_bf[:, :], rhs=q[:, :],
                                                      start=(k == 0), stop=(k == 7)),
                     reads=[R("sq", k % 4), R("ones")], writes=[R("ps", nb)], signal=True)
            S.op("act", lambda: act.activation(out=lnv[:, :], in_=ps[nb][:, :], func=AF.Ln,
                                               bias=EPS, scale=1.0 / D),
                 reads=[R("ps", nb)], writes=[R("lnv")])
            S.op("act", lambda: act.activation(out=ps[nb][:, :], in_=lnv[:, :], func=AF.Exp,
                                               scale=-0.5),
                 reads=[R("lnv")], writes=[R("ps", nb)])
            for k in range(8):
                S.op("dve", lambda k=k: dve.scalar_tensor_tensor(
                    out=xn[:, k, slab(s)], in0=h[:, k, slab(s)],
                    scalar=gains[:, gidx * 8 + k:gidx * 8 + k + 1], in1=ps[nb][:, :],
                    op0=ALU.mult, op1=ALU.mult),
                    reads=[R("h", k, s), R("ps", nb), R("gains")], writes=[R("xn", k, s)])

    step_ctr = [0]

    def ffn(w, extras=()):
        extras = list(extras)
        prev = None
        for g in range(11):
            slot = next_group(pf=False)
            prefetch(0)
            r = ring[slot]
            Wg = r[:, 0:2048].rearrange("p (k n) -> p k n", k=8)
            Wu = r[:, 2048:4096].rearrange("p (k n) -> p k n", k=8)
            Wd = r[:, 4096:6144].rearrange("p (k n) -> p k n", k=2)
            for s in range(2):
                ab = step_ctr[0] % 2
                step_ctr[0] += 1
                mmlist = []
                for jj in range(2):
                    for k in range(8):
                        mmlist.append((Wg, jj, jj, k))
                        mmlist.append((Wu, jj, 2 + jj, k))
                for qi in range(4):
                    for (W, jj, b, k) in mmlist[8 * qi:8 * qi + 8]:
                        S.op("pe", lambda W=W, b=b, k=k, jj=jj: pe.matmul(
                            ps[b][:, :], lhsT=W[:, k, jj * 128:(jj + 1) * 128],
                            rhs=xn[:, k, slab(s)], start=(k == 0), stop=(k == 7)),
                            reads=[R("ring", slot, 0), R("xn", k, s)], writes=[R("ps", b)],
                            signal=(k == 7))
                    jj = qi // 2
                    if qi in (1, 3):
                        sgi = 2 * ab + jj
                        S.op("act", lambda jj=jj, sgi=sgi: act.activation(
                            out=sg[sgi][:, :], in_=ps[jj][:, :], func=AF.Silu),
                            reads=[R("ps", jj)], writes=[R("sg", sgi)])
                        S.op("dve", lambda jj=jj, sgi=sgi, ab=ab: dve.tensor_tensor(
                            out=actb[ab][:, jj, :], in0=ps[2 + jj][:, :], in1=sg[sgi][:, :],
                            op=ALU.mult),
                            reads=[R("ps", 2 + jj), R("sg", sgi)], writes=[R("act", ab, jj)])
                    if prev is not None:
                        prev[2 * qi]()
                        prev[2 * qi + 1]()
                if s == 0:
                    prefetch(1)

                def mk_pair(m, s=s, ab=ab, Wd=Wd, slot=slot):
                    def pair():
                        b = 4 + m % 4
                        for jj in range(2):
                            S.op("pe", lambda jj=jj: pe.matmul(
                                ps[b][:, :], lhsT=Wd[:, jj, m * 128:(m + 1) * 128],
                                rhs=actb[ab][:, jj, :], start=(jj == 0), stop=(jj == 1)),
                                reads=[R("ring", slot, 1), R("act", ab, jj)], writes=[R("ps", b)],
                                signal=(jj == 1))
                        S.op("dve", lambda: dve.scalar_tensor_tensor(
                            out=h[:, m, slab(s)], in0=ps[b][:, :], scalar=0.5,
                            in1=h[:, m, slab(s)], op0=ALU.mult, op1=ALU.add),
                            reads=[R("ps", b), R("h", m, s)], writes=[R("h", m, s)])
                    return pair
                prev = [mk_pair(m) for m in range(8)]
                if s == 1 and extras:
                    extras.pop(0)()
        for pr in prev:
            pr()
        for ex in extras:
            ex()

    qk_ctr = [0]

    def qk_chunks(items):
        work = [(it, s) for it in items for s in range(2)]
        pend = None
        for (it, s) in work:
            lhsT_fn, gcol, dest_fn, dres_fn, slot = it
            i = qk_ctr[0]
            qk_ctr[0] += 1
            b = i % 4
            for k in range(8):
                S.op("pe", lambda k=k, b=b, lhsT_fn=lhsT_fn, s=s: pe.matmul(
                    ps[b][:, :], lhsT=lhsT_fn(k), rhs=xn[:, k, slab(s)],
                    start=(k == 0), stop=(k == 7)),
                    reads=[R("ring", slot, 0), R("xn", k, s)], writes=[R("ps", b)], signal=(k == 7))
            S.op("act", lambda b=b, i=i: act.activation(out=sq[i % 4][:, :], in_=ps[b][:, :],
                                                        func=AF.Square),
                 reads=[R("ps", b)], writes=[R("sq", i % 4)])
            if pend is not None:
                pend()

            def rest(i=i, b=b, gcol=gcol, dest_fn=dest_fn, dres_fn=dres_fn, s=s):
                q = sq[i % 4]
                sb_ = 4 + i % 2
                rs = rstd[i % 2]
                S.op("pe", lambda: pe.matmul(ps[sb_][:, :], lhsT=bones[:, :], rhs=q[:, :],
                                             start=True, stop=True),
                     reads=[R("sq", i % 4), R("bones")], writes=[R("ps", sb_)], signal=True)
                S.op("act", lambda: act.activation(out=lnv[:, :], in_=ps[sb_][:, :], func=AF.Ln,
                                                   bias=EPS, scale=1.0 / 64),
                     reads=[R("ps", sb_)], writes=[R("lnv")])
                S.op("act", lambda: act.activation(out=rs[:, :], in_=lnv[:, :], func=AF.Exp,
                                                   scale=-0.5),
                     reads=[R("lnv")], writes=[R("rstd", i % 2)])
                S.op("dve", lambda: dve.scalar_tensor_tensor(
                    out=dest_fn(s), in0=ps[b][:, :], scalar=gqk[:, gcol:gcol + 1], in1=rs[:, :],
                    op0=ALU.mult, op1=ALU.mult),
                    reads=[R("ps", b), R("rstd", i % 2), R("gqk")], writes=dres_fn(s))
            pend = rest
        pend()

    def m1(half):
        kb0 = half * 8
        t0 = half * PASS
        slot = next_group()
        W = ring[slot][:, 0:4096].rearrange("p (k n) -> p k n", k=8)
        qk_chunks([((lambda k, c=c, W=W: W[:, k, c * 128:(c + 1) * 128]), 0,
                    (lambda s, c=c: QA[:, c, slab(s)]),
                    (lambda s, c=c: [R("qa", c, 4 * s + j) for j in range(4)]), slot)
                   for c in range(4)])
        slot = next_group()
        W = ring[slot][:, 0:4096].rearrange("p (k n) -> p k n", k=8)
        qk_chunks([((lambda k, c=c, W=W: W[:, k, c * 128:(c + 1) * 128]), 1,
                    (lambda s, c=c: KA[:, c, t0 + s * SLAB:t0 + (s + 1) * SLAB]),
                    (lambda s, c=c: [R("ka", c, kb0 + 4 * s + j) for j in range(4)]), slot)
                   for c in range(4)])
        slot = next_group()
        W = ring[slot][:, 0:4096].rearrange("p (k n) -> p k n", k=8)
        qk_chunks([((lambda k, c=c, W=W: W[:, k, c * 128:(c + 1) * 128]), 2,
                    (lambda s, c=c: QB[:, c, slab(s)]),
                    (lambda s, c=c: [R("qb", c, 4 * s + j) for j in range(4)]), slot)
                   for c in range(4)])
        slot = next_group()
        W = ring[slot][:, 0:2048].rearrange("p (k n) -> p k n", k=8)
        Wvb = ring[slot][:, 2048:3072].rearrange("p (k n) -> p k n", k=8)
        qk_chunks([((lambda k, c=c, W=W: W[:, k, c * 128:(c + 1) * 128]), 3,
                    (lambda s, c=c: KB[:, c, t0 + s * SLAB:t0 + (s + 1) * SLAB]),
                    (lambda s, c=c: [R("kb", c, kb0 + 4 * s + j) for j in range(4)]), slot)
                   for c in range(2)])
        for t in range(8):
            b = 6 + t % 2
            for k in range(8):
                S.op("pe", lambda k=k, t=t: pe.matmul(
                    ps[b][:, 0:128], lhsT=xn[:, k, t * 128:(t + 1) * 128], rhs=Wvb[:, k, :],
                    start=(k == 0), stop=(k == 7)),
                    reads=[R("ring", slot, 0), R("xn", k, t // 4)], writes=[R("ps", b)],
                    signal=(k == 7))
            evac_copy(V[:, kb0 + t, 8:10, 0:64],
                      ps[b][:, 0:128].rearrange("p (a b) -> p a b", a=2),
                      [R("ps", b)], [R("vb", kb0 + t)])
        slot = next_group()
        Wva = ring[slot][:, 0:4096].rearrange("p (k n) -> p k n", k=8)
        for t in range(8):
            b = 6 + t % 2
            for k in range(8):
                S.op("pe", lambda k=k, t=t, b=b: pe.matmul(
                    ps[b][:, :], lhsT=xn[:, k, t * 128:(t + 1) * 128], rhs=Wva[:, k, :],
                    start=(k == 0), stop=(k == 7)),
                    reads=[R("ring", slot, 0), R("xn", k, t // 4)], writes=[R("ps", b)],
                    signal=(k == 7))
            evac_copy(V[:, kb0 + t, 0:8, 0:64],
                      ps[b][:, :].rearrange("p (a b) -> p a b", a=8),
                      [R("ps", b)], [R("va", kb0 + t)])

    sbank_ctr = [0]
    unit_ctr = [0]

    def m2(half):
        units = []
        for qb in range(8):
            m16 = half * 8 + qb
            for mixer in ("A", "B"):
                for g in range(2):
                    units.append((qb, m16, mixer, g))

        def stage1(u):
            qb, m16, mixer, g = u["spec"]
            nkb_full = 5 if mixer == "A" else 2
            kbs = [kb for kb in range(nkb_full) if m16 - (nkb_full - 1) + kb >= 0]
            nkb = len(kbs)
            u["kbs"] = kbs
            ui = unit_ctr[0] % 2
            unit_ctr[0] += 1
            u["ui"] = ui
            P_ = Pbuf[ui]
            order = [0, 2, 1, 3]
            u["order"] = order
            slots = [(hh, kb) for hh in order for kb in kbs]
            chunks = []
            for gi in range(2):
                base = gi * 2 * nkb
                for off in range(0, 2 * nkb, 4):
                    chunks.append(list(range(base + off, min(base + off + 4, base + 2 * nkb))))
            chunk_of = {}
            for ci, ch in enumerate(chunks):
                for sl in ch:
                    chunk_of[sl] = ci
            u["chunk_of"] = chunk_of
            SB = [0, 1, 2, 3]
            nch = len(chunks) // 2
            pair_emitters = []
            for cp in range(nch):
                def emit_pair(cp=cp):
                    pair = [(cp, chunks[cp]), (nch + cp, chunks[nch + cp])]
                    banks = []
                    for _ in pair:
                        banks.append(SB[sbank_ctr[0] % 4])
                        sbank_ctr[0] += 1
                    n = len(pair[0][1])
                    for j in range(n):
                        for pi, (ci, ch) in enumerate(pair):
                            b = banks[pi]
                            sl = ch[j]
                            hh, kb = slots[sl]
                            hd = 4 * g + hh
                            kblk = m16 - (nkb_full - 1) + kb
                            r0 = (hd % 2) * 64
                            c = hd // 2
                            if mixer == "A":
                                lhsT = KA[r0:r0 + 64, c, kblk * 128:(kblk + 1) * 128]
                                rhs = QA[r0:r0 + 64, c, qb * 128:(qb + 1) * 128]
                                rd = [R("ka", c, kblk), R("qa", c, qb)]
                            else:
                                kvh = hd // 4
                                lhsT = KB[r0:r0 + 64, kvh, kblk * 128:(kblk + 1) * 128]
                                rhs = QB[r0:r0 + 64, c, qb * 128:(qb + 1) * 128]
                                rd = [R("kb", kvh, kblk), R("qb", c, qb)]
                            S.op("pe", lambda b=b, j=j, lhsT=lhsT, rhs=rhs: pe.matmul(
                                ps[b][:, j * 128:(j + 1) * 128], lhsT=lhsT, rhs=rhs,
                                start=True, stop=True),
                                reads=rd, writes=[R("ps", b)], signal=(j == n - 1))
                    b0 = banks[0]
                    assert banks[1] == b0 + 1
                    s0 = pair[0][1][0]
                    gsz = 2 * nkb * 128
                    outv = P_[:, 0:2 * gsz].rearrange("p (g x) -> p g x", g=2)[:, :, s0 * 128:(s0 + n) * 128]
                    S.op("act", lambda: act.activation(
                        out=outv, in_=ps_all[:, b0:b0 + 2, 0:n * 128], func=AF.Exp),
                        reads=[R("ps", b0), R("ps", b0 + 1)],
                        writes=[R("p", ui, pair[0][0]), R("p", ui, pair[1][0])])
                pair_emitters.append(emit_pair)
            u["pairs"] = pair_emitters

        def stage1b(u):
            qb, m16, mixer, g = u["spec"]
            kbs, ui, nkb = u["kbs"], u["ui"], len(u["kbs"])
            order, chunk_of = u["order"], u["chunk_of"]
            P_ = Pbuf[ui]
            E = EA if mixer == "A" else EB
            nsl = 4 * nkb
            rr = [R("p", ui, c_) for c_ in sorted(set(chunk_of.values()))]
            S.op("dve", lambda: dve.tensor_tensor(
                out=P_[:, 0:nsl * 128].rearrange("p (h x) -> p h x", h=4),
                in0=P_[:, 0:nsl * 128].rearrange("p (h x) -> p h x", h=4),
                in1=E[:, 4 * g:4 * g + 4, kbs[0]:kbs[0] + nkb, :].rearrange("p h a b -> p h (a b)"),
                op=ALU.mult),
                reads=rr + [R("EA" if mixer == "A" else "EB")], writes=rr)

        def stage2_head(u, hh):
            qb, m16, mixer, g = u["spec"]
            nkb_full = 5 if mixer == "A" else 2
            kbs, ui, nkb = u["kbs"], u["ui"], len(u["kbs"])
            P_ = Pbuf[ui]
            ob = 4 + ui
            u["ob"] = ob
            hd = 4 * g + hh
            vh = hd if mixer == "A" else 8 + hd // 4
            for i, kb in enumerate(kbs):
                ti = u["order"].index(hh) * nkb + i
                kblk = m16 - (nkb_full - 1) + kb
                S.op("pe", lambda ti=ti, kblk=kblk, i=i: pe.matmul(
                    ps[ob][:, hh * 65:(hh + 1) * 65], lhsT=P_[:, ti * 128:(ti + 1) * 128],
                    rhs=V[:, kblk, vh, :], start=(i == 0), stop=(i == nkb - 1)),
                    reads=[R("p", ui, u["chunk_of"][ti]), R("va" if mixer == "A" else "vb", kblk),
                           R("vones")],
                    writes=[R("ps", ob)], signal=(i == nkb - 1))

        def stage2_norm(u):
            qb, m16, mixer, g = u["spec"]
            ui = u["ui"]
            ob = u["ob"]
            O3 = ps[ob][:, 0:260].rearrange("p (h e) -> p h e", e=65)
            if mixer == "A":
                S.op("dve", lambda: dve.reciprocal(out=rcp[ui][:, :].rearrange("p (h e) -> p h e", e=1),
                                                   in_=O3[:, :, 64:65]),
                     reads=[R("ps", ob)], writes=[R("rcp", ui)])
            else:
                S.op("dve", lambda: dve.tensor_tensor(
                    out=den[ui][:, :].rearrange("p (h e) -> p h e", e=1), in0=O3[:, :, 64:65],
                    in1=esink[:, 4 * g:4 * g + 4].rearrange("p (h e) -> p h e", e=1), op=ALU.add),
                    reads=[R("ps", ob), R("esink")], writes=[R("den", ui)])
                S.op("dve", lambda: dve.reciprocal(out=rcp[ui][:, :], in_=den[ui][:, :]),
                     reads=[R("den", ui)], writes=[R("rcp", ui)])
            S.op("dve", lambda: dve.tensor_tensor(
                out=ytok[ui][:, :].rearrange("p (h d) -> p h d", h=4), in0=O3[:, :, 0:64],
                in1=rcp[ui][:, :].rearrange("p (h e) -> p h e", e=1).broadcast_to([128, 4, 64]),
                op=ALU.mult),
                reads=[R("ps", ob), R("rcp", ui)], writes=[R("ytok", ui)])

        def stage3(u):
            qb, m16, mixer, g = u["spec"]
            ui = u["ui"]
            tb = 6 + ui
            Tb = ps[tb][:, 0:128].bitcast(BF16)
            for i in range(2):
                S.op("pe", lambda i=i: pe.transpose(
                    out=Tb[:, i * 128:(i + 1) * 128], in_=ytok[ui][:, i * 128:(i + 1) * 128],
                    identity=ident_bf[:, :]),
                    reads=[R("ytok", ui), R("ident_bf")], writes=[R("ps", tb)], signal=(i == 1))
            Q = QA if mixer == "A" else QB
            nm = "qa" if mixer == "A" else "qb"
            evac_copy(Q[:, 2 * g:2 * g + 2, qb * 128:(qb + 1) * 128],
                      Tb.rearrange("p (a b) -> p a b", a=2),
                      [R("ps", tb)], [R(nm, 2 * g, qb), R(nm, 2 * g + 1, qb)], force="dve")

        us = [{"spec": sp_} for sp_ in units]
        n = len(us)
        for i in range(n + 2):
            if i < n:
                stage1(us[i])
                for pr in us[i]["pairs"]:
                    pr()
                stage1b(us[i])
            if 0 <= i - 1 < n:
                for hh in range(4):
                    stage2_head(us[i - 1], hh)
                stage2_norm(us[i - 1])
            if 0 <= i - 2 < n:
                stage3(us[i - 2])

    m3_ctr = [0]

    def m3():
        for G in range(4):
            slot = next_group()
            r = ring[slot]
            Wga = r[:, 0:2048].rearrange("p (k n) -> p k n", k=8)
            Wgb = r[:, 2048:4096].rearrange("p (k n) -> p k n", k=8)
            WA = r[:, 4096:5120].rearrange("p (k n) -> p k n", k=4)
            WB = r[:, 5120:6144].rearrange("p (k n) -> p k n", k=4)
            for mm in range(2):
                m = 2 * G + mm
                for s in range(2):
                    par = m3_ctr[0] % 2
                    m3_ctr[0] += 1
                    bga, bgb, bpa, bpb = [4 * par + i for i in range(4)]
                    for k in range(8):
                        for (W, b) in ((Wga, bga), (Wgb, bgb)):
                            S.op("pe", lambda W=W, b=b, k=k: pe.matmul(
                                ps[b][:, :], lhsT=W[:, k, mm * 128:(mm + 1) * 128],
                                rhs=xn[:, k, slab(s)], start=(k == 0), stop=(k == 7)),
                                reads=[R("ring", slot, 0), R("xn", k, s)], writes=[R("ps", b)],
                                signal=(k == 7))
                    for (W, b, Q, nm) in ((WA, bpa, QA, "qa"), (WB, bpb, QB, "qb")):
                        for c in range(4):
                            S.op("pe", lambda W=W, b=b, c=c, Q=Q: pe.matmul(
                                ps[b][:, :], lhsT=W[:, c, mm * 128:(mm + 1) * 128],
                                rhs=Q[:, c, slab(s)], start=(c == 0), stop=(c == 3)),
                                reads=[R("ring", slot, 1)] + [R(nm, c, 4 * s + j) for j in range(4)],
                                writes=[R("ps", b)], signal=(c == 3))
                    sa, sb2 = sg[2 * par], sg[2 * par + 1]
                    S.op("act", lambda: act.activation(out=sa[:, :], in_=ps[bga][:, :],
                                                       func=AF.Sigmoid),
                         reads=[R("ps", bga)], writes=[R("sg", 2 * par)])
                    S.op("act", lambda: act.activation(out=sb2[:, :], in_=ps[bgb][:, :],
                                                       func=AF.Sigmoid),
                         reads=[R("ps", bgb)], writes=[R("sg", 2 * par + 1)])
                    S.op("dve", lambda: dve.tensor_tensor(out=t1[par][:, :], in0=ps[bpa][:, :],
                                                          in1=sa[:, :], op=ALU.mult),
                         reads=[R("ps", bpa), R("sg", 2 * par)], writes=[R("t1", par)])
                    S.op("dve", lambda: dve.tensor_tensor(out=t2[par][:, :], in0=ps[bpb][:, :],
                                                          in1=sb2[:, :], op=ALU.mult),
                         reads=[R("ps", bpb), R("sg", 2 * par + 1)], writes=[R("t2", par)])
                    S.op("pool", lambda m=m, s=s, par=par: pool.tensor_tensor(
                        out=merged[:, m, slab(s)], in0=t1[par][:, :], in1=t2[par][:, :], op=ALU.add),
                        reads=[R("t1", par), R("t2", par)],
                        writes=[R("mg", m, s), R("stg", m // 2)])
        oc = 0
        for H in range(2):
            slot = next_group()
            Wo = ring[slot][:, 0:4096].rearrange("p (k n) -> p k n", k=8)
            for mp in range(4):
                mo = 4 * H + mp
                for s in range(2):
                    b = oc % 4
                    oc += 1
                    for m in range(8):
                        S.op("pe", lambda m=m, b=b, mp=mp, s=s: pe.matmul(
                            ps[b][:, :], lhsT=Wo[:, m, mp * 128:(mp + 1) * 128],
                            rhs=merged[:, m, slab(s)], start=(m == 0), stop=(m == 7)),
                            reads=[R("ring", slot, 0), R("mg", m, s)], writes=[R("ps", b)],
                            signal=(m == 7))
                    S.op("dve", lambda b=b, mo=mo, s=s: dve.tensor_tensor(
                        out=h[:, mo, slab(s)], in0=ps[b][:, :], in1=h[:, mo, slab(s)], op=ALU.add),
                        reads=[R("ps", b), R("h", mo, s)], writes=[R("h", mo, s)])

    def p_items(tok0):
        def dma_p(t):
            st = pstg[t % 2]
            S.dma("sp", [lambda: sp.dma_start(
                out=st[:, :], in_=p_d[tok0 + t * 128:tok0 + (t + 1) * 128, :])],
                "pstg%d" % (t % 2), writes=[R("pstg", t % 2)])

        def xp(t):
            st = pstg[t % 2]
            b = 4 + t % 2
            for j in range(2):
                S.op("pe", lambda j=j: pe.transpose(
                    out=ps[b][:, j * 128:(j + 1) * 128], in_=st[:, j * 128:(j + 1) * 128],
                    identity=ident[:, :]),
                    reads=[R("pstg", t % 2), R("ident")], writes=[R("ps", b)], signal=(j == 1))
            evac_copy(pT[:, 0:2, t * 128:(t + 1) * 128],
                      ps[b][:, 0:256].rearrange("p (a b) -> p a b", a=2),
                      [R("ps", b)], [R("pT", t // 4)])

        def mk(i):
            def it():
                if i >= 1:
                    xp(i - 1)
                if i < 8:
                    dma_p(i)
            return it
        return [mk(i) for i in range(9)]

    def ple(tok0):
        oc = 0
        for H in range(2):
            slot = next_group()
            r = ring[slot]
            Wpg = r[:, 0:4096].rearrange("p (k n) -> p k n", k=8)
            Wpe = r[:, 4096:5120].rearrange("p (k n) -> p k n", k=2)
            for mp in range(4):
                mo = 4 * H + mp
                for s in range(2):
                    par = oc % 2
                    oc += 1
                    bg, bp = par, 2 + par
                    for k in range(8):
                        S.op("pe", lambda k=k, bg=bg, mp=mp, s=s: pe.matmul(
                            ps[bg][:, :], lhsT=Wpg[:, k, mp * 128:(mp + 1) * 128],
                            rhs=xn[:, k, slab(s)], start=(k == 0), stop=(k == 7)),
                            reads=[R("ring", slot, 0), R("xn", k, s)], writes=[R("ps", bg)],
                            signal=(k == 7))
                    for k in range(2):
                        S.op("pe", lambda k=k, bp=bp, mp=mp, s=s: pe.matmul(
                            ps[bp][:, :], lhsT=Wpe[:, k, mp * 128:(mp + 1) * 128],
                            rhs=pT[:, k, slab(s)], start=(k == 0), stop=(k == 1)),
                            reads=[R("ring", slot, 1), R("pT", s)], writes=[R("ps", bp)],
                            signal=(k == 1))
                    S.op("act", lambda bg=bg, par=par: act.activation(
                        out=t2[par][:, :], in_=ps[bg][:, :], func=AF.Sigmoid),
                        reads=[R("ps", bg)], writes=[R("t2", par)])
                    S.op("dve", lambda bp=bp, par=par: dve.tensor_tensor(
                        out=t1[par][:, :], in0=ps[bp][:, :], in1=t2[par][:, :], op=ALU.mult),
                        reads=[R("ps", bp), R("t2", par)], writes=[R("t1", par)])
                    S.op("pool", lambda mo=mo, s=s, par=par: pool.tensor_tensor(
                        out=h[:, mo, slab(s)], in0=h[:, mo, slab(s)], in1=t1[par][:, :], op=ALU.add),
                        reads=[R("t1", par), R("h", mo, s)], writes=[R("h", mo, s)])

    def store_out(tok0):
        for t in range(8):
            oi = t % NOST
            st = ostg[oi]
            for hf in range(2):
                b = (2 * t + hf) % 8
                for j in range(4):
                    k = hf * 4 + j
                    S.op("pe", lambda b=b, j=j, k=k, t=t: pe.transpose(
                        out=ps[b][:, j * 128:(j + 1) * 128], in_=h[:, k, t * 128:(t + 1) * 128],
                        identity=ident[:, :]),
                        reads=[R("h", k, t // 4), R("ident")], writes=[R("ps", b)],
                        signal=(j == 3))
                evac_copy(st[:, hf * 512:(hf + 1) * 512], ps[b][:, :],
                          [R("ps", b)], ostg_res(oi))
            S.dma("sp", [lambda t=t, st=st: sp.dma_start(
                out=out_d[tok0 + t * 128:tok0 + (t + 1) * 128, :], in_=st)],
                "ostg%d" % oi, reads=ostg_res(oi))

    def dump_dbg(what):
        allr = list(S.res.values())
        if what in ("m1", "m2"):
            return dump_h()
        if what in ("m1q", "m2y"):
            fns = [lambda: pool.dma_start(out=dbg_d[:, 0:4, :], in_=QA[:, :, :]),
                   lambda: pool.dma_start(out=dbg_d[:, 4:8, :], in_=QB[:, :, :])]
        elif what == "m1k":
            fns = [lambda: pool.dma_start(out=dbg_d[:, 0:4, :], in_=KA[:, :, 0:PASS]),
                   lambda: pool.dma_start(out=dbg_d[:, 4:6, :], in_=KB[:, :, 0:PASS])]
        elif what == "m1v":
            fns = [lambda: pool.dma_start(
                out=dbg_d[:, 0:6, :].rearrange("p a b -> p (a b)")[:, 0:5200].rearrange(
                    "p (a b) -> p a b", a=8),
                in_=V[:, 0:8].rearrange("p a b c -> p a (b c)"))]
        S.dma("pool", fns, "dbg", reads=allr)
        S.wait_sem("pool", "dbg")

    def dump_h():
        S.dma("sp", [lambda: sp.dma_start(out=dbg_d, in_=h[:, :, :])], "dbg",
              reads=[R("h", k, s) for k in range(8) for s in range(2)])
        S.wait_sem("sp", "dbg")

    def tok_of(pi):
        return (pi // 2) * SEQ + (pi % 2) * PASS

    for t in range(8):
        issue_x(tok_of(0), t)
    for ps_i in range(npass):
        seq, half = ps_i // 2, ps_i % 2
        tok0 = tok_of(ps_i)
        load_x(tok0)
        if stop == "load":
            dump_h(); break
        norm(0)
        ffn(0, extras=build_E_items() if ps_i == 0 else ())
        if stop == "ffn1":
            dump_h(); break
        norm(1)
        m1(half)
        if stop in ("m1", "m1q", "m1k", "m1v"):
            dump_dbg(stop); break
        m2(half)
        if stop in ("m2", "m2y"):
            dump_dbg(stop); break
        m3()
        if ps_i + 1 < npass:
            for t in range(4):
                issue_x(tok_of(ps_i + 1), t)
        if stop == "mix":
            dump_h(); break
        norm(2)
        ffn(1, extras=p_items(tok0))
        if stop == "ffn2":
            dump_h(); break
        norm(3)
        ple(tok0)
        if stop == "ple":
            dump_h(); break
        if ps_i + 1 < npass:
            for t in range(4, 8):
                issue_x(tok_of(ps_i + 1), t)
        store_out(tok0)
    for i in range(NOST):
        S.wait_sem("sp", "ostg%d" % i)
    return nc


def _host_consts():
    kl = np.arange(128)[:, None, None]
    kb = np.arange(5)[None, :, None]
    ql = np.arange(128)[None, None, :]
    rel = ql + 512 - 128 * kb - kl
    idxA = np.clip(rel, -128, 128) + 128
    jc = (128 * kb + kl) // 64
    qc = ql // 64
    validA = (jc >= qc) & (jc <= qc + 8)
    maskA = np.where(validA, 0.0, NEG).astype(np.float32)
    maskA = np.broadcast_to(maskA[:, None], (128, 8, 5, 128)).copy()
    kb2 = np.arange(2)[None, :, None]
    relB = (128 + ql) - (128 * kb2 + kl)
    jcB = (128 * kb2 + kl) // 64
    validB = (jcB >= qc) & (jcB <= qc + 2)
    slopes = np.array([2.0 ** (-8.0 * (hh + 1) / 8) for hh in range(8)], dtype=np.float32)
    biasB = -slopes[None, :, None, None] * np.abs(relB).astype(np.float32)[:, None]
    biasB = np.where(validB[:, None], biasB, NEG).astype(np.float32)
    return idxA, maskA, np.ascontiguousarray(biasB)


def make_in_maps(inputs):
    f = lambda a: np.ascontiguousarray(np.asarray(a, dtype=np.float32))
    x = f(inputs["x"])
    p = f(inputs["p"])[0]
    idxA, maskA, biasB = _host_consts()
    arb = f(inputs["a_rel_bias"])[0]
    biasA = np.ascontiguousarray(np.transpose(arb[:, idxA], (1, 0, 2, 3)))
    gains = np.stack([f(inputs[n])[0].reshape(8, 128).T for n in
                      ("ffn1_norm", "mix_norm", "ffn2_norm", "ple_norm")], axis=1)
    gains = np.ascontiguousarray(gains.reshape(128, 32))
    gqk = np.stack([np.tile(f(inputs[n])[0], 2) for n in
                    ("a_q_norm", "a_k_norm", "b_q_norm", "b_k_norm")], axis=1)
    gqk = np.ascontiguousarray(gqk)
    sinks = np.ascontiguousarray(np.broadcast_to(f(inputs["b_sinks"])[0][None, :], (128, 8)))
    shared = {
        "ffn1_w_gu": f(inputs["ffn1_w_gu"])[0], "ffn2_w_gu": f(inputs["ffn2_w_gu"])[0],
        "ffn1_w_down": f(inputs["ffn1_w_down"])[0], "ffn2_w_down": f(inputs["ffn2_w_down"])[0],
        "w_in": f(inputs["w_in"])[0], "w_gate": f(inputs["w_gate"])[0],
        "w_proj_a": f(inputs["w_proj_a"])[0], "w_proj_b": f(inputs["w_proj_b"])[0],
        "w_out": f(inputs["w_out"])[0], "w_ple_gate": f(inputs["w_ple_gate"])[0],
        "w_ple_proj": f(inputs["w_ple_proj"])[0],
        "gains": gains, "gqk": gqk, "sinks": sinks, "ident": np.eye(128, dtype=np.float32),
        "biasA": biasA, "maskA": maskA, "biasB": biasB,
    }
    in_maps = []
    for c in range(N_CORES):
        m = dict(shared)
        m["x"] = np.ascontiguousarray(x[2 * c:2 * c + 2].reshape(TOK_CORE, D))
        m["p"] = np.ascontiguousarray(p[2 * c:2 * c + 2].reshape(TOK_CORE, 256))
        in_maps.append(m)
    return in_maps


def kernel(**inputs):
    nc = build_nc()
    in_maps = make_in_maps(inputs)
    res = run_bass_kernel_spmd(nc, in_maps, core_ids=list(range(N_CORES)))
    out = np.stack([np.asarray(r["out"]).reshape(2, SEQ, D) for r in res.results], axis=0)
    return out.reshape(16, SEQ, D).astype(np.float32)
```
